# Optimizing a Trainium2 kernel written in Bass

```python
import math
import jax, jax.numpy as jnp
from jax import lax
import numpy as np

D_MODEL = 1024
BATCH = 1
SEQ = 16384
DEPTH = 4
DEC_BATCH = 8
DEC_SEQ = 2048
PAST_LEN = 128

HEAD_DIM = 64
N_RET_HEADS = 8
N_DIFF_HEADS = 8
RET_WIDTH = N_RET_HEADS * HEAD_DIM
DIFF_WIDTH = N_DIFF_HEADS * HEAD_DIM
MIX_WIDTH = RET_WIDTH + DIFF_WIDTH
DIFF_QK_DIM = HEAD_DIM // 2
IN_WIDTH = 4 * RET_WIDTH + 3 * DIFF_WIDTH
D_FF = 2816
CONV_WIDTH = 3
CHUNK = 128
Q_BLOCK = 128
N_BUCKETS = 32
MAX_DISTANCE = 128
ROPE_BASE = 10000.0
LN_EPS = 1e-5
HEAD_NORM_EPS = 1e-6
DEEPNORM_ALPHA = (2 * DEPTH) ** 0.25
DEEPNORM_BETA = (8 * DEPTH) ** -0.25

kernel_name = "hymba_style_retnet_diffattn_encoder"


def layer_norm(x, g, b):
    xf = x.astype(jnp.float32)
    mu = jnp.mean(xf, axis=-1, keepdims=True)
    var = jnp.mean(jnp.square(xf - mu), axis=-1, keepdims=True)
    y = (xf - mu) * lax.rsqrt(var + LN_EPS) * g.astype(jnp.float32) + b.astype(jnp.float32)
    return y.astype(x.dtype)


def head_rms(x):
    xf = x.astype(jnp.float32)
    return xf * lax.rsqrt(jnp.mean(jnp.square(xf), axis=-1, keepdims=True) + HEAD_NORM_EPS)


def to_heads(t, n_heads):
    b, s, _ = t.shape
    return t.reshape(b, s, n_heads, -1).transpose(0, 2, 1, 3)


def rotary(x):
    s, d = x.shape[-2], x.shape[-1]
    inv = 1.0 / (ROPE_BASE ** (jnp.arange(0, d, 2, dtype=jnp.float32) / d))
    ang = jnp.arange(s, dtype=jnp.float32)[:, None] * inv[None, :]
    cos, sin = jnp.cos(ang), jnp.sin(ang)
    x1, x2 = x[..., : d // 2].astype(jnp.float32), x[..., d // 2:].astype(jnp.float32)
    return jnp.concatenate([x1 * cos - x2 * sin, x1 * sin + x2 * cos], axis=-1)


def t5_bucket(rel):
    nb = N_BUCKETS // 2
    ret = jnp.where(rel > 0, nb, 0)
    n = jnp.abs(rel)
    max_exact = nb // 2
    nf = jnp.maximum(n, 1).astype(jnp.float32)
    large = max_exact + (jnp.log(nf / max_exact) / math.log(MAX_DISTANCE / max_exact)
                         * (nb - max_exact)).astype(jnp.int32)
    large = jnp.minimum(large, nb - 1)
    return ret + jnp.where(n < max_exact, n, large)


def relative_bias_vector(rel_bias, s):
    offsets = jnp.arange(-(s - 1), s, dtype=jnp.int32)
    return rel_bias.astype(jnp.float32)[t5_bucket(offsets)]


def bidir_retention(q, k, v, log_g_f, log_g_b):
    b, h, s, d = q.shape
    nc = s // CHUNK
    qc = q.reshape(b, h, nc, CHUNK, d)
    kc = k.reshape(b, h, nc, CHUNK, d)
    vc = v.reshape(b, h, nc, CHUNK, v.shape[-1])
    i = jnp.arange(CHUNK, dtype=jnp.float32)
    diff = i[:, None] - i[None, :]
    lf = log_g_f[:, None, None]
    lb = log_g_b[:, None, None]
    mask = (jnp.where(diff >= 0, jnp.exp(lf * jnp.maximum(diff, 0.0)), 0.0)
            + jnp.where(diff < 0, jnp.exp(lb * jnp.maximum(-diff, 0.0)), 0.0))
    scores = jnp.einsum('bhnid,bhnjd->bhnij', qc, kc) * mask[None, :, None]
    intra = jnp.einsum('bhnij,bhnje->bhnie', scores, vc)
    w_f = jnp.exp(log_g_f[:, None] * (CHUNK - 1 - i)[None, :])
    w_b = jnp.exp(log_g_b[:, None] * i[None, :])
    kv_f = jnp.einsum('bhnjd,hj,bhnje->nbhde', kc, w_f, vc)
    kv_b = jnp.einsum('bhnjd,hj,bhnje->nbhde', kc, w_b, vc)
    dec_f = jnp.exp(log_g_f * CHUNK)[None, :, None, None]
    dec_b = jnp.exp(log_g_b * CHUNK)[None, :, None, None]
    zero = jnp.zeros_like(kv_f[0])

    def step_f(r, kv):
        return dec_f * r + kv, r

    def step_b(r, kv):
        return dec_b * r + kv, r

    _, r_prev = lax.scan(step_f, zero, kv_f)
    _, r_next = lax.scan(step_b, zero, kv_b, reverse=True)
    q_f = jnp.exp(log_g_f[:, None] * (i + 1.0)[None, :])
    q_b = jnp.exp(log_g_b[:, None] * (CHUNK - i)[None, :])
    cross = (jnp.einsum('bhnid,nbhde,hi->bhnie', qc, r_prev, q_f)
             + jnp.einsum('bhnid,nbhde,hi->bhnie', qc, r_next, q_b))
    return (intra + cross).reshape(b, h, s, -1)


def diff_attention(q, k, v, lam, bias_vec):
    b, h, _, s, dq = q.shape
    nb = s // Q_BLOCK
    qb = (q * (dq ** -0.5)).reshape(b, h, 2, nb, Q_BLOCK, dq).transpose(3, 0, 1, 2, 4, 5)
    kpos = jnp.arange(s, dtype=jnp.int32)

    def block(args):
        qblk, start = args
        qpos = start + jnp.arange(Q_BLOCK, dtype=jnp.int32)
        bias = bias_vec[kpos[None, :] - qpos[:, None] + (s - 1)].transpose(2, 0, 1)
        logits = jnp.einsum('bhtqd,bhtkd->bhtqk', qblk, k).astype(jnp.float32) + bias[None, :, None]
        p = jax.nn.softmax(logits, axis=-1)
        w = p[:, :, 0] - lam * p[:, :, 1]
        return jnp.einsum('bhqk,bhkd->bhqd', w, v.astype(jnp.float32))

    starts = jnp.arange(nb, dtype=jnp.int32) * Q_BLOCK
    out = lax.map(block, (qb, starts))
    return out.transpose(1, 2, 0, 3, 4).reshape(b, h, s, -1)


def conv_glu(x, w_up, conv_w, conv_b, w_down):
    h = x @ w_up
    a, val = jnp.split(h, 2, axis=-1)
    pad = CONV_WIDTH // 2
    s = a.shape[1]
    ap = jnp.pad(a, ((0, 0), (pad, pad), (0, 0)))
    conv = conv_b
    for t in range(CONV_WIDTH):
        conv = conv + ap[:, t:t + s] * conv_w[t]
    return (jax.nn.gelu(conv, approximate=False) * val).astype(x.dtype) @ w_down


def encoder_layer(x, bias_vec, lam_init, w_in, decay_logit, lq1, lk1, lq2, lk2, dn_g,
                  w_out, ln_g, ln_b, w_up, conv_w, conv_b, w_down):
    b, s, _ = x.shape
    proj = x @ w_in
    r = RET_WIDTH
    splits = [r, 2 * r, 3 * r, 4 * r, 4 * r + DIFF_WIDTH, 4 * r + 2 * DIFF_WIDTH]
    rq, rk, rv, rg, dq, dk, dv = jnp.split(proj, splits, axis=-1)

    log_g = jax.nn.log_sigmoid(decay_logit.astype(jnp.float32))
    qr = rotary(to_heads(rq, N_RET_HEADS))
    kr = rotary(to_heads(rk, N_RET_HEADS)) * (HEAD_DIM ** -0.5)
    vr = to_heads(rv, N_RET_HEADS).astype(jnp.float32)
    yr = head_rms(bidir_retention(qr, kr, vr, log_g[0], log_g[1]))
    yr = yr.transpose(0, 2, 1, 3).reshape(b, s, RET_WIDTH)
    yr = (jax.nn.silu(rg.astype(jnp.float32)) * yr).astype(x.dtype)

    lam = (jnp.exp(jnp.sum(lq1.astype(jnp.float32) * lk1.astype(jnp.float32)))
           - jnp.exp(jnp.sum(lq2.astype(jnp.float32) * lk2.astype(jnp.float32))) + lam_init)
    qd = dq.reshape(b, s, N_DIFF_HEADS, 2, DIFF_QK_DIM).transpose(0, 2, 3, 1, 4)
    kd = dk.reshape(b, s, N_DIFF_HEADS, 2, DIFF_QK_DIM).transpose(0, 2, 3, 1, 4)
    vd = to_heads(dv, N_DIFF_HEADS)
    yd = diff_attention(qd, kd, vd, lam, bias_vec)
    yd = head_rms(yd) * dn_g.astype(jnp.float32) * (1.0 - lam_init)
    yd = yd.transpose(0, 2, 1, 3).reshape(b, s, DIFF_WIDTH).astype(x.dtype)

    mix = jnp.concatenate([yr, yd], axis=-1) @ w_out
    x = layer_norm(DEEPNORM_ALPHA * x + mix, ln_g[0], ln_b[0])
    x = layer_norm(DEEPNORM_ALPHA * x + conv_glu(x, w_up, conv_w, conv_b, w_down), ln_g[1], ln_b[1])
    return x


def setup_inputs(seed: int = 0) -> dict:
    key = jax.random.key(seed)
    ks = jax.random.split(key, 20)
    f32 = jnp.float32
    x_prompt = jax.random.normal(ks[0], (BATCH, SEQ, D_MODEL), f32)
    x_sample = jax.random.normal(ks[1], (DEC_BATCH, DEC_SEQ, D_MODEL), f32)
    col_scale = jnp.concatenate([
        jnp.ones((2 * RET_WIDTH,), f32), jnp.full((RET_WIDTH,), DEEPNORM_BETA, f32),
        jnp.ones((RET_WIDTH + 2 * DIFF_WIDTH,), f32), jnp.full((DIFF_WIDTH,), DEEPNORM_BETA, f32)])
    w_in = jax.random.normal(ks[2], (DEPTH, D_MODEL, IN_WIDTH), f32) * (D_MODEL ** -0.5) * col_scale
    base = jnp.log(2.0 ** (5.0 + jnp.arange(N_RET_HEADS, dtype=f32)) - 1.0)
    ret_decay_logit = base[None, None, :] + 0.1 * jax.random.normal(ks[3], (DEPTH, 2, N_RET_HEADS), f32)
    rel_bias = 0.5 * jax.random.normal(ks[4], (N_BUCKETS, N_DIFF_HEADS), f32)
    lambda_q1 = 0.1 * jax.random.normal(ks[5], (DEPTH, DIFF_QK_DIM), f32)
    lambda_k1 = 0.1 * jax.random.normal(ks[6], (DEPTH, DIFF_QK_DIM), f32)
    lambda_q2 = 0.1 * jax.random.normal(ks[7], (DEPTH, DIFF_QK_DIM), f32)
    lambda_k2 = 0.1 * jax.random.normal(ks[8], (DEPTH, DIFF_QK_DIM), f32)
    diff_norm_g = 1.0 + 0.02 * jax.random.normal(ks[9], (DEPTH, HEAD_DIM), f32)
    w_out = jax.random.normal(ks[10], (DEPTH, MIX_WIDTH, D_MODEL), f32) * (MIX_WIDTH ** -0.5) * DEEPNORM_BETA
    ln_g = 1.0 + 0.02 * jax.random.normal(ks[11], (DEPTH, 2, D_MODEL), f32)
    ln_b = 0.02 * jax.random.normal(ks[12], (DEPTH, 2, D_MODEL), f32)
    w_up = jax.random.normal(ks[13], (DEPTH, D_MODEL, 2 * D_FF), f32) * (D_MODEL ** -0.5)
    conv_w = jax.random.normal(ks[14], (DEPTH, CONV_WIDTH, D_FF), f32) * (CONV_WIDTH ** -0.5)
    conv_b = 0.01 * jax.random.normal(ks[15], (DEPTH, D_FF), f32)
    w_down = jax.random.normal(ks[16], (DEPTH, D_FF, D_MODEL), f32) * (D_FF ** -0.5) * DEEPNORM_BETA
    return {"x_prompt": x_prompt, "x_sample": x_sample, "w_in": w_in,
            "ret_decay_logit": ret_decay_logit, "rel_bias": rel_bias,
            "lambda_q1": lambda_q1, "lambda_k1": lambda_k1, "lambda_q2": lambda_q2,
            "lambda_k2": lambda_k2, "diff_norm_g": diff_norm_g, "w_out": w_out,
            "ln_g": ln_g, "ln_b": ln_b, "w_up": w_up, "conv_w": conv_w,
            "conv_b": conv_b, "w_down": w_down}


def trunk(x, w_in, ret_decay_logit, rel_bias, lambda_q1, lambda_k1, lambda_q2, lambda_k2,
          diff_norm_g, w_out, ln_g, ln_b, w_up, conv_w, conv_b, w_down):
    bias_vec = relative_bias_vector(rel_bias, x.shape[1])
    for l in range(DEPTH):
        lam_init = 0.8 - 0.6 * math.exp(-0.3 * l)
        x = encoder_layer(x, bias_vec, lam_init, w_in[l], ret_decay_logit[l],
                          lambda_q1[l], lambda_k1[l], lambda_q2[l], lambda_k2[l], diff_norm_g[l],
                          w_out[l], ln_g[l], ln_b[l], w_up[l], conv_w[l], conv_b[l], w_down[l])
    return x


def reference(x_prompt, x_sample, w_in, ret_decay_logit, rel_bias, lambda_q1, lambda_k1,
              lambda_q2, lambda_k2, diff_norm_g, w_out, ln_g, ln_b, w_up, conv_w, conv_b, w_down):
    y_prompt = trunk(x_prompt, w_in, ret_decay_logit, rel_bias, lambda_q1, lambda_k1, lambda_q2,
                     lambda_k2, diff_norm_g, w_out, ln_g, ln_b, w_up, conv_w, conv_b, w_down)
    y_sample = trunk(x_sample, w_in, ret_decay_logit, rel_bias, lambda_q1, lambda_k1, lambda_q2,
                     lambda_k2, diff_norm_g, w_out, ln_g, ln_b, w_up, conv_w, conv_b, w_down)
    return (y_prompt, y_sample)
```

```python
import math
import numpy as np
import ml_dtypes
import concourse.bass as bass
import concourse.mybir as mybir
from concourse.bass_utils import run_bass_kernel_spmd

F32 = mybir.dt.float32
BF16 = mybir.dt.bfloat16
AF = mybir.ActivationFunctionType
ALU = mybir.AluOpType
AX = mybir.AxisListType
NPBF = ml_dtypes.bfloat16

D = 1024
T = 2048
NTT = 16
DEPTH = 4
DFF = 2816
NFC = 22
LN_EPS = 1e-5
HN_EPS = 1e-6
ALPHA = (2 * DEPTH) ** 0.25
KILL = -240.0
VW = 80
NF_IN = 36
U0 = 639
RVLEN = 1280


class Buf:
    __slots__ = ("name", "w", "r")

    def __init__(self, name):
        self.name = name
        self.w = None
        self.r = {}


class Prog:
    ENG = ("pe", "act", "dve", "pool", "sp")

    def __init__(self, nc, n_dma_sems=40):
        self.nc = nc
        self.esem = {e: nc.alloc_semaphore("s_" + e) for e in ("pe", "act", "dve", "pool")}
        self.ecnt = {e: 0 for e in self.esem}
        self.dsem = [nc.alloc_semaphore(f"sd{i}") for i in range(n_dma_sems)]
        self.dcnt = [0] * n_dma_sems
        self.drr = 0
        self.csem = [nc.alloc_semaphore(f"sc{i}") for i in range(16)]
        self.ccnt = [0] * 16
        self.crr = 0
        self.q = {e: [] for e in self.ENG}
        self.seen = {e: {} for e in self.ENG}
        self.n_inst = 0

    def _sem(self, key):
        kind, i = key
        if kind == "e":
            return self.esem[i]
        if kind == "d":
            return self.dsem[i]
        return self.csem[i]

    def _deps(self, eng, reads, writes, extra=()):
        deps = {}
        def add(tok):
            if tok is None:
                return
            k, v = tok
            if eng == "pe" and k == ("e", "pe"):
                return
            if deps.get(k, 0) < v:
                deps[k] = v
        for b in reads:
            add(b.w)
        for b in writes:
            add(b.w)
            for t in b.r.items():
                add(t)
        for t in extra:
            add(t)
        out = []
        seen = self.seen[eng]
        for k, v in deps.items():
            if seen.get(k, 0) >= v:
                continue
            seen[k] = v
            out.append((k, v))
        return out

    def _mark(self, tok, reads, writes):
        for b in writes:
            b.w = tok
            b.r = {}
        for b in reads:
            if b not in writes:
                if b.r.get(tok[0], 0) < tok[1]:
                    b.r[tok[0]] = tok[1]

    def alias(self, new, old):
        acc = {}
        for b in old:
            toks = list(b.r.items())
            if b.w is not None:
                toks.append(b.w)
            for k, v in toks:
                if acc.get(k, 0) < v:
                    acc[k] = v
        for b in new:
            for k, v in acc.items():
                if b.r.get(k, 0) < v:
                    b.r[k] = v

    def op(self, eng, fn, reads=(), writes=()):
        waits = self._deps(eng, reads, writes)
        self.ecnt[eng] += 1
        tok = (("e", eng), self.ecnt[eng])
        self.q[eng].append((waits, fn, ("e", eng), 1))
        self._mark(tok, reads, writes)
        self.n_inst += 1 + len(waits)
        return tok

    def dma(self, eng, out, in_, reads=(), writes=(), **kw):
        i = self.drr
        self.drr = (self.drr + 1) % len(self.dsem)
        extra = ()
        if self.dcnt[i] > 0:
            extra = ((("d", i), 16 * self.dcnt[i]),)
        waits = self._deps(eng, reads, writes, extra)
        self.dcnt[i] += 1
        tok = (("d", i), 16 * self.dcnt[i])
        self.q[eng].append((waits, (lambda e, o=out, s=in_, k=kw: e.dma_start(out=o, in_=s, **k)), ("d", i), 16))
        self._mark(tok, reads, writes)
        self.n_inst += 1 + len(waits)
        return tok

    def collective(self, kind, groups, in_ap, out_ap, reads=(), writes=()):
        i = self.crr
        self.crr = (self.crr + 1) % len(self.csem)
        extra = ()
        if self.ccnt[i] > 0:
            extra = ((("c", i), self.ccnt[i]),)
        waits = self._deps("pool", reads, writes, extra)
        self.ccnt[i] += 1
        tok = (("c", i), self.ccnt[i])
        self.q["pool"].append((waits, (lambda e, a=in_ap, b=out_ap, g=groups, k=kind: e.collective_compute(
            k, ALU.bypass, replica_groups=g, ins=[a], outs=[b])), ("c", i), 1))
        self._mark(tok, reads, writes)
        return tok

    def emit(self):
        nc = self.nc
        fin = []
        for e in self.esem:
            if self.ecnt[e]:
                fin.append((("e", e), self.ecnt[e]))
        for i, c in enumerate(self.dcnt):
            if c:
                fin.append((("d", i), 16 * c))
        for i, c in enumerate(self.ccnt):
            if c:
                fin.append((("c", i), c))
        engs = {"pe": "tensor", "act": "scalar", "dve": "vector", "pool": "gpsimd", "sp": "sync"}
        with nc.Block() as block:
            for e in self.ENG:
                items = self.q[e]
                def body(eng, items=items, e=e):
                    for waits, fn, key, inc in items:
                        for k, v in waits:
                            eng.wait_ge(self._sem(k), v)
                        fn(eng).then_inc(self._sem(key), inc)
                    if e == "sp":
                        for k, v in fin:
                            eng.wait_ge(self._sem(k), v)
                getattr(block, engs[e])(body)


def _t5_bucket_np(rel):
    rel = np.asarray(rel, np.int64)
    nb = 16
    ret = np.where(rel > 0, nb, 0)
    n = np.abs(rel)
    me = 8
    nf = np.maximum(n, 1).astype(np.float32)
    large = me + (np.log(nf / np.float32(me)).astype(np.float32) / np.float32(math.log(128 / me))
                  * np.float32(nb - me)).astype(np.int32)
    large = np.minimum(large, nb - 1)
    return (ret + np.where(n < me, n, large)).astype(np.int64)


def _rv_ranges():
    u = np.arange(RVLEN)
    b = _t5_bucket_np(U0 - u)
    runs = []
    s = 0
    for i in range(1, RVLEN + 1):
        if i == RVLEN or b[i] != b[s]:
            runs.append((int(b[s]), s, i))
            s = i
    return runs


def _rope_tables(pos0):
    i = np.arange(0, 64, 2, dtype=np.float32) / np.float32(64)
    inv = (np.float32(1.0) / (np.float32(10000.0) ** i)).astype(np.float32)
    pos = (pos0 + np.arange(T)).astype(np.float32)
    ang = (pos[:, None] * inv[None, :]).astype(np.float32)
    cos = np.cos(ang.astype(np.float64)).astype(np.float32)
    sin = np.sin(ang.astype(np.float64)).astype(np.float32)
    cosT = np.zeros((128, T), np.float32)
    sinS = np.zeros((128, T), np.float32)
    for p in range(128):
        d = p % 64
        cosT[p] = cos[:, d % 32]
        sinS[p] = sin[:, d % 32] * (-1.0 if d < 32 else 1.0)
    return cosT, sinS


def _far_masks(core, nblk, prompt):
    a = np.zeros((32, nblk), np.float32)
    for t in range(4):
        Tg = (core * 4 + t) if prompt else t
        for j in range(nblk):
            if j < 4 * Tg - 1:
                a[t, j] = 1; a[4 + t, j] = 1
            elif j >= 4 * Tg + 5:
                a[8 + t, j] = 1; a[12 + t, j] = 1
            else:
                a[16 + t, j] = 1
    return np.repeat(a, 128, axis=1).astype(NPBF)


def _host_consts(core):
    c = {}
    cp, sp_ = _rope_tables(core * T)
    cs, ss = _rope_tables(0)
    c["rope"] = np.stack([cp, sp_, cs, ss], 0)
    c["augk_p"] = _far_masks(core, 128, True)
    c["augk_s"] = _far_masks(core, 16, False)
    augc = np.zeros((32, 16 * 128), np.float32)
    for i in range(8):
        if i != core - 1:
            augc[20, i * 128:(i + 1) * 128] = 1
        if i != core + 1:
            augc[20, (8 + i) * 128:(9 + i) * 128] = 1
    c["augc"] = augc.astype(NPBF)
    pq = np.zeros((128, T), np.float32)
    mh = np.zeros((128, 1), np.float32); ml = np.zeros((128, 1), np.float32); kv = np.zeros((128, 1), np.float32)
    for base in (32, 96):
        for r in range(20):
            t = r % 4
            pq[base + r, t * 512:(t + 1) * 512] = 1
        pq[base + 20, :] = 1
        for r in (0, 1, 2, 3, 8, 9, 10, 11):
            mh[base + r] = 1
        for r in (4, 5, 6, 7, 12, 13, 14, 15):
            ml[base + r] = 1
        for r in (16, 17, 18, 19, 20):
            kv[base + r] = KILL
    c["pq"] = pq
    i = np.arange(128, dtype=np.float32)
    dm = i[None, :] - i[:, None]
    bd = np.zeros((128, 128), np.float32); bd[:64, :64] = 1; bd[64:, 64:] = 1
    small = np.zeros((128, 8, 128), np.float32)
    small[:, 0] = np.maximum(dm, 0); small[:, 1] = np.maximum(-dm, 0)
    small[:, 2] = (dm >= 0); small[:, 3] = (dm < 0)
    small[:, 4] = bd
    small[:, 5] = (i + 1.0)[None, :]
    small[:, 6] = (128.0 - i)[None, :]
    c["small"] = small
    misc = np.zeros((128, 64), np.float32)
    misc[:, 0] = 127.0 - i
    misc[:, 1] = i
    misc[:, 2:18] = 128.0 * np.arange(16)[None, :]
    misc[:, 18:34] = 128.0 * (15 - np.arange(16))[None, :]
    for r in range(8):
        misc[:, 34 + r] = 2048.0 * (core - 1 - r) if r < core else 0.0
        misc[:, 42 + r] = 2048.0 * (r - core - 1) if r > core else 0.0
        misc[:, 50 + r] = 1.0 if r < core else 0.0
    c["misc"] = misc
    misc2 = np.zeros((128, 16), np.float32)
    for r in range(8):
        misc2[:, r] = 1.0 if r > core else 0.0
    misc2[:, 8:9] = mh; misc2[:, 9:10] = ml; misc2[:, 10:11] = kv
    c["misc2"] = misc2
    idn = np.zeros((128, 3, 128), np.float32)
    idn[:, 0] = np.eye(128); idn[:, 1] = np.eye(128)[::-1]; idn[:, 2] = bd
    c["idn"] = idn.astype(NPBF)
    sel = np.zeros((16, 2), np.float32)
    if core > 0:
        sel[2 * (core - 1) + 1, 0] = 1
    if core < 7:
        sel[2 * (core + 1), 1] = 1
    c["sel"] = sel.astype(NPBF)
    return c


def _host_weights(inp):
    w_in = np.asarray(inp["w_in"], np.float32)
    L = w_in.shape[0]
    rq, rk, rv, rg = (w_in[:, :, i * 512:(i + 1) * 512] for i in range(4))
    dq = w_in[:, :, 2048:2560]; dk = w_in[:, :, 2560:3072]; dv = w_in[:, :, 3072:3584]

    def swap(w):
        w = w.reshape(L, D, 8, 2, 32)
        return w[:, :, :, ::-1, :].reshape(L, D, 512)

    def pad_heads(w):
        w = w.reshape(L, D, 8, 2, 32)
        o = np.zeros((L, D, 8, 4, 32), np.float32)
        o[:, :, :, 0] = w[:, :, :, 0]; o[:, :, :, 2] = w[:, :, :, 1]
        return o.reshape(L, D, 1024)
    wf = np.concatenate([rq, swap(rq), rk, swap(rk), rg, pad_heads(dq), pad_heads(dk)], axis=2)
    wt = np.concatenate([rv, dv], axis=2)
    return np.ascontiguousarray(wf), np.ascontiguousarray(wt)


def build(nlayers=DEPTH, groups=("p", "s"), dbg_out=None):
    nc = bass.Bass("TRN2", target_bir_lowering=False)
    P = Prog(nc)
    Bf = Buf

    def din(name, shape, dt=F32):
        return nc.dram_tensor(name, list(shape), dt, kind="ExternalInput")

    def dscr(name, shape, dt=F32):
        return nc.dram_tensor(name, list(shape), dt)

    x_in = {"p": din("xp", [T, D]), "s": din("xs", [T, D])}
    wf_d = din("wf", [DEPTH, D, NF_IN * 128]); wt_d = din("wt", [DEPTH, D, 1024])
    wo_d = din("w_out", [DEPTH, D, D]); wu_d = din("w_up", [DEPTH, D, 2 * DFF]); wd_d = din("w_down", [DEPTH, DFF, D])
    rdl_d = din("rdl", [DEPTH, 16]); rb_d = din("rel_bias", [32, 8])
    lam_d = [din(n, [DEPTH, 32]) for n in ("lq1", "lk1", "lq2", "lk2")]
    dng_d = din("dng", [DEPTH, 64]); lng_d = din("ln_g", [DEPTH, 2, D]); lnb_d = din("ln_b", [DEPTH, 2, D])
    cw_d = din("conv_w", [DEPTH, 3, DFF]); cb_d = din("conv_b", [DEPTH, DFF])
    rope_d = din("rope", [4, 128, T]); augkp_d = din("augk_p", [32, 16384], BF16); augks_d = din("augk_s", [32, T], BF16)
    augc_d = din("augc", [32, T], BF16); pq_d = din("pq", [128, T]); small_d = din("small", [128, 8, 128])
    misc_d = din("misc", [128, 64]); misc2_d = din("misc2", [128, 16]); idn_d = din("idn", [128, 3, 128], BF16)
    sel_d = din("sel", [16, 2], BF16)
    y_out = {"p": nc.dram_tensor("yp", [T, D], F32, kind="ExternalOutput"),
             "s": nc.dram_tensor("ys", [T, D], F32, kind="ExternalOutput")}
    X1 = {g: dscr("x1_" + g, [T, D]) for g in "ps"}
    X2 = {g: dscr("x2_" + g, [T, D]) for g in "ps"}
    gate_d = dscr("gate", [4, 128, T])
    KTd = {g: [dscr(f"ktd_{g}{pc}", [128, T], BF16) for pc in range(4)] for g in "ps"}
    Vd = {g: [dscr(f"vd_{g}{h}", [128, 16 * VW], BF16) for h in range(8)] for g in "ps"}
    KTg4 = [dscr(f"ktg4_{pc}", [4 * 128, T], BF16) for pc in range(4)]
    KTg8 = [dscr(f"ktg8_{pc}", [8 * 128, T], BF16) for pc in range(4)]
    Vg4 = [dscr(f"vg4_{h}", [4 * 128, 16 * VW], BF16) for h in range(8)]
    Vg8 = [dscr(f"vg8_{h}", [8 * 128, 16 * VW], BF16) for h in range(8)]
    RSd = dscr("rsd", [1024, 128]); RSg4 = dscr("rsg4", [4096, 128]); RSg8 = dscr("rsg8", [8192, 128])
    HLd = dscr("hld", [2, D]); HLg4 = dscr("hlg4", [8, D]); HLg8 = dscr("hlg8", [16, D])
    RVd = dscr("rvd", [8, RVLEN])
    RKT = dscr("r_kt", [4, 128, T], BF16); RVR = dscr("r_vr", [4, 128, T], BF16); RKV = dscr("r_kv", [4, 128, 2 * 16 * 128])
    D_ktg, D_vg, D_rsg, D_hlg = Bf("d_ktg"), Bf("d_vg"), Bf("d_rsg"), Bf("d_hlg")
    D_rsg4 = Bf("d_rsg4")
    D_ktd = {g: Bf("d_ktd" + g) for g in "ps"}; D_vd = {g: Bf("d_vd" + g) for g in "ps"}
    D_x1 = {g: Bf("d_x1" + g) for g in "ps"}; D_x2 = {g: Bf("d_x2" + g) for g in "ps"}
    D_gate, D_rsd, D_hld, D_rvd = Bf("d_gate"), Bf("d_rsd"), Bf("d_hld"), Bf("d_rvd")
    D_ret = [Bf(f"d_ret{p}") for p in range(4)]
    G4 = [[0, 1, 2, 3], [4, 5, 6, 7]]; G2 = [[0, 4], [1, 5], [2, 6], [3, 7]]

    A0 = nc.alloc_sbuf_tensor("sb_a0", [128, 49152], BF16)
    QT = A0[:, 0:16384].rearrange("p (c t) -> p c t", c=8)
    MIXT = A0[:, 16384:32768].rearrange("p (c t) -> p c t", c=8)
    KTG = A0[:, 32768:49152]
    HT = A0[:, 0:22528].rearrange("p (c t) -> p c t", c=NFC)
    WD = A0[:, 22528:45056].rearrange("p (c f) -> p c f", c=NFC)
    B_qt = [Bf(f"qt{h}") for h in range(8)]; B_mix = [Bf(f"mix{c}") for c in range(8)]; B_ktg = Bf("ktg")
    B_ht = Bf("ht"); B_wd = Bf("wd")
    SCR = nc.alloc_sbuf_tensor("sb_scr", [128, 11264], F32)
    SCRB = SCR[:, :].bitcast(BF16)
    XT = nc.alloc_sbuf_tensor("sb_xt", [128, 8, T], BF16); B_xt = Bf("xt")
    XTH = nc.alloc_sbuf_tensor("sb_xth", [128, 8, 2], BF16); B_xth = Bf("xth")
    VGt = nc.alloc_sbuf_tensor("sb_vg", [128, 128 * VW], BF16); B_vg = Bf("vg")
    NWS = 6
    WS = nc.alloc_sbuf_tensor("sb_ws", [128, NWS, 1024], BF16); B_wsl = [Bf(f"ws{i}") for i in range(NWS)]
    ws_rr = [0]
    IDN = nc.alloc_sbuf_tensor("sb_idn", [128, 3, 128], BF16); B_const = Bf("const")
    MISC = nc.alloc_sbuf_tensor("sb_misc", [128, 64], F32); MISC2 = nc.alloc_sbuf_tensor("sb_misc2", [128, 16], F32)
    PRM = nc.alloc_sbuf_tensor("sb_prm", [128, 256], F32); B_prm = Bf("prm")
    CONV = nc.alloc_sbuf_tensor("sb_conv", [128, NFC, 4], F32); B_conv = Bf("conv")
    SELt = nc.alloc_sbuf_tensor("sb_sel", [16, 2], BF16)
    LGREP = PRM[:, 0:16]; LGP = PRM[:, 16:24]; W16 = PRM[:, 24:40]; DEC = PRM[:, 40:48]; W8 = PRM[:, 48:64]
    LAM = PRM[:, 64:68]; NLAM = PRM[:, 68:72]; DNG = PRM[:, 72:76]; VALS = PRM[:, 76:84]; TMPP = PRM[:, 84:148]
    LAMT = PRM[:, 148:164]
    PSW = [nc.psum_tensor(f"psw{i}", [128, 1024], F32).__enter__() for i in range(4)]
    PS = []
    for i in range(4):
        PS += [PSW[i][:, 0:512], PSW[i][:, 512:1024]]
    B_ps = [Bf(f"ps{i}") for i in range(8)]
    ident = IDN[:, 0, :]; antiid = IDN[:, 1, :]; bdones = IDN[:, 2, :]

    sp, pool = "sp", "pool"

    def mm(out, lhsT, rhs, start, stop, reads, writes, **kw):
        return P.op("pe", lambda e: e.matmul(out, lhsT=lhsT, rhs=rhs, start=start, stop=stop, **kw), reads, writes)

    def tr(out, in_, reads, writes):
        return P.op("pe", lambda e: e.transpose(out, in_, ident), reads, writes)

    def act(out, in_, func, reads, writes, **kw):
        return P.op("act", lambda e: e.activation(out=out, in_=in_, func=func, **kw), reads, writes)

    def ts(eng, out, in0, s1, s2, op0, op1, reads, writes):
        if s2 is None:
            return P.op(eng, lambda e: e.tensor_scalar(out=out, in0=in0, scalar1=s1, scalar2=None, op0=op0), reads, writes)
        return P.op(eng, lambda e: e.tensor_scalar(out=out, in0=in0, scalar1=s1, scalar2=s2, op0=op0, op1=op1), reads, writes)

    def tt(eng, out, in0, in1, op, reads, writes):
        return P.op(eng, lambda e: e.tensor_tensor(out=out, in0=in0, in1=in1, op=op), reads, writes)

    def stt(eng, out, in0, scalar, in1, op0, op1, reads, writes):
        return P.op(eng, lambda e: e.scalar_tensor_tensor(out=out, in0=in0, scalar=scalar, in1=in1, op0=op0, op1=op1), reads, writes)

    def cp(eng, out, in_, reads, writes):
        return P.op(eng, lambda e: e.tensor_copy(out=out, in_=in_), reads, writes)

    def ms(eng, ap, val, writes):
        return P.op(eng, lambda e: e.memset(ap, val), (), writes)

    def bcast_rows(dt_, off, n, cols):
        return bass.AP(dt_, off, [[0, n], [1, cols]])

    def sb3(ap2, mid, inner):
        return bass.AP(ap2.tensor, ap2.offset, [list(ap2.ap[0]), [0, mid], list(ap2.ap[-1])])

    B_scr = Bf("scr_init")
    P.dma(sp, IDN[:, :, :], idn_d[:, :, :], (), (B_const,))
    P.dma(sp, MISC[:, :], misc_d[:, :], (), (B_const,))
    P.dma(sp, MISC2[:, :], misc2_d[:, :], (), (B_const,))
    P.dma(sp, SELt[:, :], sel_d[:, :], (), (B_const,))
    RB = SCR[0:8, 0:32]; RVS = SCR[0:8, 64:64 + RVLEN]; ZR = SCR[0:8, 1408:1408 + RVLEN]
    P.dma(sp, RB, bass.AP(rb_d, 0, [[1, 8], [8, 32]]), (), (B_scr,), allow_slow_non_contiguous=True)
    ms("dve", ZR, 0.0, (B_scr,))
    for (b, lo, hi) in _rv_ranges():
        ts("dve", RVS[:, lo:hi], ZR[:, lo:hi], RB[:, b:b + 1], None, ALU.add, None, (B_scr,), (B_scr,))
    P.dma(sp, RVd[:, :], RVS, (B_scr,), (D_rvd,))
    LQ = [SCR[:, 3072 + i * 128: 3072 + (i + 1) * 128] for i in range(4)]
    for i in range(4):
        P.dma(sp, LQ[i], bcast_rows(lam_d[i], 0, 128, 128), (), (B_scr,))
    tt("dve", LQ[0], LQ[0], LQ[1], ALU.mult, (B_scr,), (B_scr,))
    tt("dve", LQ[2], LQ[2], LQ[3], ALU.mult, (B_scr,), (B_scr,))
    P.op("dve", lambda e: e.reduce_sum(out=LAMT[:, 0:4], in_=LQ[0].rearrange("p (l k) -> p l k", l=4), axis=AX.X), (B_scr,), (B_prm,))
    P.op("dve", lambda e: e.reduce_sum(out=LAMT[:, 4:8], in_=LQ[2].rearrange("p (l k) -> p l k", l=4), axis=AX.X), (B_scr,), (B_prm,))
    act(LAMT[:, 8:16], LAMT[:, 0:8], AF.Exp, (B_prm,), (B_prm,))
    tt("dve", LAM, LAMT[:, 8:12], LAMT[:, 12:16], ALU.subtract, (B_prm,), (B_prm,))
    lam_init = [0.8 - 0.6 * math.exp(-0.3 * l) for l in range(DEPTH)]
    P.dma(sp, DNG[0:64, :], bass.AP(dng_d, 0, [[1, 64], [64, 4]]), (), (B_prm,), allow_slow_non_contiguous=True)
    for l in range(DEPTH):
        ts("dve", LAM[:, l:l + 1], LAM[:, l:l + 1], float(lam_init[l]), None, ALU.add, None, (B_prm,), (B_prm,))
        ts("dve", DNG[0:64, l:l + 1], DNG[0:64, l:l + 1], float(1.0 - lam_init[l]), None, ALU.mult, None, (B_prm,), (B_prm,))
    ts("dve", NLAM, LAM, -1.0, None, ALU.mult, None, (B_prm,), (B_prm,))
    V32 = TMPP[:, 0:8]; VH = SCRB[:, 8192:8200]; VHF = TMPP[:, 8:16]; LO = TMPP[:, 16:24]; T1 = TMPP[:, 24:32]
    ms("dve", V32, 0.0, (B_prm,))
    for base in (32, 96):
        P.dma(sp, V32[base:base + 8, :], bcast_rows(rb_d, 15 * 8, 8, 8), (B_prm,), (B_prm,))
        P.dma(sp, V32[base + 8:base + 16, :], bcast_rows(rb_d, 31 * 8, 8, 8), (B_prm,), (B_prm,))
    cp("dve", VH, V32, (B_prm,), (B_scr,))
    cp("dve", VHF, VH, (B_scr,), (B_prm,))
    tt("dve", LO, V32, VHF, ALU.subtract, (B_prm,), (B_prm,))
    ts("dve", T1, VHF, MISC2[:, 8:9], None, ALU.mult, None, (B_prm, B_const), (B_prm,))
    stt("dve", VALS, LO, MISC2[:, 9:10], T1, ALU.mult, ALU.add, (B_prm, B_const), (B_prm,))
    ts("dve", VALS, VALS, MISC2[:, 10:11], None, ALU.add, None, (B_prm, B_const), (B_prm,))
    PQ = SCR[:, 4096:4096 + T]
    B_pq = Bf("pq")
    P.dma(sp, PQ, pq_d[:, :], (B_scr,), (B_pq,))
    for h in range(8):
        ts("dve", QT[:, h, :], PQ, VALS[:, h:h + 1], None, ALU.mult, None, (B_pq, B_prm), (B_qt[h],))

    dbg_list = []

    def dbg(name, ap, shape, reads, dt=F32):
        if dbg_out is None or name not in dbg_out:
            return
        o = nc.dram_tensor("dbg_" + name, list(shape), dt, kind="ExternalOutput")
        P.dma(sp, o.ap() if len(shape) != 2 else o[:, :], ap, reads, ())
        dbg_list.append("dbg_" + name)

    dbg("vals", VALS, [128, 8], (B_prm,))
    dbg("lam", LAM, [128, 4], (B_prm,))
    dbg("rvd", RVd[:, :], [8, RVLEN], (D_rvd,))
    dbg("qt0", QT[:, 0, :], [128, T], (B_qt[0],), BF16)

    S_Q = 32 ** -0.5

    class WT:
        def __init__(self, ap, buf):
            self.ap = ap; self.buf = buf
        def __getitem__(self, k):
            return self.ap[k]

    def load_w_chunk(slot, sub, src_ap):
        i = ws_rr[0]
        ws_rr[0] = (i + 1) % NWS
        dst = WS[:, i, :].rearrange("p (k f) -> p k f", k=8)
        P.dma(pool, dst, src_ap, (), (B_wsl[i],))
        return WT(dst, B_wsl[i])

    def wsrc(dt_, l, c0, n):
        return dt_[l, :, c0:c0 + n].rearrange("(k p) f -> p k f", p=128)

    def finish_token_tile(g, tile, XF, reads_xf):
        XB = SCRB[:, 20480:21504]
        B_xb = Bs["xb"]
        P.op("act", lambda e: e.activation(out=XB, in_=XF, func=AF.Copy), reads_xf, (B_xb,))
        pst = PS[7].bitcast(BF16)
        for kc in range(8):
            tr(pst[:, kc * 128:(kc + 1) * 128], XB[:, kc * 128:(kc + 1) * 128], (B_xb, B_const), (B_ps[7],))
        cp("dve", XT[:, :, tile * 128:(tile + 1) * 128], pst.rearrange("p (k t) -> p k t", k=8), (B_ps[7],), (B_xt,))

    Bs = {n: Bf(n) for n in ("xb", "xf0", "xf1", "kst", "sg", "rope0", "rope1", "rt", "qrt", "krt", "kr", "vr", "vrf",
                             "kvs", "rstate", "rp", "qft", "sm", "yret", "gt", "smallc", "rsg", "ktl", "vl", "ktc", "vc",
                             "bt", "p0", "p1", "p2", "ep", "stage", "lnx", "lny", "lnj", "lng", "lnb", "asb0", "asb1", "cc0", "cc1",
                             "gg0", "gg1", "hlsb")}

    ONESF = nc.alloc_sbuf_tensor("sb_onesf", [128, 64], F32)
    ms("dve", ONESF[:, :], 1.0, (B_const,))
    XTH2 = nc.alloc_sbuf_tensor("sb_xth2", [128, 8, 2], BF16)
    ST = PRM[:, 164:180]
    KTGR = A0[:, 32768:49152]
    W_OUT = KTGR.rearrange("p (k f) -> p k f", k=8)
    B_wout = Bf("wout")
    ret_bufs = [Bs[n] for n in ("qrt", "krt", "kr", "vr", "vrf", "rp", "qft", "sm")]
    scr_cur = [B_scr, B_pq]

    def scr_phase(names):
        new = [Bs[n] for n in names]
        P.alias(new, scr_cur)
        scr_cur.clear()
        scr_cur.extend(new)

    def layer_norm_tile(Y, Gt, Bt, OUT, rd, wr):
        JUNK = SCR[:, 10240:11264]
        bj = Bs["lnj"]
        P.op("dve", lambda e: e.reduce_sum(out=ST[:, 0:1], in_=Y, axis=AX.X), rd, (B_prm,))
        tt("dve", JUNK, Y, Y, ALU.mult, rd, (bj,))
        P.op("dve", lambda e: e.reduce_sum(out=ST[:, 1:2], in_=JUNK, axis=AX.X), (bj,), (B_prm,))
        ts("dve", ST[:, 2:3], ST[:, 0:1], 1.0 / D, None, ALU.mult, None, (B_prm,), (B_prm,))
        tt("dve", ST[:, 3:4], ST[:, 2:3], ST[:, 2:3], ALU.mult, (B_prm,), (B_prm,))
        stt("dve", ST[:, 4:5], ST[:, 1:2], 1.0 / D, ST[:, 3:4], ALU.mult, ALU.subtract, (B_prm,), (B_prm,))
        act(ST[:, 5:6], ST[:, 4:5], AF.Ln, (B_prm,), (B_prm,), bias=LN_EPS, scale=1.0)
        act(ST[:, 5:6], ST[:, 5:6], AF.Exp, (B_prm,), (B_prm,), scale=-0.5)
        stt("dve", OUT, Y, ST[:, 2:3], Gt, ALU.subtract, ALU.mult, tuple(rd) + (B_prm, Bs["lng"]), wr)
        stt("dve", OUT, OUT, ST[:, 5:6], Bt, ALU.mult, ALU.add, tuple(wr) + (B_prm, Bs["lnb"]), wr)

    def finish_tile(tile, XF, rd, tbank=5):
        XB = SCRB[:, 20480:21504]
        bxb = Bs["lnj"]
        cp("dve", XB, XF, tuple(rd), (bxb,))
        pst = PS[tbank].bitcast(BF16)
        for kc in range(8):
            tr(pst[:, kc * 128:(kc + 1) * 128], XB[:, kc * 128:(kc + 1) * 128], (bxb, B_const), (B_ps[tbank],))
        cp("dve", XT[:, :, tile * 128:(tile + 1) * 128], pst.rearrange("p (k t) -> p k t", k=8), (B_ps[tbank],), (B_xt,))

    def proj8(bank, wtile, t0, n):
        for kc in range(8):
            mm(PS[bank][:, 0:n], wtile[:, kc, :], XT[:, kc, t0:t0 + n], kc == 0, kc == 7,
               (wtile.buf, B_xt), (B_ps[bank],))

    def logsig_params(l):
        X = TMPP[:, 0:16]; NX = TMPP[:, 16:32]; E = TMPP[:, 32:48]; Z = TMPP[:, 48:64]
        P.dma(sp, X, bcast_rows(rdl_d, l * 16, 128, 16), (B_prm,), (B_prm,))
        ts("dve", NX, X, -1.0, None, ALU.mult, None, (B_prm,), (B_prm,))
        tt("dve", E, X, NX, ALU.max, (B_prm,), (B_prm,))
        act(E, E, AF.Exp, (B_prm,), (B_prm,), scale=-1.0)
        ts("dve", Z, E, 2.0, None, ALU.add, None, (B_prm,), (B_prm,))
        P.op("dve", lambda e: e.reciprocal(out=Z, in_=Z), (B_prm,), (B_prm,))
        tt("dve", Z, Z, E, ALU.mult, (B_prm,), (B_prm,))
        tt("dve", E, Z, Z, ALU.mult, (B_prm,), (B_prm,))
        ts("dve", NX, E, 1.0 / 9, 1.0 / 7, ALU.mult, ALU.add, (B_prm,), (B_prm,))
        for cst in (1.0 / 5, 1.0 / 3, 1.0):
            tt("dve", NX, NX, E, ALU.mult, (B_prm,), (B_prm,))
            ts("dve", NX, NX, cst, None, ALU.add, None, (B_prm,), (B_prm,))
        tt("dve", NX, NX, Z, ALU.mult, (B_prm,), (B_prm,))
        ts("dve", X, X, 0.0, None, ALU.min, None, (B_prm,), (B_prm,))
        stt("dve", LGREP, NX, -2.0, X, ALU.mult, ALU.add, (B_prm,), (B_prm,))
        LGR4 = LGREP.rearrange("p (d q h) -> p d q h", d=2, q=4)
        cp("dve", LGP[0:64, :].rearrange("p (d q) -> p d q", d=2), LGR4[0:64, :, :, 0], (B_prm,), (B_prm,))
        cp("dve", LGP[64:128, :].rearrange("p (d q) -> p d q", d=2), LGR4[64:128, :, :, 1], (B_prm,), (B_prm,))
        ts("dve", TMPP[:, 0:8], LGREP[:, 0:8], MISC[:, 0:1], None, ALU.mult, None, (B_prm, B_const), (B_prm,))
        ts("dve", TMPP[:, 8:16], LGREP[:, 8:16], MISC[:, 1:2], None, ALU.mult, None, (B_prm, B_const), (B_prm,))
        act(W16, TMPP[:, 0:16], AF.Exp, (B_prm,), (B_prm,))
        act(DEC, LGP, AF.Exp, (B_prm,), (B_prm,), scale=128.0)

    def rope_tile(g, tt_, rb):
        RO = SCR[:, rb * 1024:(rb + 1) * 1024].rearrange("p (c t) -> p c t", c=2)
        ri = 0 if g == "p" else 2
        P.dma(sp, RO, rope_d[ri:ri + 2, :, tt_ * 512:(tt_ + 1) * 512].rearrange("c p t -> p c t"), (), (Bs[f"rope{rb}"],))
        return RO

    def rotary(bank_a, bank_b, RO, rb, OUT, out_buf):
        RT = SCR[:, 2048:3072]
        tt("dve", RT[:, 0:512], PS[bank_a], RO[:, 0, :], ALU.mult, (B_ps[bank_a], Bs[f"rope{rb}"]), (Bs["rt"],))
        tt("dve", RT[:, 512:1024], PS[bank_b], RO[:, 1, :], ALU.mult, (B_ps[bank_b], Bs[f"rope{rb}"]), (Bs["rt"],))
        tt("dve", OUT, RT[:, 0:512], RT[:, 512:1024], ALU.add, (Bs["rt"],), (out_buf,))

    QrT = KTGR[:, 0:2048]; KrT = KTGR[:, 2048:4096]; QfT = KTGR[:, 4096:6144]; QbT = KTGR[:, 6144:8192]
    KR = KTGR[:, 8192:10240].rearrange("p (n d) -> p n d", n=16); VR = KTGR[:, 10240:12288].rearrange("p (n d) -> p n d", n=16)
    VRF = KTGR[:, 12288:16384].rearrange("p (r n d) -> p r n d", r=2, n=16)
    RPb = KTGR[:, 12288:16384].rearrange("p (r n d) -> p r n d", r=2, n=16)
    SMt = KTGR[:, 4096 + 0:4096 + 0]
    KVS = SCR[:, 3072:7168].rearrange("p (r n d) -> p r n d", r=2, n=16)
    SMALLC = SCR[:, 7168:8192].rearrange("p (c d) -> p c d", c=8)
    MASK = SCR[:, 8192:8448].rearrange("p (h d) -> p h d", h=2)
    WFT = SCR[:, 8448:8704].rearrange("p (h d) -> p h d", h=2)
    QFB = SCR[:, 8704:8960].rearrange("p (h d) -> p h d", h=2)
    MTMP = SCR[:, 8960:9216].rearrange("p (h d) -> p h d", h=2)
    RST = SCR[:, 9216:9472].rearrange("p (h d) -> p h d", h=2)
    RSTART = SCR[:, 9472:9728].rearrange("p (h d) -> p h d", h=2)
    RSGS = KTGR[:, 8192:10240].bitcast(F32).rearrange("p (r d) -> p r d", r=8)
    SMB = SCRB[:, 21504:22528].rearrange("p (j h d) -> p j h d", j=4, h=2)
    BDm = SMALLC[:, 4, :]

    def ret_pass1(l, g, p):
        slot = p % 2
        wk = load_w_chunk(slot, 0, wsrc(wf_d, l, (8 + p) * 128, 128))
        wks = load_w_chunk(slot, 1, wsrc(wf_d, l, (12 + p) * 128, 128))
        wv = load_w_chunk(slot, 2, wsrc(wt_d, l, p * 128, 128))
        for t4 in range(4):
            rb = t4 % 2
            RO = rope_tile(g, t4, rb)
            bk = 2 * (t4 % 2)
            proj8(bk, wk, t4 * 512, 512)
            proj8(bk + 1, wks, t4 * 512, 512)
            rotary(bk, bk + 1, RO, rb, KrT[:, t4 * 512:(t4 + 1) * 512], Bs["krt"])
            for j in range(4):
                tok = (t4 * 4 + j) * 128
                for kc in range(8):
                    mm(PS[4][:, j * 128:(j + 1) * 128], XT[:, kc, tok:tok + 128], wv[:, kc, :], kc == 0, kc == 7,
                       (B_xt, wv.buf), (B_ps[4],))
            P.op("act", lambda e, o=VR[:, t4 * 4:(t4 + 1) * 4, :], i=PS[4].rearrange("p (j d) -> p j d", j=4):
                 e.activation(out=o, in_=i, func=AF.Copy), (B_ps[4],), (Bs["vr"],))
            pst = PS[5].bitcast(BF16)
            for j in range(4):
                tok = (t4 * 4 + j) * 128
                tr(pst[:, j * 128:(j + 1) * 128], KrT[:, tok:tok + 128], (Bs["krt"], B_const), (B_ps[5],))
            cp("dve", KR[:, t4 * 4:(t4 + 1) * 4, :], pst[:, 0:512].rearrange("p (j d) -> p j d", j=4), (B_ps[5],), (Bs["kr"],))
        ms("dve", WFT, 0.125, (Bs["smallc"],))
        for d_ in range(2):
            for hh in range(2):
                ts("dve", WFT[:, d_, hh * 64:(hh + 1) * 64], WFT[:, d_, hh * 64:(hh + 1) * 64],
                   W16[:, d_ * 8 + 2 * p + hh: d_ * 8 + 2 * p + hh + 1], None, ALU.mult, None, (B_prm, Bs["smallc"]), (Bs["smallc"],))
        for d_ in range(2):
            tt("dve", VRF[:, d_, :, :], VR, sb3(WFT[:, d_, :], 16, 128), ALU.mult, (Bs["vr"], Bs["smallc"]), (Bs["vrf"],))
        for n0 in range(0, 16, 2):
            for i in range(2):
                for d_ in range(2):
                    c0 = (d_ * 2 + i) * 128
                    mm(PS[6][:, c0:c0 + 128], KR[:, n0 + i, :], VRF[:, d_, n0 + i, :], True, True, (Bs["kr"], Bs["vrf"]), (B_ps[6],))
            bd4 = bass.AP(BDm.tensor, BDm.offset, [list(BDm.ap[0]), [0, 2], [0, 2], list(BDm.ap[-1])])
            tt("dve", KVS[:, :, n0:n0 + 2, :], PS[6].rearrange("p (r i d) -> p r i d", r=2, i=2), bd4, ALU.mult,
               (B_ps[6], Bs["smallc"]), (Bs["kvs"],))
        if g == "p":
            for d_ in range(2):
                ms("dve", RST[:, d_, :], 0.0, (Bs["rstate"],))
                order = range(16) if d_ == 0 else range(15, -1, -1)
                for n in order:
                    stt("dve", RST[:, d_, :], RST[:, d_, :], DEC[:, d_ * 4 + p: d_ * 4 + p + 1], KVS[:, d_, n, :],
                        ALU.mult, ALU.add, (Bs["rstate"], Bs["kvs"], B_prm), (Bs["rstate"],))
                P.dma(sp, RSd[(d_ * 4 + p) * 128:(d_ * 4 + p + 1) * 128, :], RST[:, d_, :], (Bs["rstate"],), (D_rsd,))
        P.dma(sp, RKT[p, :, :], KrT, (Bs["krt"],), (D_ret[p],))
        P.dma(sp, RVR[p, :, :], KTGR[:, 10240:12288], (Bs["vr"],), (D_ret[p],))
        P.dma(sp, RKV[p, :, :], SCR[:, 3072:7168], (Bs["kvs"],), (D_ret[p],))

    def ret_pass2(l, g, p):
        slot = p % 2
        wq = load_w_chunk(slot, 0, wsrc(wf_d, l, p * 128, 128))
        wqs = load_w_chunk(slot, 1, wsrc(wf_d, l, (4 + p) * 128, 128))
        P.alias([Bs["kvs"]], [Bs["kst"], Bs["sg"]])
        P.alias([Bs["rsg"]], [Bs["kr"]])
        P.dma(sp, KrT, RKT[p, :, :], (D_ret[p],), (Bs["krt"],))
        P.dma(sp, KTGR[:, 10240:12288], RVR[p, :, :], (D_ret[p],), (Bs["vr"],))
        P.dma(sp, SCR[:, 3072:7168], RKV[p, :, :], (D_ret[p],), (Bs["kvs"],))
        for t4 in range(4):
            rb = t4 % 2
            RO = rope_tile(g, t4, rb)
            proj8(6, wq, t4 * 512, 512)
            proj8(7, wqs, t4 * 512, 512)
            rotary(6, 7, RO, rb, QrT[:, t4 * 512:(t4 + 1) * 512], Bs["qrt"])
        for hh in range(2):
            h = 2 * p + hh
            act(MTMP[:, 0, :], SMALLC[:, 0, :], AF.Exp, (Bs["smallc"], B_prm), (Bs["smallc"],), scale=LGREP[:, h:h + 1])
            act(MTMP[:, 1, :], SMALLC[:, 1, :], AF.Exp, (Bs["smallc"], B_prm), (Bs["smallc"],), scale=LGREP[:, 8 + h:9 + h])
            stt("dve", MTMP[:, 0, :], MTMP[:, 0, :], 0.125, SMALLC[:, 2, :], ALU.mult, ALU.mult, (Bs["smallc"],), (Bs["smallc"],))
            stt("dve", MTMP[:, 1, :], MTMP[:, 1, :], 0.125, SMALLC[:, 3, :], ALU.mult, ALU.mult, (Bs["smallc"],), (Bs["smallc"],))
            tt("dve", MASK[:, hh, :], MTMP[:, 0, :], MTMP[:, 1, :], ALU.add, (Bs["smallc"],), (Bs["smallc"],))
        act(QFB[:, 0, :], SMALLC[:, 5, :], AF.Exp, (Bs["smallc"], B_prm), (Bs["smallc"],), scale=LGP[:, p:p + 1])
        act(QFB[:, 1, :], SMALLC[:, 6, :], AF.Exp, (Bs["smallc"], B_prm), (Bs["smallc"],), scale=LGP[:, 4 + p:5 + p])
        tt("dve", QfT.rearrange("p (n d) -> p n d", n=16), QrT.rearrange("p (n d) -> p n d", n=16), sb3(QFB[:, 0, :], 16, 128),
           ALU.mult, (Bs["qrt"], Bs["smallc"]), (Bs["qft"],))
        tt("dve", QbT.rearrange("p (n d) -> p n d", n=16), QrT.rearrange("p (n d) -> p n d", n=16), sb3(QFB[:, 1, :], 16, 128),
           ALU.mult, (Bs["qrt"], Bs["smallc"]), (Bs["qft"],))
        for d_ in range(2):
            if g == "p":
                src = bass.AP(RSg8, (d_ * 4 + p) * 128 * 128, [[128, 128], [1024 * 128, 8], [1, 128]])
                P.dma(sp, RSGS, src, (D_rsg,), (Bs["rsg"],))
                ecol = 34 if d_ == 0 else 42
                act(W8[:, d_ * 8:(d_ + 1) * 8], MISC[:, ecol:ecol + 8], AF.Exp, (B_const, B_prm), (B_prm,), scale=LGP[:, d_ * 4 + p: d_ * 4 + p + 1])
                msk = MISC[:, 50:58] if d_ == 0 else MISC2[:, 0:8]
                tt("dve", W8[:, d_ * 8:(d_ + 1) * 8], W8[:, d_ * 8:(d_ + 1) * 8], msk, ALU.mult, (B_prm, B_const), (B_prm,))
                ts("dve", RSTART[:, d_, :], RSGS[:, 0, :], W8[:, d_ * 8:d_ * 8 + 1], None, ALU.mult, None, (Bs["rsg"], B_prm), (Bs["rstate"],))
                for r in range(1, 8):
                    stt("dve", RSTART[:, d_, :], RSGS[:, r, :], W8[:, d_ * 8 + r:d_ * 8 + r + 1], RSTART[:, d_, :], ALU.mult, ALU.add,
                        (Bs["rsg"], B_prm, Bs["rstate"]), (Bs["rstate"],))
                cp("dve", RST[:, d_, :], RSTART[:, d_, :], (Bs["rstate"],), (Bs["rstate"],))
            else:
                ms("dve", RST[:, d_, :], 0.0, (Bs["rstate"],))
            order = range(16) if d_ == 0 else range(15, -1, -1)
            for n in order:
                cp("dve", RPb[:, d_, n, :], RST[:, d_, :], (Bs["rstate"],), (Bs["rp"],))
                stt("dve", RST[:, d_, :], RST[:, d_, :], DEC[:, d_ * 4 + p: d_ * 4 + p + 1], KVS[:, d_, n, :],
                    ALU.mult, ALU.add, (Bs["rstate"], Bs["kvs"], B_prm), (Bs["rstate"],))
        YT = SCR[:, 9728:10240]; GTt = SCR[:, 10240:10752]; CR = SCR[:, 0:512]; SQ = SCRB[:, 1024:1536]
        for t4 in range(4):
            for j in range(4):
                n = t4 * 4 + j
                c = n * 128
                mm(PS[0][:, j * 128:(j + 1) * 128], KrT[0:64, c:c + 128], QrT[0:64, c:c + 128], True, True, (Bs["krt"], Bs["qrt"]), (B_ps[0],))
                mm(PS[1][:, j * 128:(j + 1) * 128], KrT[64:128, c:c + 128], QrT[64:128, c:c + 128], True, True, (Bs["krt"], Bs["qrt"]), (B_ps[1],))
            for hh in range(2):
                tt("dve", SMB[:, :, hh, :], PS[hh].rearrange("p (j d) -> p j d", j=4), sb3(MASK[:, hh, :], 4, 128), ALU.mult,
                   (B_ps[hh], Bs["smallc"]), (Bs["sm"],))
            for j in range(4):
                n = t4 * 4 + j
                bank = 2 + j // 2
                c0 = (j % 2) * 256
                mm(PS[bank][:, c0:c0 + 256], VR[:, n, :], SMB[:, j, :, :].rearrange("p h d -> p (h d)"), True, True, (Bs["vr"], Bs["sm"]), (B_ps[bank],))
                c = n * 128
                mm(PS[4][:, j * 128:(j + 1) * 128], RPb[:, 0, n, :], QfT[:, c:c + 128], True, False, (Bs["rp"], Bs["qft"]), (B_ps[4],))
                mm(PS[4][:, j * 128:(j + 1) * 128], RPb[:, 1, n, :], QbT[:, c:c + 128], False, True, (Bs["rp"], Bs["qft"]), (B_ps[4],))
            act(CR, PS[4], AF.Copy, (B_ps[4],), (Bs["rope0"],))
            for j in range(4):
                bank = 2 + j // 2
                c0 = (j % 2) * 256
                tt("dve", YT[0:64, j * 128:(j + 1) * 128], PS[bank][0:64, c0:c0 + 128], CR[0:64, j * 128:(j + 1) * 128], ALU.add,
                   (B_ps[bank], Bs["rope0"]), (Bs["yret"],))
                tt("dve", YT[64:128, j * 128:(j + 1) * 128], PS[bank][64:128, c0 + 128:c0 + 256], CR[64:128, j * 128:(j + 1) * 128], ALU.add,
                   (B_ps[bank], Bs["rope0"]), (Bs["yret"],))
            P.dma(sp, GTt, gate_d[p, :, t4 * 512:(t4 + 1) * 512], (D_gate,), (Bs["gt"],))
            tt("dve", SQ, YT, YT, ALU.mult, (Bs["yret"],), (Bs["rope0"],))
            mm(PS[5], bdones, SQ, True, True, (B_const, Bs["rope0"]), (B_ps[5],))
            act(CR, PS[5], AF.Ln, (B_ps[5],), (Bs["rope0"],), bias=HN_EPS, scale=1.0 / 64)
            act(CR, CR, AF.Exp, (Bs["rope0"],), (Bs["rope0"],), scale=-0.5)
            tt("dve", YT, YT, CR, ALU.mult, (Bs["yret"], Bs["rope0"]), (Bs["yret"],))
            tt("dve", MIXT[:, p, t4 * 512:(t4 + 1) * 512], YT, GTt, ALU.mult, (Bs["yret"], Bs["gt"]), (B_mix[p],))

    def proj_dq_gate(l, g):
        PQt = SCR[:, 5120:7168]
        P.alias([Bs["sg"]], [Bs["kvs"], Bs["kst"]])
        P.dma(sp, PQt, pq_d[:, :], (), (Bs["kvs"],))
        for h in range(8):
            ts("dve", QT[:, h, :], PQt, VALS[:, h:h + 1], None, ALU.mult, None, (Bs["kvs"], B_prm), (B_qt[h],))
        SG = SCR[:, 4096:4608]
        for h in range(8):
            wq = load_w_chunk(0, 0, wsrc(wf_d, l, (20 + h) * 128, 128))
            for t4 in range(4):
                bq = t4 % 4
                tq = slice(t4 * 512, (t4 + 1) * 512)
                proj8(bq, wq, t4 * 512, 512)
                for r0 in (0, 64):
                    act(QT[r0:r0 + 32, h, tq], PS[bq][r0:r0 + 32, :], AF.Identity, (B_ps[bq],), (B_qt[h],), scale=float(S_Q))
        for p in range(4):
            wg = load_w_chunk(0, 0, wsrc(wf_d, l, (16 + p) * 128, 128))
            for t4 in range(4):
                b = 4 + (t4 % 2)
                proj8(b, wg, t4 * 512, 512)
                act(SG, PS[b], AF.Silu, (B_ps[b],), (Bs["sg"],))
                P.dma(sp, gate_d[p, :, t4 * 512:(t4 + 1) * 512], SG, (Bs["sg"],), (D_gate,))

    def proj_dk_dv(l, g):
        KST = SCRB[:, 6144:8192]
        P.alias([Bs["kst"]], [Bs["kvs"], Bs["sg"]])
        for h in range(8):
            wk = load_w_chunk(0, 0, wsrc(wf_d, l, (28 + h) * 128, 128))
            for t4 in range(4):
                bk_ = t4 % 4
                tq = slice(t4 * 512, (t4 + 1) * 512)
                proj8(bk_, wk, t4 * 512, 512)
                for r0 in (0, 64):
                    cp("dve", KST[r0:r0 + 32, tq], PS[bk_][r0:r0 + 32, :], (B_ps[bk_],), (Bs["kst"],))
            kr0 = (h % 2) * 64
            P.dma(sp, KTd[g][h // 2][kr0:kr0 + 32, :], KST[0:32, :], (Bs["kst"],), (D_ktd[g],))
            P.dma(sp, KTd[g][h // 2][kr0 + 32:kr0 + 64, :], KST[64:96, :], (Bs["kst"],), (D_ktd[g],))
        VST = VGt[:, 0:8 * 16 * VW]
        ms("pool", VST, 0.0, (B_vg,))
        VST4 = VST.rearrange("p (h b e) -> p h b e", h=8, b=16)
        ms("pool", VST4[:, :, :, 64:65], 1.0, (B_vg,))
        for jv in range(4):
            slot = jv % 2
            wv = load_w_chunk(slot, 0, wsrc(wt_d, l, 512 + jv * 128, 128))
            for t0 in range(0, 16, 4):
                b = 6 + ((t0 // 4) % 2)
                for j in range(4):
                    tok = (t0 + j) * 128
                    for kc in range(8):
                        mm(PS[b][:, j * 128:(j + 1) * 128], XT[:, kc, tok:tok + 128], wv[:, kc, :], kc == 0, kc == 7,
                           (B_xt, wv.buf), (B_ps[b],))
                pe0 = VST.ap[0]
                o = bass.AP(VST.tensor, VST.offset + (2 * jv) * 16 * VW + t0 * VW, [list(pe0), [VW, 4], [16 * VW, 2], [1, 64]])
                cp("dve", o, PS[b].rearrange("p (j h e) -> p j h e", j=4, h=2), (B_ps[b],), (B_vg,))
        for h in range(8):
            P.dma(sp, Vd[g][h][:, :], VST[:, h * 16 * VW:(h + 1) * 16 * VW], (B_vg,), (D_vd[g],))

    D_kt4 = [Bf(f"d_kt4_{i}") for i in range(4)]; D_v4 = [Bf(f"d_v4_{i}") for i in range(8)]

    def gather_stage(stage):
        if stage == 0:
            for pc in range(4):
                P.collective("AllGather", G4, KTd["p"][pc].ap().opt(), KTg4[pc].ap().opt(), (D_ktd["p"],), (D_kt4[pc],))
            for h in range(8):
                P.collective("AllGather", G4, Vd["p"][h].ap().opt(), Vg4[h].ap().opt(), (D_vd["p"],), (D_v4[h],))
        else:
            for pc in range(4):
                P.collective("AllGather", G2, KTg4[pc].ap().opt(), KTg8[pc].ap().opt(), (D_kt4[pc],), (D_ktg,))
            for h in range(8):
                P.collective("AllGather", G2, Vg4[h].ap().opt(), Vg8[h].ap().opt(), (D_v4[h],), (D_vg,))

    KTL = SCRB[:, 0:2048]; KTC = SCRB[:, 2048:4096]
    VL = SCRB[:, 4096:4096 + 16 * VW]; VC = SCRB[:, 5376:5376 + 16 * VW]
    BT = SCRB[:, 6656:6656 + 3072].rearrange("p (d q) -> p d q", d=6)
    PT = [SCRB[:, 9728:10752], SCRB[:, 10752:11776], SCRB[:, 18944:19968]]
    A12 = [SCR[:, 5888:6400], SCR[:, 6400:6912]]
    R12 = [SCR[:, 6912:7424], SCR[:, 7424:7936]]
    T1e = SCR[:, 7936:8448]; Oe = SCR[:, 8448:8960]
    SQe = SCRB[:, 17920:18432]; STG = SCRB[:, 18432:18944]

    def attention(l, g):
        prompt = g == "p"
        NB = 128 if prompt else 16
        NK = NB * 128
        scr_phase(["ktl", "vl", "ktc", "vc", "bt", "p0", "p1", "p2", "ep", "stage"])
        P.alias([B_ktg], ret_bufs + [Bs["rsg"], B_wout])
        aug = augkp_d if prompt else augks_d
        P.dma(sp, KTG[32:64, 0:NK], aug[:, :], (), (B_ktg,))
        P.dma(sp, KTG[96:128, 0:NK], aug[:, :], (), (B_ktg,))
        ms("pool", KTL, 0.0, (Bs["ktl"],))
        if prompt:
            ms("pool", KTC, 0.0, (Bs["ktc"],))
            P.dma(sp, KTC[32:64, :], augc_d[:, :], (Bs["ktc"],), (Bs["ktc"],))
            P.dma(sp, KTC[96:128, :], augc_d[:, :], (Bs["ktc"],), (Bs["ktc"],))
        pend = [None, None]
        for h in range(8):
            for half, r0 in ((0, 0), (1, 64)):
                row = (h % 2) * 64 + half * 32
                pc = h // 2
                if prompt:
                    src = bass.AP(KTg8[pc], row * T, [[T, 32], [128 * T, 8], [1, T]])
                    P.dma(sp, KTG[r0:r0 + 32, :].rearrange("p (r t) -> p r t", r=8), src, (D_ktg,), (B_ktg,))
                    for ci, c0 in ((0, 15 * 128), (1, 0)):
                        srcc = bass.AP(KTg8[pc], row * T + c0, [[T, 32], [128 * T, 8], [1, 128]])
                        P.dma(sp, KTC[r0:r0 + 32, ci * 1024:(ci + 1) * 1024].rearrange("p (r t) -> p r t", r=8), srcc, (D_ktg,), (Bs["ktc"],))
                else:
                    P.dma(sp, KTG[r0:r0 + 32, 0:T], KTd["s"][pc][row:row + 32, :], (D_ktd["s"],), (B_ktg,))
                P.dma(sp, KTL[r0:r0 + 32, :], KTd[g][pc][row:row + 32, :], (D_ktd[g],), (Bs["ktl"],))
            if prompt:
                src = bass.AP(Vg8[h], 0, [[16 * VW, 128], [128 * 16 * VW, 8], [1, 16 * VW]])
                P.dma(sp, VGt[:, :].rearrange("p (r f) -> p r f", r=8), src, (D_vg,), (B_vg,))
                for ci, c0 in ((0, 15 * VW), (1, 0)):
                    srcc = bass.AP(Vg8[h], c0, [[16 * VW, 128], [128 * 16 * VW, 8], [1, VW]])
                    P.dma(sp, VC[:, ci * 8 * VW:(ci + 1) * 8 * VW].rearrange("p (r f) -> p r f", r=8), srcc, (D_vg,), (Bs["vc"],))
            else:
                P.dma(sp, VGt[:, 0:16 * VW], Vd["s"][h][:, :], (D_vd["s"],), (B_vg,))
            P.dma(sp, VL, Vd[g][h][:, :], (D_vd[g],), (Bs["vl"],))
            for di in range(6):
                delta = di - 1
                src = bass.AP(RVd, h * RVLEN + (U0 - 128 * delta - 127), [[1, 128], [1, 512]])
                P.dma(pool, BT[:, di, :], src, (D_rvd,), (Bs["bt"],))
            for t in range(4):
                tq = slice(t * 512, (t + 1) * 512)
                blocks = [(KTG, j * 128, VGt, j * VW, None, B_ktg, B_vg) for j in range(NB)]
                for di in range(6):
                    jl = 4 * t + di - 1
                    if 0 <= jl < 16:
                        blocks.append((KTL, jl * 128, VL, jl * VW, BT[:, di, :], Bs["ktl"], Bs["vl"]))
                if prompt and t == 0:
                    blocks += [(KTC, i * 128, VC, i * VW, BT[:, 0, :], Bs["ktc"], Bs["vc"]) for i in range(8)]
                if prompt and t == 3:
                    blocks += [(KTC, i * 128, VC, i * VW, BT[:, 5, :], Bs["ktc"], Bs["vc"]) for i in range(8, 16)]
                nb = len(blocks)

                def qk(idx):
                    Ks, kc0, Vs, vc0, bias, bk, bv = blocks[idx]
                    sb_ = idx % 2
                    S1, S2 = PS[2 * sb_], PS[2 * sb_ + 1]
                    bS = (B_ps[2 * sb_], B_ps[2 * sb_ + 1])
                    mm(S1, Ks[0:64, kc0:kc0 + 128], QT[0:64, h, tq], True, bias is None, (bk, B_qt[h]), bS)
                    mm(S2, Ks[64:128, kc0:kc0 + 128], QT[64:128, h, tq], True, bias is None, (bk, B_qt[h]), bS)
                    if bias is not None:
                        mm(S1, antiid, bias, False, True, (B_const, Bs["bt"]), bS)
                        mm(S2, antiid, bias, False, True, (B_const, Bs["bt"]), bS)

                qk(0)
                qk(1)
                for idx in range(nb):
                    Ks, kc0, Vs, vc0, bias, bk, bv = blocks[idx]
                    sb_ = idx % 2
                    bS = (B_ps[2 * sb_], B_ps[2 * sb_ + 1])
                    pb_ = idx % 3
                    bp = Bs[f"p{pb_}"]
                    act(PT[pb_], PSW[sb_][:, :], AF.Exp, bS, (bp,))
                    if idx + 2 < nb:
                        qk(idx + 2)
                    mm(PS[4][0:VW, :], Vs[:, vc0:vc0 + VW], PT[pb_][:, 0:512], idx == 0, idx == nb - 1, (bv, bp), (B_ps[4],))
                    mm(PS[5][0:VW, :], Vs[:, vc0:vc0 + VW], PT[pb_][:, 512:1024], idx == 0, idx == nb - 1, (bv, bp), (B_ps[5],))
                    if idx == 2 and pend[0] is not None:
                        pend[0]()
                    if idx == 7 and pend[1] is not None:
                        pend[1]()
                bep = Bs["ep"]
                for k2 in range(2):
                    cp("dve", A12[k2][0:VW, :], PS[4 + k2][0:VW, :], (B_ps[4 + k2],), (bep,))

                def part_b1(bep=bep):
                    for k2 in range(2):
                        mm(PS[6 + k2][0:64, :], ONESF[64:65, 0:64], A12[k2][64:65, :], True, True, (B_const, bep), (B_ps[6 + k2],))
                    for k2 in range(2):
                        P.op("dve", lambda e, o=R12[k2][0:64, :], i=PS[6 + k2][0:64, :]: e.reciprocal(out=o, in_=i), (B_ps[6 + k2],), (bep,))
                    tt("dve", T1e[0:64, :], A12[0][0:64, :], R12[0][0:64, :], ALU.mult, (bep,), (bep,))
                    tt("dve", R12[1][0:64, :], A12[1][0:64, :], R12[1][0:64, :], ALU.mult, (bep,), (bep,))
                    stt("dve", Oe[0:64, :], R12[1][0:64, :], NLAM[0:64, l:l + 1], T1e[0:64, :], ALU.mult, ALU.add, (bep, B_prm), (bep,))
                    tt("dve", SQe[0:64, :], Oe[0:64, :], Oe[0:64, :], ALU.mult, (bep,), (bep,))
                    pend[0] = None

                def part_b2(bep=bep, h=h, tq=tq):
                    mm(PS[6][0:64, :], bdones[0:64, 0:64], SQe[0:64, :], True, True, (B_const, bep), (B_ps[6],))
                    act(R12[0][0:64, :], PS[6][0:64, :], AF.Ln, (B_ps[6],), (bep,), bias=HN_EPS, scale=1.0 / 64)
                    act(R12[0][0:64, :], R12[0][0:64, :], AF.Exp, (bep,), (bep,), scale=-0.5)
                    tt("dve", Oe[0:64, :], Oe[0:64, :], R12[0][0:64, :], ALU.mult, (bep,), (bep,))
                    c = 4 + h // 2
                    if h % 2 == 0:
                        ts("dve", MIXT[0:64, c, tq], Oe[0:64, :], DNG[0:64, l:l + 1], None, ALU.mult, None, (bep, B_prm), (B_mix[c],))
                    else:
                        ts("dve", STG[0:64, :], Oe[0:64, :], DNG[0:64, l:l + 1], None, ALU.mult, None, (bep, B_prm), (Bs["stage"],))
                        P.dma(sp, MIXT[64:128, c, tq], STG[0:64, :], (Bs["stage"],), (B_mix[c],))
                    pend[1] = None
                pend[0] = part_b1
                pend[1] = part_b2
        if pend[0] is not None:
            pend[0]()
        if pend[1] is not None:
            pend[1]()

    XF2 = [SCR[:, 8192:9216], SCR[:, 9216:10240]]
    Gt = SCR[:, 6144:7168]; Bt_ = SCR[:, 7168:8192]

    def wout_ln1(l, g):
        prompt = g == "p"
        scr_phase(["xf0", "xf1", "lnj", "lng", "lnb", "hlsb"])
        P.alias([B_wout], [B_ktg])
        for q4 in range(4):
            P.dma(pool, W_OUT[:, :, q4 * 256:(q4 + 1) * 256], wo_d[l, :, q4 * 256:(q4 + 1) * 256].rearrange("(k p) f -> p k f", p=128), (), (B_wout,))
        P.dma(sp, Gt, bcast_rows(lng_d, (l * 2 + 0) * D, 128, D), (), (Bs["lng"],))
        P.dma(sp, Bt_, bcast_rows(lnb_d, (l * 2 + 0) * D, 128, D), (), (Bs["lnb"],))
        xsrc = x_in[g] if l == 0 else X2[g]
        xrd = () if l == 0 else (D_x2[g],)
        for tile in range(16):
            k = tile % 2
            rows = slice(tile * 128, (tile + 1) * 128)
            for hf in range(2):
                for c in range(8):
                    mm(PSW[k][:, hf * 512:(hf + 1) * 512], MIXT[:, c, rows], W_OUT[:, c, hf * 512:(hf + 1) * 512], c == 0, c == 7,
                       (B_mix[c], B_wout), (B_ps[2 * k], B_ps[2 * k + 1]))
            XF = XF2[k]; bxf = Bs[f"xf{k}"]
            P.dma(sp, XF, xsrc[rows, :], xrd, (bxf,))
            stt("dve", XF, XF, float(ALPHA), PSW[k][:, :], ALU.mult, ALU.add, (bxf, B_ps[2 * k], B_ps[2 * k + 1]), (bxf,))
            layer_norm_tile(XF, Gt, Bt_, XF, (bxf,), (bxf,))
            P.dma(sp, X1[g][rows, :], XF, (bxf,), (D_x1[g],))
            if prompt and tile == 0:
                P.dma(sp, HLd[0:1, :], XF[0:1, :], (bxf,), (D_hld,))
            if prompt and tile == 15:
                P.dma(sp, HLd[1:2, :], XF[127:128, :], (bxf,), (D_hld,))
            finish_tile(tile, XF, (bxf,))
        if prompt:
            P.collective("AllGather", G4, HLd.ap().opt(), HLg4.ap().opt(), (D_hld,), (D_hlg,))
            P.collective("AllGather", G2, HLg4.ap().opt(), HLg8.ap().opt(), (D_hlg,), (D_hlg,))
            HLSB = SCRB[0:16, 0:1024]
            P.dma(pool, HLSB, HLg8[:, :], (D_hlg,), (Bs["hlsb"],))
            for kc in range(8):
                mm(PS[6][:, kc * 2:(kc + 1) * 2], HLSB[:, kc * 128:(kc + 1) * 128], SELt[:, :], True, True, (Bs["hlsb"], B_const), (B_ps[6],))
            cp("dve", XTH[:, :, :], PS[6][:, 0:16].rearrange("p (k j) -> p k j", k=8), (B_ps[6],), (B_xth,))
        else:
            ms("dve", XTH[:, :, :], 0.0, (B_xth,))

    ASB = [SCR[:, 0:1026], SCR[:, 1032:1032 + 1026]]
    CC = SCR[:, 2064:3088]
    GG = [SCR[:, 3088:4112], SCR[:, 4112:5136]]

    def ffn_ln2(l, g, last):
        scr_phase(["xf0", "xf1", "lnj", "lng", "lnb", "asb0", "asb1", "cc0", "gg0", "gg1"])
        P.alias([B_ht, B_wd], B_qt + B_mix + [B_ktg, B_wout] + ret_bufs)
        for t3 in range(3):
            P.dma(sp, CONV[:, :, t3:t3 + 1], bass.AP(cw_d, (l * 3 + t3) * DFF, [[1, 128], [128, NFC], [1, 1]]), (), (B_conv,), allow_slow_non_contiguous=True)
        P.dma(sp, CONV[:, :, 3:4], bass.AP(cb_d, l * DFF, [[1, 128], [128, NFC], [1, 1]]), (), (B_conv,), allow_slow_non_contiguous=True)
        P.dma(sp, Gt, bcast_rows(lng_d, (l * 2 + 1) * D, 128, D), (), (Bs["lng"],))
        P.dma(sp, Bt_, bcast_rows(lnb_d, (l * 2 + 1) * D, 128, D), (), (Bs["lnb"],))
        cp("dve", XTH2[:, :, :], XT[:, :, 1023:1025], (B_xt,), (B_xth,))
        for half in range(2):
            tok0 = half * 1024
            for fc in range(NFC):
                slot = fc % 2
                wa = load_w_chunk(slot, 0, wsrc(wu_d, l, fc * 128, 128))
                wv = load_w_chunk(slot, 1, wsrc(wu_d, l, DFF + fc * 128, 128))
                if half == 0 and fc in (2, 5, 8, 11):
                    q4 = (fc - 2) // 3
                    P.dma(pool, WD[:, :, q4 * 256:(q4 + 1) * 256], wd_d[l, :, q4 * 256:(q4 + 1) * 256].rearrange("(c p) f -> p c f", p=128), (), (B_wd,))
                proj8(0, wa, tok0, 512)
                proj8(1, wa, tok0 + 512, 512)
                for kc in range(8):
                    lo = XTH[:, kc, 0:1] if half == 0 else XTH2[:, kc, 0:1]
                    hi = XTH2[:, kc, 1:2] if half == 0 else XTH[:, kc, 1:2]
                    mm(PS[2][:, 0:1], wa[:, kc, :], lo, kc == 0, kc == 7, (wa.buf, B_xth), (B_ps[2],))
                for kc in range(8):
                    hi = XTH2[:, kc, 1:2] if half == 0 else XTH[:, kc, 1:2]
                    mm(PS[2][:, 1:2], wa[:, kc, :], hi, kc == 0, kc == 7, (wa.buf, B_xth), (B_ps[2],))
                A = ASB[fc % 2]; ba = Bs[f"asb{fc % 2}"]
                act(A[:, 1:513], PS[0], AF.Copy, (B_ps[0],), (ba,))
                act(A[:, 513:1025], PS[1], AF.Copy, (B_ps[1],), (ba,))
                cp("dve", A[:, 0:1], PS[2][:, 0:1], (B_ps[2],), (ba,))
                cp("dve", A[:, 1025:1026], PS[2][:, 1:2], (B_ps[2],), (ba,))
                ts("dve", CC, A[:, 0:1024], CONV[:, fc, 0:1], CONV[:, fc, 3:4], ALU.mult, ALU.add, (ba, B_conv), (Bs["cc0"],))
                stt("dve", CC, A[:, 1:1025], CONV[:, fc, 1:2], CC, ALU.mult, ALU.add, (ba, B_conv, Bs["cc0"]), (Bs["cc0"],))
                stt("dve", CC, A[:, 2:1026], CONV[:, fc, 2:3], CC, ALU.mult, ALU.add, (ba, B_conv, Bs["cc0"]), (Bs["cc0"],))
                G_ = GG[fc % 2]; bg = Bs[f"gg{fc % 2}"]
                act(G_, CC, AF.Gelu, (Bs["cc0"],), (bg,))
                proj8(3, wv, tok0, 512)
                proj8(4, wv, tok0 + 512, 512)
                tt("dve", HT[:, fc, 0:512], G_[:, 0:512], PS[3], ALU.mult, (bg, B_ps[3]), (B_ht,))
                tt("dve", HT[:, fc, 512:1024], G_[:, 512:1024], PS[4], ALU.mult, (bg, B_ps[4]), (B_ht,))
            for t8 in range(8):
                tile = half * 8 + t8
                rows = slice(tile * 128, (tile + 1) * 128)
                pw = 2 + (t8 % 2)
                bpw = (B_ps[2 * pw], B_ps[2 * pw + 1])
                for hf in range(2):
                    for fc in range(NFC):
                        mm(PSW[pw][:, hf * 512:(hf + 1) * 512], HT[:, fc, t8 * 128:(t8 + 1) * 128], WD[:, fc, hf * 512:(hf + 1) * 512],
                           fc == 0, fc == NFC - 1, (B_ht, B_wd), bpw)
                k = tile % 2
                XF = XF2[k]; bxf = Bs[f"xf{k}"]
                P.dma(sp, XF, X1[g][rows, :], (D_x1[g],), (bxf,))
                stt("dve", XF, XF, float(ALPHA), PSW[pw][:, :], ALU.mult, ALU.add, (bxf,) + bpw, (bxf,))
                layer_norm_tile(XF, Gt, Bt_, XF, (bxf,), (bxf,))
                if last:
                    P.dma(sp, y_out[g][rows, :], XF, (bxf,), ())
                else:
                    P.dma(sp, X2[g][rows, :], XF, (bxf,), (D_x2[g],))
                    finish_tile(tile, XF, (bxf,), tbank=3)

    def block(l, g, last):
        prompt = g == "p"
        if l == 0:
            scr_phase(["xf0", "xf1", "lnj"])
            for tile in range(16):
                k = tile % 2
                P.dma(sp, XF2[k], x_in[g][tile * 128:(tile + 1) * 128, :], (), (Bs[f"xf{k}"],))
                finish_tile(tile, XF2[k], (Bs[f"xf{k}"],))
        scr_phase(["rope0", "rope1", "rt", "kvs", "smallc", "rstate", "yret", "gt", "sm", "kst", "sg"])
        P.alias(ret_bufs, [B_ktg, B_ht, B_wd, B_wout])
        P.alias(B_qt + B_mix, [B_ht, B_wd])
        logsig_params(l)
        P.dma(sp, SCR[:, 7168:8192], small_d[:, :, :].rearrange("p c d -> p (c d)"), (), (Bs["smallc"],))
        for p in range(4):
            ret_pass1(l, g, p)
        if prompt:
            P.collective("AllGather", G4, RSd.ap().opt(), RSg4.ap().opt(), (D_rsd,), (D_rsg4,))
            P.collective("AllGather", G2, RSg4.ap().opt(), RSg8.ap().opt(), (D_rsg4,), (D_rsg,))
        proj_dk_dv(l, g)
        if prompt:
            gather_stage(0)
        proj_dq_gate(l, g)
        ret_pass2(l, g, 0)
        ret_pass2(l, g, 1)
        if prompt:
            gather_stage(1)
        ret_pass2(l, g, 2)
        ret_pass2(l, g, 3)
        attention(l, g)
        wout_ln1(l, g)
        ffn_ln2(l, g, last)

    for g in groups:
        for l in range(nlayers):
            block(l, g, l == nlayers - 1)
    if dbg_out is not None:
        for g in groups:
            dbg("x1_" + g, X1[g][:, :], [T, D], (D_x1[g],))
    P.emit()
    return nc, P, dbg_list


_CACHE = {}


def _in_maps(inp):
    wf, wt = _host_weights(inp)
    maps = []
    shared = {
        "wf": wf, "wt": wt, "w_out": np.asarray(inp["w_out"], np.float32), "w_up": np.asarray(inp["w_up"], np.float32),
        "w_down": np.asarray(inp["w_down"], np.float32),
        "rdl": np.asarray(inp["ret_decay_logit"], np.float32).reshape(DEPTH, 16),
        "rel_bias": np.asarray(inp["rel_bias"], np.float32),
        "lq1": np.asarray(inp["lambda_q1"], np.float32), "lk1": np.asarray(inp["lambda_k1"], np.float32),
        "lq2": np.asarray(inp["lambda_q2"], np.float32), "lk2": np.asarray(inp["lambda_k2"], np.float32),
        "dng": np.asarray(inp["diff_norm_g"], np.float32), "ln_g": np.asarray(inp["ln_g"], np.float32),
        "ln_b": np.asarray(inp["ln_b"], np.float32), "conv_w": np.asarray(inp["conv_w"], np.float32),
        "conv_b": np.asarray(inp["conv_b"], np.float32),
    }
    xp = np.asarray(inp["x_prompt"], np.float32)[0]
    xs = np.asarray(inp["x_sample"], np.float32)
    for c in range(8):
        m = dict(shared)
        m["xp"] = np.ascontiguousarray(xp[c * T:(c + 1) * T])
        m["xs"] = np.ascontiguousarray(xs[c])
        m.update(_host_consts(c))
        maps.append(m)
    return maps


def kernel(**inp):
    if "nc" not in _CACHE:
        _CACHE["nc"] = build()[0]
    nc = _CACHE["nc"]
    maps = _in_maps(inp)
    res = run_bass_kernel_spmd(nc, maps, core_ids=list(range(8)))
    yp = np.concatenate([res.results[c]["yp"] for c in range(8)], axis=0)[None].astype(np.float32)
    ys = np.stack([res.results[c]["ys"] for c in range(8)], axis=0).astype(np.float32)
    return (yp, ys)
```

```python
import math
import numpy as np
import ml_dtypes
import concourse.bass as bass
import concourse.mybir as mybir
from concourse.bass_utils import run_bass_kernel_spmd

F32 = mybir.dt.float32
BF16 = mybir.dt.bfloat16
AF = mybir.ActivationFunctionType
ALU = mybir.AluOpType
AX = mybir.AxisListType
NPBF = ml_dtypes.bfloat16

D = 1024
T = 2048
NTT = 16
DEPTH = 4
DFF = 2816
NFC = 22
LN_EPS = 1e-5
HN_EPS = 1e-6
ALPHA = (2 * DEPTH) ** 0.25
KILL = -240.0
VW = 80
NF_IN = 36
U0 = 639
RVLEN = 1280


class Buf:
    __slots__ = ("name", "w", "r")

    def __init__(self, name):
        self.name = name
        self.w = None
        self.r = {}


class Prog:
    ENG = ("pe", "act", "dve", "pool", "sp")

    def __init__(self, nc, n_dma_sems=40):
        self.nc = nc
        self.esem = {e: nc.alloc_semaphore("s_" + e) for e in ("pe", "act", "dve", "pool")}
        self.ecnt = {e: 0 for e in self.esem}
        self.dsem = [nc.alloc_semaphore(f"sd{i}") for i in range(n_dma_sems)]
        self.dcnt = [0] * n_dma_sems
        self.drr = 0
        self.csem = [nc.alloc_semaphore(f"sc{i}") for i in range(16)]
        self.ccnt = [0] * 16
        self.crr = 0
        self.q = {e: [] for e in self.ENG}
        self.seen = {e: {} for e in self.ENG}
        self.n_inst = 0

    def _sem(self, key):
        kind, i = key
        if kind == "e":
            return self.esem[i]
        if kind == "d":
            return self.dsem[i]
        return self.csem[i]

    def _deps(self, eng, reads, writes, extra=()):
        deps = {}
        def add(tok):
            if tok is None:
                return
            k, v = tok
            if eng == "pe" and k == ("e", "pe"):
                return
            if deps.get(k, 0) < v:
                deps[k] = v
        for b in reads:
            add(b.w)
        for b in writes:
            add(b.w)
            for t in b.r.items():
                add(t)
        for t in extra:
            add(t)
        out = []
        seen = self.seen[eng]
        for k, v in deps.items():
            if seen.get(k, 0) >= v:
                continue
            seen[k] = v
            out.append((k, v))
        return out

    def _mark(self, tok, reads, writes):
        for b in writes:
            b.w = tok
            b.r = {}
        for b in reads:
            if b not in writes:
                if b.r.get(tok[0], 0) < tok[1]:
                    b.r[tok[0]] = tok[1]

    def alias(self, new, old):
        acc = {}
        for b in old:
            toks = list(b.r.items())
            if b.w is not None:
                toks.append(b.w)
            for k, v in toks:
                if acc.get(k, 0) < v:
                    acc[k] = v
        for b in new:
            for k, v in acc.items():
                if b.r.get(k, 0) < v:
                    b.r[k] = v

    def op(self, eng, fn, reads=(), writes=()):
        waits = self._deps(eng, reads, writes)
        self.ecnt[eng] += 1
        tok = (("e", eng), self.ecnt[eng])
        self.q[eng].append((waits, fn, ("e", eng), 1))
        self._mark(tok, reads, writes)
        self.n_inst += 1 + len(waits)
        return tok

    def dma(self, eng, out, in_, reads=(), writes=(), **kw):
        i = self.drr
        self.drr = (self.drr + 1) % len(self.dsem)
        extra = ()
        if self.dcnt[i] > 0:
            extra = ((("d", i), 16 * self.dcnt[i]),)
        waits = self._deps(eng, reads, writes, extra)
        self.dcnt[i] += 1
        tok = (("d", i), 16 * self.dcnt[i])
        self.q[eng].append((waits, (lambda e, o=out, s=in_, k=kw: e.dma_start(out=o, in_=s, **k)), ("d", i), 16))
        self._mark(tok, reads, writes)
        self.n_inst += 1 + len(waits)
        return tok

    def collective(self, kind, groups, in_ap, out_ap, reads=(), writes=()):
        i = self.crr
        self.crr = (self.crr + 1) % len(self.csem)
        extra = ()
        if self.ccnt[i] > 0:
            extra = ((("c", i), self.ccnt[i]),)
        waits = self._deps("pool", reads, writes, extra)
        self.ccnt[i] += 1
        tok = (("c", i), self.ccnt[i])
        self.q["pool"].append((waits, (lambda e, a=in_ap, b=out_ap, g=groups, k=kind: e.collective_compute(
            k, ALU.bypass, replica_groups=g, ins=[a], outs=[b])), ("c", i), 1))
        self._mark(tok, reads, writes)
        return tok

    def emit(self):
        nc = self.nc
        fin = []
        for e in self.esem:
            if self.ecnt[e]:
                fin.append((("e", e), self.ecnt[e]))
        for i, c in enumerate(self.dcnt):
            if c:
                fin.append((("d", i), 16 * c))
        for i, c in enumerate(self.ccnt):
            if c:
                fin.append((("c", i), c))
        engs = {"pe": "tensor", "act": "scalar", "dve": "vector", "pool": "gpsimd", "sp": "sync"}
        with nc.Block() as block:
            for e in self.ENG:
                items = self.q[e]
                def body(eng, items=items, e=e):
                    for waits, fn, key, inc in items:
                        for k, v in waits:
                            eng.wait_ge(self._sem(k), v)
                        fn(eng).then_inc(self._sem(key), inc)
                    if e == "sp":
                        for k, v in fin:
                            eng.wait_ge(self._sem(k), v)
                getattr(block, engs[e])(body)


def _t5_bucket_np(rel):
    rel = np.asarray(rel, np.int64)
    nb = 16
    ret = np.where(rel > 0, nb, 0)
    n = np.abs(rel)
    me = 8
    nf = np.maximum(n, 1).astype(np.float32)
    large = me + (np.log(nf / np.float32(me)).astype(np.float32) / np.float32(math.log(128 / me))
                  * np.float32(nb - me)).astype(np.int32)
    large = np.minimum(large, nb - 1)
    return (ret + np.where(n < me, n, large)).astype(np.int64)


def _rv_ranges():
    u = np.arange(RVLEN)
    b = _t5_bucket_np(U0 - u)
    runs = []
    s = 0
    for i in range(1, RVLEN + 1):
        if i == RVLEN or b[i] != b[s]:
            runs.append((int(b[s]), s, i))
            s = i
    return runs


def _rope_tables(pos0):
    i = np.arange(0, 64, 2, dtype=np.float32) / np.float32(64)
    inv = (np.float32(1.0) / (np.float32(10000.0) ** i)).astype(np.float32)
    pos = (pos0 + np.arange(T)).astype(np.float32)
    ang = (pos[:, None] * inv[None, :]).astype(np.float32)
    cos = np.cos(ang.astype(np.float64)).astype(np.float32)
    sin = np.sin(ang.astype(np.float64)).astype(np.float32)
    cosT = np.zeros((128, T), np.float32)
    sinS = np.zeros((128, T), np.float32)
    for p in range(128):
        d = p % 64
        cosT[p] = cos[:, d % 32]
        sinS[p] = sin[:, d % 32] * (-1.0 if d < 32 else 1.0)
    return cosT, sinS


def _far_masks(core, nblk, prompt):
    a = np.zeros((32, nblk), np.float32)
    for t in range(4):
        Tg = (core * 4 + t) if prompt else t
        for j in range(nblk):
            if j < 4 * Tg - 1:
                a[t, j] = 1; a[4 + t, j] = 1
            elif j >= 4 * Tg + 5:
                a[8 + t, j] = 1; a[12 + t, j] = 1
            else:
                a[16 + t, j] = 1
    return np.repeat(a, 128, axis=1).astype(NPBF)


def _host_consts(core):
    c = {}
    cp, sp_ = _rope_tables(core * T)
    cs, ss = _rope_tables(0)
    c["rope"] = np.stack([cp, sp_, cs, ss], 0)
    c["augk_p"] = _far_masks(core, 128, True)
    c["augk_s"] = _far_masks(core, 16, False)
    augc = np.zeros((32, 16 * 128), np.float32)
    for i in range(8):
        if i != core - 1:
            augc[20, i * 128:(i + 1) * 128] = 1
        if i != core + 1:
            augc[20, (8 + i) * 128:(9 + i) * 128] = 1
    c["augc"] = augc.astype(NPBF)
    pq = np.zeros((128, T), np.float32)
    mh = np.zeros((128, 1), np.float32); ml = np.zeros((128, 1), np.float32); kv = np.zeros((128, 1), np.float32)
    for base in (32, 96):
        for r in range(20):
            t = r % 4
            pq[base + r, t * 512:(t + 1) * 512] = 1
        pq[base + 20, :] = 1
        for r in (0, 1, 2, 3, 8, 9, 10, 11):
            mh[base + r] = 1
        for r in (4, 5, 6, 7, 12, 13, 14, 15):
            ml[base + r] = 1
        for r in (16, 17, 18, 19, 20):
            kv[base + r] = KILL
    c["pq"] = pq
    i = np.arange(128, dtype=np.float32)
    dm = i[None, :] - i[:, None]
    bd = np.zeros((128, 128), np.float32); bd[:64, :64] = 1; bd[64:, 64:] = 1
    small = np.zeros((128, 8, 128), np.float32)
    small[:, 0] = np.maximum(dm, 0); small[:, 1] = np.maximum(-dm, 0)
    small[:, 2] = (dm >= 0); small[:, 3] = (dm < 0)
    small[:, 4] = bd
    small[:, 5] = (i + 1.0)[None, :]
    small[:, 6] = (128.0 - i)[None, :]
    c["small"] = small
    misc = np.zeros((128, 64), np.float32)
    misc[:, 0] = 127.0 - i
    misc[:, 1] = i
    misc[:, 2:18] = 128.0 * np.arange(16)[None, :]
    misc[:, 18:34] = 128.0 * (15 - np.arange(16))[None, :]
    for r in range(8):
        misc[:, 34 + r] = 2048.0 * (core - 1 - r) if r < core else 0.0
        misc[:, 42 + r] = 2048.0 * (r - core - 1) if r > core else 0.0
        misc[:, 50 + r] = 1.0 if r < core else 0.0
    c["misc"] = misc
    misc2 = np.zeros((128, 16), np.float32)
    for r in range(8):
        misc2[:, r] = 1.0 if r > core else 0.0
    misc2[:, 8:9] = mh; misc2[:, 9:10] = ml; misc2[:, 10:11] = kv
    c["misc2"] = misc2
    idn = np.zeros((128, 3, 128), np.float32)
    idn[:, 0] = np.eye(128); idn[:, 1] = np.eye(128)[::-1]; idn[:, 2] = bd
    c["idn"] = idn.astype(NPBF)
    sel = np.zeros((16, 2), np.float32)
    if core > 0:
        sel[2 * (core - 1) + 1, 0] = 1
    if core < 7:
        sel[2 * (core + 1), 1] = 1
    c["sel"] = sel.astype(NPBF)
    return c


def _host_weights(inp):
    w_in = np.asarray(inp["w_in"], np.float32)
    L = w_in.shape[0]
    rq, rk, rv, rg = (w_in[:, :, i * 512:(i + 1) * 512] for i in range(4))
    dq = w_in[:, :, 2048:2560]; dk = w_in[:, :, 2560:3072]; dv = w_in[:, :, 3072:3584]

    def swap(w):
        w = w.reshape(L, D, 8, 2, 32)
        return w[:, :, :, ::-1, :].reshape(L, D, 512)

    def pad_heads(w):
        w = w.reshape(L, D, 8, 2, 32)
        o = np.zeros((L, D, 8, 4, 32), np.float32)
        o[:, :, :, 0] = w[:, :, :, 0]; o[:, :, :, 2] = w[:, :, :, 1]
        return o.reshape(L, D, 1024)
    wf = np.concatenate([rq, swap(rq), rk, swap(rk), rg, pad_heads(dq), pad_heads(dk)], axis=2)
    wt = np.concatenate([rv, dv], axis=2)
    return np.ascontiguousarray(wf), np.ascontiguousarray(wt)


def build(nlayers=DEPTH, groups=("p", "s"), dbg_out=None):
    nc = bass.Bass("TRN2", target_bir_lowering=False)
    P = Prog(nc)
    Bf = Buf

    def din(name, shape, dt=F32):
        return nc.dram_tensor(name, list(shape), dt, kind="ExternalInput")

    def dscr(name, shape, dt=F32):
        return nc.dram_tensor(name, list(shape), dt)

    x_in = {"p": din("xp", [T, D]), "s": din("xs", [T, D])}
    wf_d = din("wf", [DEPTH, D, NF_IN * 128]); wt_d = din("wt", [DEPTH, D, 1024])
    wo_d = din("w_out", [DEPTH, D, D]); wu_d = din("w_up", [DEPTH, D, 2 * DFF]); wd_d = din("w_down", [DEPTH, DFF, D])
    rdl_d = din("rdl", [DEPTH, 16]); rb_d = din("rel_bias", [32, 8])
    lam_d = [din(n, [DEPTH, 32]) for n in ("lq1", "lk1", "lq2", "lk2")]
    dng_d = din("dng", [DEPTH, 64]); lng_d = din("ln_g", [DEPTH, 2, D]); lnb_d = din("ln_b", [DEPTH, 2, D])
    cw_d = din("conv_w", [DEPTH, 3, DFF]); cb_d = din("conv_b", [DEPTH, DFF])
    rope_d = din("rope", [4, 128, T]); augkp_d = din("augk_p", [32, 16384], BF16); augks_d = din("augk_s", [32, T], BF16)
    augc_d = din("augc", [32, T], BF16); pq_d = din("pq", [128, T]); small_d = din("small", [128, 8, 128])
    misc_d = din("misc", [128, 64]); misc2_d = din("misc2", [128, 16]); idn_d = din("idn", [128, 3, 128], BF16)
    sel_d = din("sel", [16, 2], BF16)
    y_out = {"p": nc.dram_tensor("yp", [T, D], F32, kind="ExternalOutput"),
             "s": nc.dram_tensor("ys", [T, D], F32, kind="ExternalOutput")}
    X1 = {g: dscr("x1_" + g, [T, D]) for g in "ps"}
    X2 = {g: dscr("x2_" + g, [T, D]) for g in "ps"}
    gate_d = dscr("gate", [4, 128, T])
    KTd = {g: [dscr(f"ktd_{g}{pc}", [128, T], BF16) for pc in range(4)] for g in "ps"}
    Vd = {g: [dscr(f"vd_{g}{h}", [128, 16 * VW], BF16) for h in range(8)] for g in "ps"}
    KTg4 = [dscr(f"ktg4_{pc}", [4 * 128, T], BF16) for pc in range(4)]
    KTg8 = [dscr(f"ktg8_{pc}", [8 * 128, T], BF16) for pc in range(4)]
    Vg4 = [dscr(f"vg4_{h}", [4 * 128, 16 * VW], BF16) for h in range(8)]
    Vg8 = [dscr(f"vg8_{h}", [8 * 128, 16 * VW], BF16) for h in range(8)]
    RSd = dscr("rsd", [1024, 128]); RSg4 = dscr("rsg4", [4096, 128]); RSg8 = dscr("rsg8", [8192, 128])
    HLd = dscr("hld", [2, D]); HLg4 = dscr("hlg4", [8, D]); HLg8 = dscr("hlg8", [16, D])
    RVd = dscr("rvd", [8, RVLEN])
    RKT = dscr("r_kt", [4, 128, T], BF16); RVR = dscr("r_vr", [4, 128, T], BF16); RKV = dscr("r_kv", [4, 128, 2 * 16 * 128])
    D_ktg, D_vg, D_rsg, D_hlg = Bf("d_ktg"), Bf("d_vg"), Bf("d_rsg"), Bf("d_hlg")
    D_rsg4 = Bf("d_rsg4")
    D_ktd = {g: Bf("d_ktd" + g) for g in "ps"}; D_vd = {g: Bf("d_vd" + g) for g in "ps"}
    D_x1 = {g: Bf("d_x1" + g) for g in "ps"}; D_x2 = {g: Bf("d_x2" + g) for g in "ps"}
    D_gate, D_rsd, D_hld, D_rvd = Bf("d_gate"), Bf("d_rsd"), Bf("d_hld"), Bf("d_rvd")
    D_ret = [Bf(f"d_ret{p}") for p in range(4)]
    G4 = [[0, 1, 2, 3], [4, 5, 6, 7]]; G2 = [[0, 4], [1, 5], [2, 6], [3, 7]]

    A0 = nc.alloc_sbuf_tensor("sb_a0", [128, 49152], BF16)
    QT = A0[:, 0:16384].rearrange("p (c t) -> p c t", c=8)
    MIXT = A0[:, 16384:32768].rearrange("p (c t) -> p c t", c=8)
    KTG = A0[:, 32768:49152]
    HT = A0[:, 0:22528].rearrange("p (c t) -> p c t", c=NFC)
    WD = A0[:, 22528:45056].rearrange("p (c f) -> p c f", c=NFC)
    B_qt = [Bf(f"qt{h}") for h in range(8)]; B_mix = [Bf(f"mix{c}") for c in range(8)]; B_ktg = Bf("ktg")
    B_ht = Bf("ht"); B_wd = Bf("wd")
    SCR = nc.alloc_sbuf_tensor("sb_scr", [128, 11264], F32)
    SCRB = SCR[:, :].bitcast(BF16)
    XT = nc.alloc_sbuf_tensor("sb_xt", [128, 8, T], BF16); B_xt = Bf("xt")
    XTH = nc.alloc_sbuf_tensor("sb_xth", [128, 8, 2], BF16); B_xth = Bf("xth")
    VGt = nc.alloc_sbuf_tensor("sb_vg", [128, 128 * VW], BF16); B_vg = Bf("vg")
    NWS = 6
    WS = nc.alloc_sbuf_tensor("sb_ws", [128, NWS, 1024], BF16); B_wsl = [Bf(f"ws{i}") for i in range(NWS)]
    ws_rr = [0]
    IDN = nc.alloc_sbuf_tensor("sb_idn", [128, 3, 128], BF16); B_const = Bf("const")
    MISC = nc.alloc_sbuf_tensor("sb_misc", [128, 64], F32); MISC2 = nc.alloc_sbuf_tensor("sb_misc2", [128, 16], F32)
    PRM = nc.alloc_sbuf_tensor("sb_prm", [128, 256], F32); B_prm = Bf("prm")
    CONV = nc.alloc_sbuf_tensor("sb_conv", [128, NFC, 4], F32); B_conv = Bf("conv")
    SELt = nc.alloc_sbuf_tensor("sb_sel", [16, 2], BF16)
    LGREP = PRM[:, 0:16]; LGP = PRM[:, 16:24]; W16 = PRM[:, 24:40]; DEC = PRM[:, 40:48]; W8 = PRM[:, 48:64]
    LAM = PRM[:, 64:68]; NLAM = PRM[:, 68:72]; DNG = PRM[:, 72:76]; VALS = PRM[:, 76:84]; TMPP = PRM[:, 84:148]
    LAMT = PRM[:, 148:164]
    PSW = [nc.psum_tensor(f"psw{i}", [128, 1024], F32).__enter__() for i in range(4)]
    PS = []
    for i in range(4):
        PS += [PSW[i][:, 0:512], PSW[i][:, 512:1024]]
    B_ps = [Bf(f"ps{i}") for i in range(8)]
    ident = IDN[:, 0, :]; antiid = IDN[:, 1, :]; bdones = IDN[:, 2, :]

    sp, pool = "sp", "pool"

    def mm(out, lhsT, rhs, start, stop, reads, writes, **kw):
        return P.op("pe", lambda e: e.matmul(out, lhsT=lhsT, rhs=rhs, start=start, stop=stop, **kw), reads, writes)

    def tr(out, in_, reads, writes):
        return P.op("pe", lambda e: e.transpose(out, in_, ident), reads, writes)

    def act(out, in_, func, reads, writes, **kw):
        return P.op("act", lambda e: e.activation(out=out, in_=in_, func=func, **kw), reads, writes)

    def ts(eng, out, in0, s1, s2, op0, op1, reads, writes):
        if s2 is None:
            return P.op(eng, lambda e: e.tensor_scalar(out=out, in0=in0, scalar1=s1, scalar2=None, op0=op0), reads, writes)
        return P.op(eng, lambda e: e.tensor_scalar(out=out, in0=in0, scalar1=s1, scalar2=s2, op0=op0, op1=op1), reads, writes)

    def tt(eng, out, in0, in1, op, reads, writes):
        return P.op(eng, lambda e: e.tensor_tensor(out=out, in0=in0, in1=in1, op=op), reads, writes)

    def stt(eng, out, in0, scalar, in1, op0, op1, reads, writes):
        return P.op(eng, lambda e: e.scalar_tensor_tensor(out=out, in0=in0, scalar=scalar, in1=in1, op0=op0, op1=op1), reads, writes)

    def cp(eng, out, in_, reads, writes):
        return P.op(eng, lambda e: e.tensor_copy(out=out, in_=in_), reads, writes)

    def ms(eng, ap, val, writes):
        return P.op(eng, lambda e: e.memset(ap, val), (), writes)

    def bcast_rows(dt_, off, n, cols):
        return bass.AP(dt_, off, [[0, n], [1, cols]])

    def sb3(ap2, mid, inner):
        return bass.AP(ap2.tensor, ap2.offset, [list(ap2.ap[0]), [0, mid], list(ap2.ap[-1])])

    B_scr = Bf("scr_init")
    P.dma(sp, IDN[:, :, :], idn_d[:, :, :], (), (B_const,))
    P.dma(sp, MISC[:, :], misc_d[:, :], (), (B_const,))
    P.dma(sp, MISC2[:, :], misc2_d[:, :], (), (B_const,))
    P.dma(sp, SELt[:, :], sel_d[:, :], (), (B_const,))
    RB = SCR[0:8, 0:32]; RVS = SCR[0:8, 64:64 + RVLEN]; ZR = SCR[0:8, 1408:1408 + RVLEN]
    P.dma(sp, RB, bass.AP(rb_d, 0, [[1, 8], [8, 32]]), (), (B_scr,), allow_slow_non_contiguous=True)
    ms("dve", ZR, 0.0, (B_scr,))
    for (b, lo, hi) in _rv_ranges():
        ts("dve", RVS[:, lo:hi], ZR[:, lo:hi], RB[:, b:b + 1], None, ALU.add, None, (B_scr,), (B_scr,))
    P.dma(sp, RVd[:, :], RVS, (B_scr,), (D_rvd,))
    LQ = [SCR[:, 3072 + i * 128: 3072 + (i + 1) * 128] for i in range(4)]
    for i in range(4):
        P.dma(sp, LQ[i], bcast_rows(lam_d[i], 0, 128, 128), (), (B_scr,))
    tt("dve", LQ[0], LQ[0], LQ[1], ALU.mult, (B_scr,), (B_scr,))
    tt("dve", LQ[2], LQ[2], LQ[3], ALU.mult, (B_scr,), (B_scr,))
    P.op("dve", lambda e: e.reduce_sum(out=LAMT[:, 0:4], in_=LQ[0].rearrange("p (l k) -> p l k", l=4), axis=AX.X), (B_scr,), (B_prm,))
    P.op("dve", lambda e: e.reduce_sum(out=LAMT[:, 4:8], in_=LQ[2].rearrange("p (l k) -> p l k", l=4), axis=AX.X), (B_scr,), (B_prm,))
    act(LAMT[:, 8:16], LAMT[:, 0:8], AF.Exp, (B_prm,), (B_prm,))
    tt("dve", LAM, LAMT[:, 8:12], LAMT[:, 12:16], ALU.subtract, (B_prm,), (B_prm,))
    lam_init = [0.8 - 0.6 * math.exp(-0.3 * l) for l in range(DEPTH)]
    P.dma(sp, DNG[0:64, :], bass.AP(dng_d, 0, [[1, 64], [64, 4]]), (), (B_prm,), allow_slow_non_contiguous=True)
    for l in range(DEPTH):
        ts("dve", LAM[:, l:l + 1], LAM[:, l:l + 1], float(lam_init[l]), None, ALU.add, None, (B_prm,), (B_prm,))
        ts("dve", DNG[0:64, l:l + 1], DNG[0:64, l:l + 1], float(1.0 - lam_init[l]), None, ALU.mult, None, (B_prm,), (B_prm,))
    ts("dve", NLAM, LAM, -1.0, None, ALU.mult, None, (B_prm,), (B_prm,))
    V32 = TMPP[:, 0:8]; VH = SCRB[:, 8192:8200]; VHF = TMPP[:, 8:16]; LO = TMPP[:, 16:24]; T1 = TMPP[:, 24:32]
    ms("dve", V32, 0.0, (B_prm,))
    for base in (32, 96):
        P.dma(sp, V32[base:base + 8, :], bcast_rows(rb_d, 15 * 8, 8, 8), (B_prm,), (B_prm,))
        P.dma(sp, V32[base + 8:base + 16, :], bcast_rows(rb_d, 31 * 8, 8, 8), (B_prm,), (B_prm,))
    cp("dve", VH, V32, (B_prm,), (B_scr,))
    cp("dve", VHF, VH, (B_scr,), (B_prm,))
    tt("dve", LO, V32, VHF, ALU.subtract, (B_prm,), (B_prm,))
    ts("dve", T1, VHF, MISC2[:, 8:9], None, ALU.mult, None, (B_prm, B_const), (B_prm,))
    stt("dve", VALS, LO, MISC2[:, 9:10], T1, ALU.mult, ALU.add, (B_prm, B_const), (B_prm,))
    ts("dve", VALS, VALS, MISC2[:, 10:11], None, ALU.add, None, (B_prm, B_const), (B_prm,))
    PQ = SCR[:, 4096:4096 + T]
    B_pq = Bf("pq")
    P.dma(sp, PQ, pq_d[:, :], (B_scr,), (B_pq,))
    for h in range(8):
        ts("dve", QT[:, h, :], PQ, VALS[:, h:h + 1], None, ALU.mult, None, (B_pq, B_prm), (B_qt[h],))

    dbg_list = []

    def dbg(name, ap, shape, reads, dt=F32):
        if dbg_out is None or name not in dbg_out:
            return
        o = nc.dram_tensor("dbg_" + name, list(shape), dt, kind="ExternalOutput")
        P.dma(sp, o.ap() if len(shape) != 2 else o[:, :], ap, reads, ())
        dbg_list.append("dbg_" + name)

    dbg("vals", VALS, [128, 8], (B_prm,))
    dbg("lam", LAM, [128, 4], (B_prm,))
    dbg("rvd", RVd[:, :], [8, RVLEN], (D_rvd,))
    dbg("qt0", QT[:, 0, :], [128, T], (B_qt[0],), BF16)

    S_Q = 32 ** -0.5

    class WT:
        def __init__(self, ap, buf):
            self.ap = ap; self.buf = buf
        def __getitem__(self, k):
            return self.ap[k]

    def load_w_chunk(slot, sub, src_ap):
        i = ws_rr[0]
        ws_rr[0] = (i + 1) % NWS
        dst = WS[:, i, :].rearrange("p (k f) -> p k f", k=8)
        P.dma(pool, dst, src_ap, (), (B_wsl[i],))
        return WT(dst, B_wsl[i])

    def wsrc(dt_, l, c0, n):
        return dt_[l, :, c0:c0 + n].rearrange("(k p) f -> p k f", p=128)

    def finish_token_tile(g, tile, XF, reads_xf):
        XB = SCRB[:, 20480:21504]
        B_xb = Bs["xb"]
        P.op("act", lambda e: e.activation(out=XB, in_=XF, func=AF.Copy), reads_xf, (B_xb,))
        pst = PS[7].bitcast(BF16)
        for kc in range(8):
            tr(pst[:, kc * 128:(kc + 1) * 128], XB[:, kc * 128:(kc + 1) * 128], (B_xb, B_const), (B_ps[7],))
        cp("dve", XT[:, :, tile * 128:(tile + 1) * 128], pst.rearrange("p (k t) -> p k t", k=8), (B_ps[7],), (B_xt,))

    Bs = {n: Bf(n) for n in ("xb", "xf0", "xf1", "kst", "sg", "rope0", "rope1", "rt", "qrt", "krt", "kr", "vr", "vrf",
                             "kvs", "rstate", "rp", "qft", "sm", "yret", "gt", "smallc", "rsg", "ktl", "vl", "ktc", "vc",
                             "bt", "p0", "p1", "p2", "ep", "stage", "lnx", "lny", "lnj", "lng", "lnb", "asb0", "asb1", "cc0", "cc1",
                             "gg0", "gg1", "hlsb")}

    ONESF = nc.alloc_sbuf_tensor("sb_onesf", [128, 64], F32)
    ms("dve", ONESF[:, :], 1.0, (B_const,))
    XTH2 = nc.alloc_sbuf_tensor("sb_xth2", [128, 8, 2], BF16)
    ST = PRM[:, 164:180]
    KTGR = A0[:, 32768:49152]
    W_OUT = KTGR.rearrange("p (k f) -> p k f", k=8)
    B_wout = Bf("wout")
    ret_bufs = [Bs[n] for n in ("qrt", "krt", "kr", "vr", "vrf", "rp", "qft", "sm")]
    scr_cur = [B_scr, B_pq]

    def scr_phase(names):
        new = [Bs[n] for n in names]
        P.alias(new, scr_cur)
        scr_cur.clear()
        scr_cur.extend(new)

    def layer_norm_tile(Y, Gt, Bt, OUT, rd, wr):
        JUNK = SCR[:, 10240:11264]
        bj = Bs["lnj"]
        P.op("dve", lambda e: e.reduce_sum(out=ST[:, 0:1], in_=Y, axis=AX.X), rd, (B_prm,))
        tt("dve", JUNK, Y, Y, ALU.mult, rd, (bj,))
        P.op("dve", lambda e: e.reduce_sum(out=ST[:, 1:2], in_=JUNK, axis=AX.X), (bj,), (B_prm,))
        ts("dve", ST[:, 2:3], ST[:, 0:1], 1.0 / D, None, ALU.mult, None, (B_prm,), (B_prm,))
        tt("dve", ST[:, 3:4], ST[:, 2:3], ST[:, 2:3], ALU.mult, (B_prm,), (B_prm,))
        stt("dve", ST[:, 4:5], ST[:, 1:2], 1.0 / D, ST[:, 3:4], ALU.mult, ALU.subtract, (B_prm,), (B_prm,))
        act(ST[:, 5:6], ST[:, 4:5], AF.Ln, (B_prm,), (B_prm,), bias=LN_EPS, scale=1.0)
        act(ST[:, 5:6], ST[:, 5:6], AF.Exp, (B_prm,), (B_prm,), scale=-0.5)
        stt("dve", ST[:, 6:7], ST[:, 2:3], -1.0, ST[:, 5:6], ALU.mult, ALU.mult, (B_prm,), (B_prm,))
        ts("dve", OUT, Y, ST[:, 5:6], ST[:, 6:7], ALU.mult, ALU.add, tuple(rd) + (B_prm,), wr)
        tt("dve", OUT, OUT, Gt, ALU.mult, tuple(wr) + (Bs["lng"],), wr)
        tt("dve", OUT, OUT, Bt, ALU.add, tuple(wr) + (Bs["lnb"],), wr)

    def finish_tile(tile, XF, rd, tbank=5):
        XB = SCRB[:, 20480:21504]
        bxb = Bs["lnj"]
        cp("dve", XB, XF, tuple(rd), (bxb,))
        pst = PS[tbank].bitcast(BF16)
        for kc in range(8):
            tr(pst[:, kc * 128:(kc + 1) * 128], XB[:, kc * 128:(kc + 1) * 128], (bxb, B_const), (B_ps[tbank],))
        cp("dve", XT[:, :, tile * 128:(tile + 1) * 128], pst.rearrange("p (k t) -> p k t", k=8), (B_ps[tbank],), (B_xt,))

    def proj8(bank, wtile, t0, n):
        for kc in range(8):
            mm(PS[bank][:, 0:n], wtile[:, kc, :], XT[:, kc, t0:t0 + n], kc == 0, kc == 7,
               (wtile.buf, B_xt), (B_ps[bank],))

    def logsig_params(l):
        X = TMPP[:, 0:16]; NX = TMPP[:, 16:32]; E = TMPP[:, 32:48]; Z = TMPP[:, 48:64]
        P.dma(sp, X, bcast_rows(rdl_d, l * 16, 128, 16), (B_prm,), (B_prm,))
        ts("dve", NX, X, -1.0, None, ALU.mult, None, (B_prm,), (B_prm,))
        tt("dve", E, X, NX, ALU.max, (B_prm,), (B_prm,))
        act(E, E, AF.Exp, (B_prm,), (B_prm,), scale=-1.0)
        ts("dve", Z, E, 2.0, None, ALU.add, None, (B_prm,), (B_prm,))
        P.op("dve", lambda e: e.reciprocal(out=Z, in_=Z), (B_prm,), (B_prm,))
        tt("dve", Z, Z, E, ALU.mult, (B_prm,), (B_prm,))
        tt("dve", E, Z, Z, ALU.mult, (B_prm,), (B_prm,))
        ts("dve", NX, E, 1.0 / 9, 1.0 / 7, ALU.mult, ALU.add, (B_prm,), (B_prm,))
        for cst in (1.0 / 5, 1.0 / 3, 1.0):
            tt("dve", NX, NX, E, ALU.mult, (B_prm,), (B_prm,))
            ts("dve", NX, NX, cst, None, ALU.add, None, (B_prm,), (B_prm,))
        tt("dve", NX, NX, Z, ALU.mult, (B_prm,), (B_prm,))
        ts("dve", X, X, 0.0, None, ALU.min, None, (B_prm,), (B_prm,))
        stt("dve", LGREP, NX, -2.0, X, ALU.mult, ALU.add, (B_prm,), (B_prm,))
        LGR4 = LGREP.rearrange("p (d q h) -> p d q h", d=2, q=4)
        cp("dve", LGP[0:64, :].rearrange("p (d q) -> p d q", d=2), LGR4[0:64, :, :, 0], (B_prm,), (B_prm,))
        cp("dve", LGP[64:128, :].rearrange("p (d q) -> p d q", d=2), LGR4[64:128, :, :, 1], (B_prm,), (B_prm,))
        ts("dve", TMPP[:, 0:8], LGREP[:, 0:8], MISC[:, 0:1], None, ALU.mult, None, (B_prm, B_const), (B_prm,))
        ts("dve", TMPP[:, 8:16], LGREP[:, 8:16], MISC[:, 1:2], None, ALU.mult, None, (B_prm, B_const), (B_prm,))
        act(W16, TMPP[:, 0:16], AF.Exp, (B_prm,), (B_prm,))
        act(DEC, LGP, AF.Exp, (B_prm,), (B_prm,), scale=128.0)

    def rope_tile(g, tt_, rb):
        RO = SCR[:, rb * 1024:(rb + 1) * 1024].rearrange("p (c t) -> p c t", c=2)
        ri = 0 if g == "p" else 2
        P.dma(sp, RO, rope_d[ri:ri + 2, :, tt_ * 512:(tt_ + 1) * 512].rearrange("c p t -> p c t"), (), (Bs[f"rope{rb}"],))
        return RO

    def rotary(bank_a, bank_b, RO, rb, OUT, out_buf):
        RT = SCR[:, 2048:3072]
        tt("dve", RT[:, 0:512], PS[bank_a], RO[:, 0, :], ALU.mult, (B_ps[bank_a], Bs[f"rope{rb}"]), (Bs["rt"],))
        tt("dve", RT[:, 512:1024], PS[bank_b], RO[:, 1, :], ALU.mult, (B_ps[bank_b], Bs[f"rope{rb}"]), (Bs["rt"],))
        tt("dve", OUT, RT[:, 0:512], RT[:, 512:1024], ALU.add, (Bs["rt"],), (out_buf,))

    QrT = KTGR[:, 0:2048]; KrT = KTGR[:, 2048:4096]; QfT = KTGR[:, 4096:6144]; QbT = KTGR[:, 6144:8192]
    KR = KTGR[:, 8192:10240].rearrange("p (n d) -> p n d", n=16); VR = KTGR[:, 10240:12288].rearrange("p (n d) -> p n d", n=16)
    VRF = KTGR[:, 12288:16384].rearrange("p (r n d) -> p r n d", r=2, n=16)
    RPb = KTGR[:, 12288:16384].rearrange("p (r n d) -> p r n d", r=2, n=16)
    SMt = KTGR[:, 4096 + 0:4096 + 0]
    KVS = SCR[:, 3072:7168].rearrange("p (r n d) -> p r n d", r=2, n=16)
    SMALLC = SCR[:, 7168:8192].rearrange("p (c d) -> p c d", c=8)
    MASK = SCR[:, 8192:8448].rearrange("p (h d) -> p h d", h=2)
    WFT = SCR[:, 8448:8704].rearrange("p (h d) -> p h d", h=2)
    QFB = SCR[:, 8704:8960].rearrange("p (h d) -> p h d", h=2)
    MTMP = SCR[:, 8960:9216].rearrange("p (h d) -> p h d", h=2)
    RST = SCR[:, 9216:9472].rearrange("p (h d) -> p h d", h=2)
    RSTART = SCR[:, 9472:9728].rearrange("p (h d) -> p h d", h=2)
    RSGS = KTGR[:, 8192:10240].bitcast(F32).rearrange("p (r d) -> p r d", r=8)
    SMB = SCRB[:, 21504:22528].rearrange("p (j h d) -> p j h d", j=4, h=2)
    BDm = SMALLC[:, 4, :]

    def ret_pass1(l, g, p):
        slot = p % 2
        wk = load_w_chunk(slot, 0, wsrc(wf_d, l, (8 + p) * 128, 128))
        wks = load_w_chunk(slot, 1, wsrc(wf_d, l, (12 + p) * 128, 128))
        wv = load_w_chunk(slot, 2, wsrc(wt_d, l, p * 128, 128))
        for t4 in range(4):
            rb = t4 % 2
            RO = rope_tile(g, t4, rb)
            bk = 2 * (t4 % 2)
            proj8(bk, wk, t4 * 512, 512)
            proj8(bk + 1, wks, t4 * 512, 512)
            rotary(bk, bk + 1, RO, rb, KrT[:, t4 * 512:(t4 + 1) * 512], Bs["krt"])
            for j in range(4):
                tok = (t4 * 4 + j) * 128
                for kc in range(8):
                    mm(PS[4][:, j * 128:(j + 1) * 128], XT[:, kc, tok:tok + 128], wv[:, kc, :], kc == 0, kc == 7,
                       (B_xt, wv.buf), (B_ps[4],))
            P.op("act", lambda e, o=VR[:, t4 * 4:(t4 + 1) * 4, :], i=PS[4].rearrange("p (j d) -> p j d", j=4):
                 e.activation(out=o, in_=i, func=AF.Copy), (B_ps[4],), (Bs["vr"],))
            pst = PS[5].bitcast(BF16)
            for j in range(4):
                tok = (t4 * 4 + j) * 128
                tr(pst[:, j * 128:(j + 1) * 128], KrT[:, tok:tok + 128], (Bs["krt"], B_const), (B_ps[5],))
            cp("dve", KR[:, t4 * 4:(t4 + 1) * 4, :], pst[:, 0:512].rearrange("p (j d) -> p j d", j=4), (B_ps[5],), (Bs["kr"],))
        ms("dve", WFT, 0.125, (Bs["smallc"],))
        for d_ in range(2):
            for hh in range(2):
                ts("dve", WFT[:, d_, hh * 64:(hh + 1) * 64], WFT[:, d_, hh * 64:(hh + 1) * 64],
                   W16[:, d_ * 8 + 2 * p + hh: d_ * 8 + 2 * p + hh + 1], None, ALU.mult, None, (B_prm, Bs["smallc"]), (Bs["smallc"],))
        for d_ in range(2):
            tt("dve", VRF[:, d_, :, :], VR, sb3(WFT[:, d_, :], 16, 128), ALU.mult, (Bs["vr"], Bs["smallc"]), (Bs["vrf"],))
        for n0 in range(0, 16, 2):
            for i in range(2):
                for d_ in range(2):
                    c0 = (d_ * 2 + i) * 128
                    mm(PS[6][:, c0:c0 + 128], KR[:, n0 + i, :], VRF[:, d_, n0 + i, :], True, True, (Bs["kr"], Bs["vrf"]), (B_ps[6],))
            bd4 = bass.AP(BDm.tensor, BDm.offset, [list(BDm.ap[0]), [0, 2], [0, 2], list(BDm.ap[-1])])
            tt("dve", KVS[:, :, n0:n0 + 2, :], PS[6].rearrange("p (r i d) -> p r i d", r=2, i=2), bd4, ALU.mult,
               (B_ps[6], Bs["smallc"]), (Bs["kvs"],))
        if g == "p":
            for d_ in range(2):
                ms("dve", RST[:, d_, :], 0.0, (Bs["rstate"],))
                order = range(16) if d_ == 0 else range(15, -1, -1)
                for n in order:
                    stt("dve", RST[:, d_, :], RST[:, d_, :], DEC[:, d_ * 4 + p: d_ * 4 + p + 1], KVS[:, d_, n, :],
                        ALU.mult, ALU.add, (Bs["rstate"], Bs["kvs"], B_prm), (Bs["rstate"],))
                P.dma(sp, RSd[(d_ * 4 + p) * 128:(d_ * 4 + p + 1) * 128, :], RST[:, d_, :], (Bs["rstate"],), (D_rsd,))
        P.dma(sp, RKT[p, :, :], KrT, (Bs["krt"],), (D_ret[p],))
        P.dma(sp, RVR[p, :, :], KTGR[:, 10240:12288], (Bs["vr"],), (D_ret[p],))
        P.dma(sp, RKV[p, :, :], SCR[:, 3072:7168], (Bs["kvs"],), (D_ret[p],))

    def ret_pass2(l, g, p):
        slot = p % 2
        wq = load_w_chunk(slot, 0, wsrc(wf_d, l, p * 128, 128))
        wqs = load_w_chunk(slot, 1, wsrc(wf_d, l, (4 + p) * 128, 128))
        P.alias([Bs["kvs"]], [Bs["kst"], Bs["sg"]])
        P.alias([Bs["rsg"]], [Bs["kr"]])
        P.dma(sp, KrT, RKT[p, :, :], (D_ret[p],), (Bs["krt"],))
        P.dma(sp, KTGR[:, 10240:12288], RVR[p, :, :], (D_ret[p],), (Bs["vr"],))
        P.dma(sp, SCR[:, 3072:7168], RKV[p, :, :], (D_ret[p],), (Bs["kvs"],))
        for t4 in range(4):
            rb = t4 % 2
            RO = rope_tile(g, t4, rb)
            proj8(6, wq, t4 * 512, 512)
            proj8(7, wqs, t4 * 512, 512)
            rotary(6, 7, RO, rb, QrT[:, t4 * 512:(t4 + 1) * 512], Bs["qrt"])
        for hh in range(2):
            h = 2 * p + hh
            act(MTMP[:, 0, :], SMALLC[:, 0, :], AF.Exp, (Bs["smallc"], B_prm), (Bs["smallc"],), scale=LGREP[:, h:h + 1])
            act(MTMP[:, 1, :], SMALLC[:, 1, :], AF.Exp, (Bs["smallc"], B_prm), (Bs["smallc"],), scale=LGREP[:, 8 + h:9 + h])
            stt("dve", MTMP[:, 0, :], MTMP[:, 0, :], 0.125, SMALLC[:, 2, :], ALU.mult, ALU.mult, (Bs["smallc"],), (Bs["smallc"],))
            stt("dve", MTMP[:, 1, :], MTMP[:, 1, :], 0.125, SMALLC[:, 3, :], ALU.mult, ALU.mult, (Bs["smallc"],), (Bs["smallc"],))
            tt("dve", MASK[:, hh, :], MTMP[:, 0, :], MTMP[:, 1, :], ALU.add, (Bs["smallc"],), (Bs["smallc"],))
        act(QFB[:, 0, :], SMALLC[:, 5, :], AF.Exp, (Bs["smallc"], B_prm), (Bs["smallc"],), scale=LGP[:, p:p + 1])
        act(QFB[:, 1, :], SMALLC[:, 6, :], AF.Exp, (Bs["smallc"], B_prm), (Bs["smallc"],), scale=LGP[:, 4 + p:5 + p])
        tt("dve", QfT.rearrange("p (n d) -> p n d", n=16), QrT.rearrange("p (n d) -> p n d", n=16), sb3(QFB[:, 0, :], 16, 128),
           ALU.mult, (Bs["qrt"], Bs["smallc"]), (Bs["qft"],))
        tt("dve", QbT.rearrange("p (n d) -> p n d", n=16), QrT.rearrange("p (n d) -> p n d", n=16), sb3(QFB[:, 1, :], 16, 128),
           ALU.mult, (Bs["qrt"], Bs["smallc"]), (Bs["qft"],))
        for d_ in range(2):
            if g == "p":
                src = bass.AP(RSg8, (d_ * 4 + p) * 128 * 128, [[128, 128], [1024 * 128, 8], [1, 128]])
                P.dma(sp, RSGS, src, (D_rsg,), (Bs["rsg"],))
                ecol = 34 if d_ == 0 else 42
                act(W8[:, d_ * 8:(d_ + 1) * 8], MISC[:, ecol:ecol + 8], AF.Exp, (B_const, B_prm), (B_prm,), scale=LGP[:, d_ * 4 + p: d_ * 4 + p + 1])
                msk = MISC[:, 50:58] if d_ == 0 else MISC2[:, 0:8]
                tt("dve", W8[:, d_ * 8:(d_ + 1) * 8], W8[:, d_ * 8:(d_ + 1) * 8], msk, ALU.mult, (B_prm, B_const), (B_prm,))
                ts("dve", RSTART[:, d_, :], RSGS[:, 0, :], W8[:, d_ * 8:d_ * 8 + 1], None, ALU.mult, None, (Bs["rsg"], B_prm), (Bs["rstate"],))
                for r in range(1, 8):
                    stt("dve", RSTART[:, d_, :], RSGS[:, r, :], W8[:, d_ * 8 + r:d_ * 8 + r + 1], RSTART[:, d_, :], ALU.mult, ALU.add,
                        (Bs["rsg"], B_prm, Bs["rstate"]), (Bs["rstate"],))
                cp("dve", RST[:, d_, :], RSTART[:, d_, :], (Bs["rstate"],), (Bs["rstate"],))
            else:
                ms("dve", RST[:, d_, :], 0.0, (Bs["rstate"],))
            order = range(16) if d_ == 0 else range(15, -1, -1)
            for n in order:
                cp("dve", RPb[:, d_, n, :], RST[:, d_, :], (Bs["rstate"],), (Bs["rp"],))
                stt("dve", RST[:, d_, :], RST[:, d_, :], DEC[:, d_ * 4 + p: d_ * 4 + p + 1], KVS[:, d_, n, :],
                    ALU.mult, ALU.add, (Bs["rstate"], Bs["kvs"], B_prm), (Bs["rstate"],))
        YT = SCR[:, 9728:10240]; GTt = SCR[:, 10240:10752]; CR = SCR[:, 0:512]; SQ = SCRB[:, 1024:1536]
        for t4 in range(4):
            for j in range(4):
                n = t4 * 4 + j
                c = n * 128
                mm(PS[0][:, j * 128:(j + 1) * 128], KrT[0:64, c:c + 128], QrT[0:64, c:c + 128], True, True, (Bs["krt"], Bs["qrt"]), (B_ps[0],))
                mm(PS[1][:, j * 128:(j + 1) * 128], KrT[64:128, c:c + 128], QrT[64:128, c:c + 128], True, True, (Bs["krt"], Bs["qrt"]), (B_ps[1],))
            for hh in range(2):
                tt("dve", SMB[:, :, hh, :], PS[hh].rearrange("p (j d) -> p j d", j=4), sb3(MASK[:, hh, :], 4, 128), ALU.mult,
                   (B_ps[hh], Bs["smallc"]), (Bs["sm"],))
            for j in range(4):
                n = t4 * 4 + j
                bank = 2 + j // 2
                c0 = (j % 2) * 256
                mm(PS[bank][:, c0:c0 + 256], VR[:, n, :], SMB[:, j, :, :].rearrange("p h d -> p (h d)"), True, True, (Bs["vr"], Bs["sm"]), (B_ps[bank],))
                c = n * 128
                mm(PS[4][:, j * 128:(j + 1) * 128], RPb[:, 0, n, :], QfT[:, c:c + 128], True, False, (Bs["rp"], Bs["qft"]), (B_ps[4],))
                mm(PS[4][:, j * 128:(j + 1) * 128], RPb[:, 1, n, :], QbT[:, c:c + 128], False, True, (Bs["rp"], Bs["qft"]), (B_ps[4],))
            act(CR, PS[4], AF.Copy, (B_ps[4],), (Bs["rope0"],))
            for j in range(4):
                bank = 2 + j // 2
                c0 = (j % 2) * 256
                tt("dve", YT[0:64, j * 128:(j + 1) * 128], PS[bank][0:64, c0:c0 + 128], CR[0:64, j * 128:(j + 1) * 128], ALU.add,
                   (B_ps[bank], Bs["rope0"]), (Bs["yret"],))
                tt("dve", YT[64:128, j * 128:(j + 1) * 128], PS[bank][64:128, c0 + 128:c0 + 256], CR[64:128, j * 128:(j + 1) * 128], ALU.add,
                   (B_ps[bank], Bs["rope0"]), (Bs["yret"],))
            P.dma(sp, GTt, gate_d[p, :, t4 * 512:(t4 + 1) * 512], (D_gate,), (Bs["gt"],))
            tt("dve", SQ, YT, YT, ALU.mult, (Bs["yret"],), (Bs["rope0"],))
            mm(PS[5], bdones, SQ, True, True, (B_const, Bs["rope0"]), (B_ps[5],))
            act(CR, PS[5], AF.Ln, (B_ps[5],), (Bs["rope0"],), bias=HN_EPS, scale=1.0 / 64)
            act(CR, CR, AF.Exp, (Bs["rope0"],), (Bs["rope0"],), scale=-0.5)
            tt("dve", YT, YT, CR, ALU.mult, (Bs["yret"], Bs["rope0"]), (Bs["yret"],))
            tt("dve", MIXT[:, p, t4 * 512:(t4 + 1) * 512], YT, GTt, ALU.mult, (Bs["yret"], Bs["gt"]), (B_mix[p],))

    def proj_dq_gate(l, g):
        PQt = SCR[:, 5120:7168]
        P.alias([Bs["sg"]], [Bs["kvs"], Bs["kst"]])
        P.dma(sp, PQt, pq_d[:, :], (), (Bs["kvs"],))
        for h in range(8):
            ts("dve", QT[:, h, :], PQt, VALS[:, h:h + 1], None, ALU.mult, None, (Bs["kvs"], B_prm), (B_qt[h],))
        SG = SCR[:, 4096:4608]
        for h in range(8):
            wq = load_w_chunk(0, 0, wsrc(wf_d, l, (20 + h) * 128, 128))
            for t4 in range(4):
                bq = t4 % 4
                tq = slice(t4 * 512, (t4 + 1) * 512)
                proj8(bq, wq, t4 * 512, 512)
                for r0 in (0, 64):
                    act(QT[r0:r0 + 32, h, tq], PS[bq][r0:r0 + 32, :], AF.Identity, (B_ps[bq],), (B_qt[h],), scale=float(S_Q))
        for p in range(4):
            wg = load_w_chunk(0, 0, wsrc(wf_d, l, (16 + p) * 128, 128))
            for t4 in range(4):
                b = 4 + (t4 % 2)
                proj8(b, wg, t4 * 512, 512)
                act(SG, PS[b], AF.Silu, (B_ps[b],), (Bs["sg"],))
                P.dma(sp, gate_d[p, :, t4 * 512:(t4 + 1) * 512], SG, (Bs["sg"],), (D_gate,))

    def proj_dk_dv(l, g):
        KST = SCRB[:, 6144:8192]
        P.alias([Bs["kst"]], [Bs["kvs"], Bs["sg"]])
        for h in range(8):
            wk = load_w_chunk(0, 0, wsrc(wf_d, l, (28 + h) * 128, 128))
            for t4 in range(4):
                bk_ = t4 % 4
                tq = slice(t4 * 512, (t4 + 1) * 512)
                proj8(bk_, wk, t4 * 512, 512)
                for r0 in (0, 64):
                    cp("dve", KST[r0:r0 + 32, tq], PS[bk_][r0:r0 + 32, :], (B_ps[bk_],), (Bs["kst"],))
            kr0 = (h % 2) * 64
            P.dma(sp, KTd[g][h // 2][kr0:kr0 + 32, :], KST[0:32, :], (Bs["kst"],), (D_ktd[g],))
            P.dma(sp, KTd[g][h // 2][kr0 + 32:kr0 + 64, :], KST[64:96, :], (Bs["kst"],), (D_ktd[g],))
        VST = VGt[:, 0:8 * 16 * VW]
        ms("pool", VST, 0.0, (B_vg,))
        VST4 = VST.rearrange("p (h b e) -> p h b e", h=8, b=16)
        ms("pool", VST4[:, :, :, 64:65], 1.0, (B_vg,))
        for jv in range(4):
            slot = jv % 2
            wv = load_w_chunk(slot, 0, wsrc(wt_d, l, 512 + jv * 128, 128))
            for t0 in range(0, 16, 4):
                b = 6 + ((t0 // 4) % 2)
                for j in range(4):
                    tok = (t0 + j) * 128
                    for kc in range(8):
                        mm(PS[b][:, j * 128:(j + 1) * 128], XT[:, kc, tok:tok + 128], wv[:, kc, :], kc == 0, kc == 7,
                           (B_xt, wv.buf), (B_ps[b],))
                pe0 = VST.ap[0]
                o = bass.AP(VST.tensor, VST.offset + (2 * jv) * 16 * VW + t0 * VW, [list(pe0), [VW, 4], [16 * VW, 2], [1, 64]])
                cp("dve", o, PS[b].rearrange("p (j h e) -> p j h e", j=4, h=2), (B_ps[b],), (B_vg,))
        for h in range(8):
            P.dma(sp, Vd[g][h][:, :], VST[:, h * 16 * VW:(h + 1) * 16 * VW], (B_vg,), (D_vd[g],))

    D_kt4 = [Bf(f"d_kt4_{i}") for i in range(4)]; D_v4 = [Bf(f"d_v4_{i}") for i in range(8)]

    def gather_stage(stage):
        if stage == 0:
            for pc in range(4):
                P.collective("AllGather", G4, KTd["p"][pc].ap().opt(), KTg4[pc].ap().opt(), (D_ktd["p"],), (D_kt4[pc],))
            for h in range(8):
                P.collective("AllGather", G4, Vd["p"][h].ap().opt(), Vg4[h].ap().opt(), (D_vd["p"],), (D_v4[h],))
        else:
            for pc in range(4):
                P.collective("AllGather", G2, KTg4[pc].ap().opt(), KTg8[pc].ap().opt(), (D_kt4[pc],), (D_ktg,))
            for h in range(8):
                P.collective("AllGather", G2, Vg4[h].ap().opt(), Vg8[h].ap().opt(), (D_v4[h],), (D_vg,))

    KTL = SCRB[:, 0:2048]; KTC = SCRB[:, 2048:4096]
    VL = SCRB[:, 4096:4096 + 16 * VW]; VC = SCRB[:, 5376:5376 + 16 * VW]
    BT = SCRB[:, 6656:6656 + 3072].rearrange("p (d q) -> p d q", d=6)
    PT = [SCRB[:, 9728:10752], SCRB[:, 10752:11776], SCRB[:, 18944:19968]]
    A12 = [SCR[:, 5888:6400], SCR[:, 6400:6912]]
    R12 = [SCR[:, 6912:7424], SCR[:, 7424:7936]]
    T1e = SCR[:, 7936:8448]; Oe = SCR[:, 8448:8960]
    SQe = SCRB[:, 17920:18432]; STG = SCRB[:, 18432:18944]

    def attention(l, g):
        prompt = g == "p"
        NB = 128 if prompt else 16
        NK = NB * 128
        scr_phase(["ktl", "vl", "ktc", "vc", "bt", "p0", "p1", "p2", "ep", "stage"])
        P.alias([B_ktg], ret_bufs + [Bs["rsg"], B_wout])
        aug = augkp_d if prompt else augks_d
        P.dma(sp, KTG[32:64, 0:NK], aug[:, :], (), (B_ktg,))
        P.dma(sp, KTG[96:128, 0:NK], aug[:, :], (), (B_ktg,))
        ms("pool", KTL, 0.0, (Bs["ktl"],))
        if prompt:
            ms("pool", KTC, 0.0, (Bs["ktc"],))
            P.dma(sp, KTC[32:64, :], augc_d[:, :], (Bs["ktc"],), (Bs["ktc"],))
            P.dma(sp, KTC[96:128, :], augc_d[:, :], (Bs["ktc"],), (Bs["ktc"],))
        B_kh = [Bf("ktgA"), Bf("ktgB")]; B_vh = [Bf("vgA"), Bf("vgB")]
        P.alias(B_kh, [B_ktg]); P.alias(B_vh, [B_vg])
        HK = NK // 2; HV = (NB // 2) * VW
        pend = [None, None]
        for h in range(8):
            for half, r0 in ((0, 0), (1, 64)):
                row = (h % 2) * 64 + half * 32
                pc = h // 2
                if prompt:
                    for hh_ in range(2):
                        src = bass.AP(KTg8[pc], row * T + hh_ * 4 * 128 * T, [[T, 32], [128 * T, 4], [1, T]])
                        P.dma(sp, KTG[r0:r0 + 32, hh_ * HK:(hh_ + 1) * HK].rearrange("p (r t) -> p r t", r=4), src, (D_ktg,), (B_kh[hh_],))
                    for ci, c0 in ((0, 15 * 128), (1, 0)):
                        srcc = bass.AP(KTg8[pc], row * T + c0, [[T, 32], [128 * T, 8], [1, 128]])
                        P.dma(sp, KTC[r0:r0 + 32, ci * 1024:(ci + 1) * 1024].rearrange("p (r t) -> p r t", r=8), srcc, (D_ktg,), (Bs["ktc"],))
                else:
                    for hh_ in range(2):
                        P.dma(sp, KTG[r0:r0 + 32, hh_ * HK:(hh_ + 1) * HK], KTd["s"][pc][row:row + 32, hh_ * HK:(hh_ + 1) * HK], (D_ktd["s"],), (B_kh[hh_],))
                P.dma(sp, KTL[r0:r0 + 32, :], KTd[g][pc][row:row + 32, :], (D_ktd[g],), (Bs["ktl"],))
            if prompt:
                for hh_ in range(2):
                    src = bass.AP(Vg8[h], hh_ * 4 * 128 * 16 * VW, [[16 * VW, 128], [128 * 16 * VW, 4], [1, 16 * VW]])
                    P.dma(sp, VGt[:, hh_ * HV:(hh_ + 1) * HV].rearrange("p (r f) -> p r f", r=4), src, (D_vg,), (B_vh[hh_],))
                for ci, c0 in ((0, 15 * VW), (1, 0)):
                    srcc = bass.AP(Vg8[h], c0, [[16 * VW, 128], [128 * 16 * VW, 8], [1, VW]])
                    P.dma(sp, VC[:, ci * 8 * VW:(ci + 1) * 8 * VW].rearrange("p (r f) -> p r f", r=8), srcc, (D_vg,), (Bs["vc"],))
            else:
                for hh_ in range(2):
                    P.dma(sp, VGt[:, hh_ * HV:(hh_ + 1) * HV], Vd["s"][h][:, hh_ * HV:(hh_ + 1) * HV], (D_vd["s"],), (B_vh[hh_],))
            P.dma(sp, VL, Vd[g][h][:, :], (D_vd[g],), (Bs["vl"],))
            for di in range(6):
                delta = di - 1
                src = bass.AP(RVd, h * RVLEN + (U0 - 128 * delta - 127), [[1, 128], [1, 512]])
                P.dma(pool, BT[:, di, :], src, (D_rvd,), (Bs["bt"],))
            for t in range(4):
                tq = slice(t * 512, (t + 1) * 512)
                blocks = [(KTG, j * 128, VGt, j * VW, None, B_kh[j // (NB // 2)], B_vh[j // (NB // 2)]) for j in range(NB)]
                for di in range(6):
                    jl = 4 * t + di - 1
                    if 0 <= jl < 16:
                        blocks.append((KTL, jl * 128, VL, jl * VW, BT[:, di, :], Bs["ktl"], Bs["vl"]))
                if prompt and t == 0:
                    blocks += [(KTC, i * 128, VC, i * VW, BT[:, 0, :], Bs["ktc"], Bs["vc"]) for i in range(8)]
                if prompt and t == 3:
                    blocks += [(KTC, i * 128, VC, i * VW, BT[:, 5, :], Bs["ktc"], Bs["vc"]) for i in range(8, 16)]
                nb = len(blocks)

                def qk(idx):
                    Ks, kc0, Vs, vc0, bias, bk, bv = blocks[idx]
                    sb_ = idx % 2
                    S1, S2 = PS[2 * sb_], PS[2 * sb_ + 1]
                    bS = (B_ps[2 * sb_], B_ps[2 * sb_ + 1])
                    mm(S1, Ks[0:64, kc0:kc0 + 128], QT[0:64, h, tq], True, bias is None, (bk, B_qt[h]), bS)
                    mm(S2, Ks[64:128, kc0:kc0 + 128], QT[64:128, h, tq], True, bias is None, (bk, B_qt[h]), bS)
                    if bias is not None:
                        mm(S1, antiid, bias, False, True, (B_const, Bs["bt"]), bS)
                        mm(S2, antiid, bias, False, True, (B_const, Bs["bt"]), bS)

                qk(0)
                qk(1)
                for idx in range(nb):
                    Ks, kc0, Vs, vc0, bias, bk, bv = blocks[idx]
                    sb_ = idx % 2
                    bS = (B_ps[2 * sb_], B_ps[2 * sb_ + 1])
                    pb_ = idx % 3
                    bp = Bs[f"p{pb_}"]
                    act(PT[pb_], PSW[sb_][:, :], AF.Exp, bS, (bp,))
                    if idx + 2 < nb:
                        qk(idx + 2)
                    mm(PS[4][0:VW, :], Vs[:, vc0:vc0 + VW], PT[pb_][:, 0:512], idx == 0, idx == nb - 1, (bv, bp), (B_ps[4],))
                    mm(PS[5][0:VW, :], Vs[:, vc0:vc0 + VW], PT[pb_][:, 512:1024], idx == 0, idx == nb - 1, (bv, bp), (B_ps[5],))
                    if idx == 2 and pend[0] is not None:
                        pend[0]()
                    if idx == 7 and pend[1] is not None:
                        pend[1]()
                bep = Bs["ep"]
                for k2 in range(2):
                    cp("dve", A12[k2][0:VW, :], PS[4 + k2][0:VW, :], (B_ps[4 + k2],), (bep,))

                def part_b1(bep=bep):
                    for k2 in range(2):
                        mm(PS[6 + k2][0:64, :], ONESF[64:65, 0:64], A12[k2][64:65, :], True, True, (B_const, bep), (B_ps[6 + k2],))
                    for k2 in range(2):
                        P.op("dve", lambda e, o=R12[k2][0:64, :], i=PS[6 + k2][0:64, :]: e.reciprocal(out=o, in_=i), (B_ps[6 + k2],), (bep,))
                    tt("dve", T1e[0:64, :], A12[0][0:64, :], R12[0][0:64, :], ALU.mult, (bep,), (bep,))
                    tt("dve", R12[1][0:64, :], A12[1][0:64, :], R12[1][0:64, :], ALU.mult, (bep,), (bep,))
                    stt("dve", Oe[0:64, :], R12[1][0:64, :], NLAM[0:64, l:l + 1], T1e[0:64, :], ALU.mult, ALU.add, (bep, B_prm), (bep,))
                    tt("dve", SQe[0:64, :], Oe[0:64, :], Oe[0:64, :], ALU.mult, (bep,), (bep,))
                    pend[0] = None

                def part_b2(bep=bep, h=h, tq=tq):
                    mm(PS[6][0:64, :], bdones[0:64, 0:64], SQe[0:64, :], True, True, (B_const, bep), (B_ps[6],))
                    act(R12[0][0:64, :], PS[6][0:64, :], AF.Ln, (B_ps[6],), (bep,), bias=HN_EPS, scale=1.0 / 64)
                    act(R12[0][0:64, :], R12[0][0:64, :], AF.Exp, (bep,), (bep,), scale=-0.5)
                    tt("dve", Oe[0:64, :], Oe[0:64, :], R12[0][0:64, :], ALU.mult, (bep,), (bep,))
                    c = 4 + h // 2
                    if h % 2 == 0:
                        ts("dve", MIXT[0:64, c, tq], Oe[0:64, :], DNG[0:64, l:l + 1], None, ALU.mult, None, (bep, B_prm), (B_mix[c],))
                    else:
                        ts("dve", STG[0:64, :], Oe[0:64, :], DNG[0:64, l:l + 1], None, ALU.mult, None, (bep, B_prm), (Bs["stage"],))
                        P.dma(sp, MIXT[64:128, c, tq], STG[0:64, :], (Bs["stage"],), (B_mix[c],))
                    pend[1] = None
                pend[0] = part_b1
                pend[1] = part_b2
        if pend[0] is not None:
            pend[0]()
        if pend[1] is not None:
            pend[1]()
        P.alias([B_ktg], B_kh); P.alias([B_vg], B_vh)

    XF2 = [SCR[:, 8192:9216], SCR[:, 9216:10240]]
    Gt = SCR[:, 6144:7168]; Bt_ = SCR[:, 7168:8192]

    def wout_ln1(l, g):
        prompt = g == "p"
        scr_phase(["xf0", "xf1", "lnj", "lng", "lnb", "hlsb"])
        P.alias([B_wout], [B_ktg])
        for q4 in range(4):
            P.dma(pool, W_OUT[:, :, q4 * 256:(q4 + 1) * 256], wo_d[l, :, q4 * 256:(q4 + 1) * 256].rearrange("(k p) f -> p k f", p=128), (), (B_wout,))
        P.dma(sp, Gt, bcast_rows(lng_d, (l * 2 + 0) * D, 128, D), (), (Bs["lng"],))
        P.dma(sp, Bt_, bcast_rows(lnb_d, (l * 2 + 0) * D, 128, D), (), (Bs["lnb"],))
        xsrc = x_in[g] if l == 0 else X2[g]
        xrd = () if l == 0 else (D_x2[g],)
        for tile in range(16):
            k = tile % 2
            rows = slice(tile * 128, (tile + 1) * 128)
            for hf in range(2):
                for c in range(8):
                    mm(PSW[k][:, hf * 512:(hf + 1) * 512], MIXT[:, c, rows], W_OUT[:, c, hf * 512:(hf + 1) * 512], c == 0, c == 7,
                       (B_mix[c], B_wout), (B_ps[2 * k], B_ps[2 * k + 1]))
            XF = XF2[k]; bxf = Bs[f"xf{k}"]
            P.dma(sp, XF, xsrc[rows, :], xrd, (bxf,))
            stt("dve", XF, XF, float(ALPHA), PSW[k][:, :], ALU.mult, ALU.add, (bxf, B_ps[2 * k], B_ps[2 * k + 1]), (bxf,))
            layer_norm_tile(XF, Gt, Bt_, XF, (bxf,), (bxf,))
            P.dma(sp, X1[g][rows, :], XF, (bxf,), (D_x1[g],))
            if prompt and tile == 0:
                P.dma(sp, HLd[0:1, :], XF[0:1, :], (bxf,), (D_hld,))
            if prompt and tile == 15:
                P.dma(sp, HLd[1:2, :], XF[127:128, :], (bxf,), (D_hld,))
            finish_tile(tile, XF, (bxf,))
        if prompt:
            P.collective("AllGather", G4, HLd.ap().opt(), HLg4.ap().opt(), (D_hld,), (D_hlg,))
            P.collective("AllGather", G2, HLg4.ap().opt(), HLg8.ap().opt(), (D_hlg,), (D_hlg,))
            HLSB = SCRB[0:16, 0:1024]
            P.dma(pool, HLSB, HLg8[:, :], (D_hlg,), (Bs["hlsb"],))
            for kc in range(8):
                mm(PS[6][:, kc * 2:(kc + 1) * 2], HLSB[:, kc * 128:(kc + 1) * 128], SELt[:, :], True, True, (Bs["hlsb"], B_const), (B_ps[6],))
            cp("dve", XTH[:, :, :], PS[6][:, 0:16].rearrange("p (k j) -> p k j", k=8), (B_ps[6],), (B_xth,))
        else:
            ms("dve", XTH[:, :, :], 0.0, (B_xth,))

    ASB = [SCR[:, 0:1026], SCR[:, 1032:1032 + 1026]]
    CC = SCR[:, 2064:3088]
    GG = [SCR[:, 3088:4112], SCR[:, 4112:5136]]

    def ffn_ln2(l, g, last):
        scr_phase(["xf0", "xf1", "lnj", "lng", "lnb", "asb0", "asb1", "cc0", "gg0", "gg1"])
        P.alias([B_ht, B_wd], B_qt + B_mix + [B_ktg, B_wout] + ret_bufs)
        for t3 in range(3):
            P.dma(sp, CONV[:, :, t3:t3 + 1], bass.AP(cw_d, (l * 3 + t3) * DFF, [[1, 128], [128, NFC], [1, 1]]), (), (B_conv,), allow_slow_non_contiguous=True)
        P.dma(sp, CONV[:, :, 3:4], bass.AP(cb_d, l * DFF, [[1, 128], [128, NFC], [1, 1]]), (), (B_conv,), allow_slow_non_contiguous=True)
        P.dma(sp, Gt, bcast_rows(lng_d, (l * 2 + 1) * D, 128, D), (), (Bs["lng"],))
        P.dma(sp, Bt_, bcast_rows(lnb_d, (l * 2 + 1) * D, 128, D), (), (Bs["lnb"],))
        cp("dve", XTH2[:, :, :], XT[:, :, 1023:1025], (B_xt,), (B_xth,))
        for half in range(2):
            tok0 = half * 1024
            for fc in range(NFC):
                slot = fc % 2
                wa = load_w_chunk(slot, 0, wsrc(wu_d, l, fc * 128, 128))
                wv = load_w_chunk(slot, 1, wsrc(wu_d, l, DFF + fc * 128, 128))
                if half == 0 and fc in (2, 5, 8, 11):
                    q4 = (fc - 2) // 3
                    P.dma(pool, WD[:, :, q4 * 256:(q4 + 1) * 256], wd_d[l, :, q4 * 256:(q4 + 1) * 256].rearrange("(c p) f -> p c f", p=128), (), (B_wd,))
                proj8(0, wa, tok0, 512)
                proj8(1, wa, tok0 + 512, 512)
                for kc in range(8):
                    lo = XTH[:, kc, 0:1] if half == 0 else XTH2[:, kc, 0:1]
                    hi = XTH2[:, kc, 1:2] if half == 0 else XTH[:, kc, 1:2]
                    mm(PS[2][:, 0:1], wa[:, kc, :], lo, kc == 0, kc == 7, (wa.buf, B_xth), (B_ps[2],))
                for kc in range(8):
                    hi = XTH2[:, kc, 1:2] if half == 0 else XTH[:, kc, 1:2]
                    mm(PS[2][:, 1:2], wa[:, kc, :], hi, kc == 0, kc == 7, (wa.buf, B_xth), (B_ps[2],))
                A = ASB[fc % 2]; ba = Bs[f"asb{fc % 2}"]
                act(A[:, 1:513], PS[0], AF.Copy, (B_ps[0],), (ba,))
                act(A[:, 513:1025], PS[1], AF.Copy, (B_ps[1],), (ba,))
                cp("dve", A[:, 0:1], PS[2][:, 0:1], (B_ps[2],), (ba,))
                cp("dve", A[:, 1025:1026], PS[2][:, 1:2], (B_ps[2],), (ba,))
                ts("dve", CC, A[:, 0:1024], CONV[:, fc, 0:1], CONV[:, fc, 3:4], ALU.mult, ALU.add, (ba, B_conv), (Bs["cc0"],))
                stt("dve", CC, A[:, 1:1025], CONV[:, fc, 1:2], CC, ALU.mult, ALU.add, (ba, B_conv, Bs["cc0"]), (Bs["cc0"],))
                stt("dve", CC, A[:, 2:1026], CONV[:, fc, 2:3], CC, ALU.mult, ALU.add, (ba, B_conv, Bs["cc0"]), (Bs["cc0"],))
                G_ = GG[fc % 2]; bg = Bs[f"gg{fc % 2}"]
                act(G_, CC, AF.Gelu, (Bs["cc0"],), (bg,))
                proj8(3, wv, tok0, 512)
                proj8(4, wv, tok0 + 512, 512)
                tt("dve", HT[:, fc, 0:512], G_[:, 0:512], PS[3], ALU.mult, (bg, B_ps[3]), (B_ht,))
                tt("dve", HT[:, fc, 512:1024], G_[:, 512:1024], PS[4], ALU.mult, (bg, B_ps[4]), (B_ht,))
            for t8 in range(8):
                tile = half * 8 + t8
                rows = slice(tile * 128, (tile + 1) * 128)
                for hf in range(2):
                    for fc in range(NFC):
                        mm(PSW[3][:, hf * 512:(hf + 1) * 512], HT[:, fc, t8 * 128:(t8 + 1) * 128], WD[:, fc, hf * 512:(hf + 1) * 512],
                           fc == 0, fc == NFC - 1, (B_ht, B_wd), (B_ps[6], B_ps[7]))
                k = tile % 2
                XF = XF2[k]; bxf = Bs[f"xf{k}"]
                P.dma(sp, XF, X1[g][rows, :], (D_x1[g],), (bxf,))
                stt("dve", XF, XF, float(ALPHA), PSW[3][:, :], ALU.mult, ALU.add, (bxf, B_ps[6], B_ps[7]), (bxf,))
                layer_norm_tile(XF, Gt, Bt_, XF, (bxf,), (bxf,))
                if last:
                    P.dma(sp, y_out[g][rows, :], XF, (bxf,), ())
                else:
                    P.dma(sp, X2[g][rows, :], XF, (bxf,), (D_x2[g],))
                    finish_tile(tile, XF, (bxf,))

    def block(l, g, last):
        prompt = g == "p"
        if l == 0:
            scr_phase(["xf0", "xf1", "lnj"])
            for tile in range(16):
                k = tile % 2
                P.dma(sp, XF2[k], x_in[g][tile * 128:(tile + 1) * 128, :], (), (Bs[f"xf{k}"],))
                finish_tile(tile, XF2[k], (Bs[f"xf{k}"],))
        scr_phase(["rope0", "rope1", "rt", "kvs", "smallc", "rstate", "yret", "gt", "sm", "kst", "sg"])
        P.alias(ret_bufs, [B_ktg, B_ht, B_wd, B_wout])
        P.alias(B_qt + B_mix, [B_ht, B_wd])
        logsig_params(l)
        P.dma(sp, SCR[:, 7168:8192], small_d[:, :, :].rearrange("p c d -> p (c d)"), (), (Bs["smallc"],))
        for p in range(4):
            ret_pass1(l, g, p)
        if prompt:
            P.collective("AllGather", G4, RSd.ap().opt(), RSg4.ap().opt(), (D_rsd,), (D_rsg4,))
            P.collective("AllGather", G2, RSg4.ap().opt(), RSg8.ap().opt(), (D_rsg4,), (D_rsg,))
        proj_dk_dv(l, g)
        if prompt:
            gather_stage(0)
        proj_dq_gate(l, g)
        ret_pass2(l, g, 0)
        ret_pass2(l, g, 1)
        if prompt:
            gather_stage(1)
        ret_pass2(l, g, 2)
        ret_pass2(l, g, 3)
        attention(l, g)
        wout_ln1(l, g)
        ffn_ln2(l, g, last)

    for g in groups:
        for l in range(nlayers):
            block(l, g, l == nlayers - 1)
    if dbg_out is not None:
        for g in groups:
            dbg("x1_" + g, X1[g][:, :], [T, D], (D_x1[g],))
    P.emit()
    return nc, P, dbg_list


_CACHE = {}


def _in_maps(inp):
    wf, wt = _host_weights(inp)
    maps = []
    shared = {
        "wf": wf, "wt": wt, "w_out": np.asarray(inp["w_out"], np.float32), "w_up": np.asarray(inp["w_up"], np.float32),
        "w_down": np.asarray(inp["w_down"], np.float32),
        "rdl": np.asarray(inp["ret_decay_logit"], np.float32).reshape(DEPTH, 16),
        "rel_bias": np.asarray(inp["rel_bias"], np.float32),
        "lq1": np.asarray(inp["lambda_q1"], np.float32), "lk1": np.asarray(inp["lambda_k1"], np.float32),
        "lq2": np.asarray(inp["lambda_q2"], np.float32), "lk2": np.asarray(inp["lambda_k2"], np.float32),
        "dng": np.asarray(inp["diff_norm_g"], np.float32), "ln_g": np.asarray(inp["ln_g"], np.float32),
        "ln_b": np.asarray(inp["ln_b"], np.float32), "conv_w": np.asarray(inp["conv_w"], np.float32),
        "conv_b": np.asarray(inp["conv_b"], np.float32),
    }
    xp = np.asarray(inp["x_prompt"], np.float32)[0]
    xs = np.asarray(inp["x_sample"], np.float32)
    for c in range(8):
        m = dict(shared)
        m["xp"] = np.ascontiguousarray(xp[c * T:(c + 1) * T])
        m["xs"] = np.ascontiguousarray(xs[c])
        m.update(_host_consts(c))
        maps.append(m)
    return maps


def kernel(**inp):
    if "nc" not in _CACHE:
        _CACHE["nc"] = build()[0]
    nc = _CACHE["nc"]
    maps = _in_maps(inp)
    res = run_bass_kernel_spmd(nc, maps, core_ids=list(range(8)))
    yp = np.concatenate([res.results[c]["yp"] for c in range(8)], axis=0)[None].astype(np.float32)
    ys = np.stack([res.results[c]["ys"] for c in range(8)], axis=0).astype(np.float32)
    return (yp, ys)
```

```python
import math
import numpy as np
import ml_dtypes
import concourse.bass as bass
import concourse.mybir as mybir
from concourse.bass_utils import run_bass_kernel_spmd

F32 = mybir.dt.float32
BF16 = mybir.dt.bfloat16
AF = mybir.ActivationFunctionType
ALU = mybir.AluOpType
AX = mybir.AxisListType
NPBF = ml_dtypes.bfloat16

D = 1024
T = 2048
NTT = 16
DEPTH = 4
DFF = 2816
NFC = 22
LN_EPS = 1e-5
HN_EPS = 1e-6
ALPHA = (2 * DEPTH) ** 0.25
KILL = -240.0
VW = 80
NF_IN = 36
U0 = 639
RVLEN = 1280


class Buf:
    __slots__ = ("name", "w", "r")

    def __init__(self, name):
        self.name = name
        self.w = None
        self.r = {}


class Prog:
    ENG = ("pe", "act", "dve", "pool", "sp")

    def __init__(self, nc, n_dma_sems=40):
        self.nc = nc
        self.esem = {e: nc.alloc_semaphore("s_" + e) for e in ("pe", "act", "dve", "pool")}
        self.ecnt = {e: 0 for e in self.esem}
        self.dsem = [nc.alloc_semaphore(f"sd{i}") for i in range(n_dma_sems)]
        self.dcnt = [0] * n_dma_sems
        self.drr = 0
        self.csem = [nc.alloc_semaphore(f"sc{i}") for i in range(16)]
        self.ccnt = [0] * 16
        self.crr = 0
        self.q = {e: [] for e in self.ENG}
        self.seen = {e: {} for e in self.ENG}
        self.n_inst = 0

    def _sem(self, key):
        kind, i = key
        if kind == "e":
            return self.esem[i]
        if kind == "d":
            return self.dsem[i]
        return self.csem[i]

    def _deps(self, eng, reads, writes, extra=()):
        deps = {}
        def add(tok):
            if tok is None:
                return
            k, v = tok
            if eng == "pe" and k == ("e", "pe"):
                return
            if deps.get(k, 0) < v:
                deps[k] = v
        for b in reads:
            add(b.w)
        for b in writes:
            add(b.w)
            for t in b.r.items():
                add(t)
        for t in extra:
            add(t)
        out = []
        seen = self.seen[eng]
        for k, v in deps.items():
            if seen.get(k, 0) >= v:
                continue
            seen[k] = v
            out.append((k, v))
        return out

    def _mark(self, tok, reads, writes):
        for b in writes:
            b.w = tok
            b.r = {}
        for b in reads:
            if b not in writes:
                if b.r.get(tok[0], 0) < tok[1]:
                    b.r[tok[0]] = tok[1]

    def alias(self, new, old):
        acc = {}
        for b in old:
            toks = list(b.r.items())
            if b.w is not None:
                toks.append(b.w)
            for k, v in toks:
                if acc.get(k, 0) < v:
                    acc[k] = v
        for b in new:
            for k, v in acc.items():
                if b.r.get(k, 0) < v:
                    b.r[k] = v

    def op(self, eng, fn, reads=(), writes=()):
        waits = self._deps(eng, reads, writes)
        self.ecnt[eng] += 1
        tok = (("e", eng), self.ecnt[eng])
        self.q[eng].append((waits, fn, ("e", eng), 1))
        self._mark(tok, reads, writes)
        self.n_inst += 1 + len(waits)
        return tok

    def dma(self, eng, out, in_, reads=(), writes=(), **kw):
        i = self.drr
        self.drr = (self.drr + 1) % len(self.dsem)
        extra = ()
        if self.dcnt[i] > 0:
            extra = ((("d", i), 16 * self.dcnt[i]),)
        waits = self._deps(eng, reads, writes, extra)
        self.dcnt[i] += 1
        tok = (("d", i), 16 * self.dcnt[i])
        self.q[eng].append((waits, (lambda e, o=out, s=in_, k=kw: e.dma_start(out=o, in_=s, **k)), ("d", i), 16))
        self._mark(tok, reads, writes)
        self.n_inst += 1 + len(waits)
        return tok

    def collective(self, kind, groups, in_ap, out_ap, reads=(), writes=()):
        i = self.crr
        self.crr = (self.crr + 1) % len(self.csem)
        extra = ()
        if self.ccnt[i] > 0:
            extra = ((("c", i), self.ccnt[i]),)
        waits = self._deps("pool", reads, writes, extra)
        self.ccnt[i] += 1
        tok = (("c", i), self.ccnt[i])
        self.q["pool"].append((waits, (lambda e, a=in_ap, b=out_ap, g=groups, k=kind: e.collective_compute(
            k, ALU.bypass, replica_groups=g, ins=[a], outs=[b])), ("c", i), 1))
        self._mark(tok, reads, writes)
        return tok

    def emit(self):
        nc = self.nc
        fin = []
        for e in self.esem:
            if self.ecnt[e]:
                fin.append((("e", e), self.ecnt[e]))
        for i, c in enumerate(self.dcnt):
            if c:
                fin.append((("d", i), 16 * c))
        for i, c in enumerate(self.ccnt):
            if c:
                fin.append((("c", i), c))
        engs = {"pe": "tensor", "act": "scalar", "dve": "vector", "pool": "gpsimd", "sp": "sync"}
        with nc.Block() as block:
            for e in self.ENG:
                items = self.q[e]
                def body(eng, items=items, e=e):
                    for waits, fn, key, inc in items:
                        for k, v in waits:
                            eng.wait_ge(self._sem(k), v)
                        fn(eng).then_inc(self._sem(key), inc)
                    if e == "sp":
                        for k, v in fin:
                            eng.wait_ge(self._sem(k), v)
                getattr(block, engs[e])(body)


def _t5_bucket_np(rel):
    rel = np.asarray(rel, np.int64)
    nb = 16
    ret = np.where(rel > 0, nb, 0)
    n = np.abs(rel)
    me = 8
    nf = np.maximum(n, 1).astype(np.float32)
    large = me + (np.log(nf / np.float32(me)).astype(np.float32) / np.float32(math.log(128 / me))
                  * np.float32(nb - me)).astype(np.int32)
    large = np.minimum(large, nb - 1)
    return (ret + np.where(n < me, n, large)).astype(np.int64)


def _rv_ranges():
    u = np.arange(RVLEN)
    b = _t5_bucket_np(U0 - u)
    runs = []
    s = 0
    for i in range(1, RVLEN + 1):
        if i == RVLEN or b[i] != b[s]:
            runs.append((int(b[s]), s, i))
            s = i
    return runs


def _rope_tables(pos0):
    i = np.arange(0, 64, 2, dtype=np.float32) / np.float32(64)
    inv = (np.float32(1.0) / (np.float32(10000.0) ** i)).astype(np.float32)
    pos = (pos0 + np.arange(T)).astype(np.float32)
    ang = (pos[:, None] * inv[None, :]).astype(np.float32)
    cos = np.cos(ang.astype(np.float64)).astype(np.float32)
    sin = np.sin(ang.astype(np.float64)).astype(np.float32)
    cosT = np.zeros((128, T), np.float32)
    sinS = np.zeros((128, T), np.float32)
    for p in range(128):
        d = p % 64
        cosT[p] = cos[:, d % 32]
        sinS[p] = sin[:, d % 32] * (-1.0 if d < 32 else 1.0)
    return cosT, sinS


def _far_masks(core, nblk, prompt):
    a = np.zeros((32, nblk), np.float32)
    for t in range(4):
        Tg = (core * 4 + t) if prompt else t
        for j in range(nblk):
            if j < 4 * Tg - 1:
                a[t, j] = 1; a[4 + t, j] = 1
            elif j >= 4 * Tg + 5:
                a[8 + t, j] = 1; a[12 + t, j] = 1
            else:
                a[16 + t, j] = 1
    return np.repeat(a, 128, axis=1).astype(NPBF)


def _host_consts(core):
    c = {}
    cp, sp_ = _rope_tables(core * T)
    cs, ss = _rope_tables(0)
    c["rope"] = np.stack([cp, sp_, cs, ss], 0)
    c["augk_p"] = _far_masks(core, 128, True)
    c["augk_s"] = _far_masks(core, 16, False)
    augc = np.zeros((32, 16 * 128), np.float32)
    for i in range(8):
        if i != core - 1:
            augc[20, i * 128:(i + 1) * 128] = 1
        if i != core + 1:
            augc[20, (8 + i) * 128:(9 + i) * 128] = 1
    c["augc"] = augc.astype(NPBF)
    pq = np.zeros((128, T), np.float32)
    mh = np.zeros((128, 1), np.float32); ml = np.zeros((128, 1), np.float32); kv = np.zeros((128, 1), np.float32)
    for base in (32, 96):
        for r in range(20):
            t = r % 4
            pq[base + r, t * 512:(t + 1) * 512] = 1
        pq[base + 20, :] = 1
        for r in (0, 1, 2, 3, 8, 9, 10, 11):
            mh[base + r] = 1
        for r in (4, 5, 6, 7, 12, 13, 14, 15):
            ml[base + r] = 1
        for r in (16, 17, 18, 19, 20):
            kv[base + r] = KILL
    c["pq"] = pq
    i = np.arange(128, dtype=np.float32)
    dm = i[None, :] - i[:, None]
    bd = np.zeros((128, 128), np.float32); bd[:64, :64] = 1; bd[64:, 64:] = 1
    small = np.zeros((128, 8, 128), np.float32)
    small[:, 0] = np.maximum(dm, 0); small[:, 1] = np.maximum(-dm, 0)
    small[:, 2] = (dm >= 0); small[:, 3] = (dm < 0)
    small[:, 4] = bd
    small[:, 5] = (i + 1.0)[None, :]
    small[:, 6] = (128.0 - i)[None, :]
    c["small"] = small
    misc = np.zeros((128, 64), np.float32)
    misc[:, 0] = 127.0 - i
    misc[:, 1] = i
    misc[:, 2:18] = 128.0 * np.arange(16)[None, :]
    misc[:, 18:34] = 128.0 * (15 - np.arange(16))[None, :]
    for r in range(8):
        misc[:, 34 + r] = 2048.0 * (core - 1 - r) if r < core else 0.0
        misc[:, 42 + r] = 2048.0 * (r - core - 1) if r > core else 0.0
        misc[:, 50 + r] = 1.0 if r < core else 0.0
    c["misc"] = misc
    misc2 = np.zeros((128, 16), np.float32)
    for r in range(8):
        misc2[:, r] = 1.0 if r > core else 0.0
    misc2[:, 8:9] = mh; misc2[:, 9:10] = ml; misc2[:, 10:11] = kv
    c["misc2"] = misc2
    idn = np.zeros((128, 3, 128), np.float32)
    idn[:, 0] = np.eye(128); idn[:, 1] = np.eye(128)[::-1]; idn[:, 2] = bd
    c["idn"] = idn.astype(NPBF)
    sel = np.zeros((16, 2), np.float32)
    if core > 0:
        sel[2 * (core - 1) + 1, 0] = 1
    if core < 7:
        sel[2 * (core + 1), 1] = 1
    c["sel"] = sel.astype(NPBF)
    return c


def _host_weights(inp):
    w_in = np.asarray(inp["w_in"], np.float32)
    L = w_in.shape[0]
    rq, rk, rv, rg = (w_in[:, :, i * 512:(i + 1) * 512] for i in range(4))
    dq = w_in[:, :, 2048:2560]; dk = w_in[:, :, 2560:3072]; dv = w_in[:, :, 3072:3584]

    def swap(w):
        w = w.reshape(L, D, 8, 2, 32)
        return w[:, :, :, ::-1, :].reshape(L, D, 512)

    def pad_heads(w):
        w = w.reshape(L, D, 8, 2, 32)
        o = np.zeros((L, D, 8, 4, 32), np.float32)
        o[:, :, :, 0] = w[:, :, :, 0]; o[:, :, :, 2] = w[:, :, :, 1]
        return o.reshape(L, D, 1024)
    wf = np.concatenate([rq, swap(rq), rk, swap(rk), rg, pad_heads(dq), pad_heads(dk)], axis=2)
    wt = np.concatenate([rv, dv], axis=2)
    return np.ascontiguousarray(wf), np.ascontiguousarray(wt)


def build(nlayers=DEPTH, groups=("p", "s"), dbg_out=None):
    nc = bass.Bass("TRN2", target_bir_lowering=False)
    P = Prog(nc)
    Bf = Buf

    def din(name, shape, dt=F32):
        return nc.dram_tensor(name, list(shape), dt, kind="ExternalInput")

    def dscr(name, shape, dt=F32):
        return nc.dram_tensor(name, list(shape), dt)

    x_in = {"p": din("xp", [T, D]), "s": din("xs", [T, D])}
    wf_d = din("wf", [DEPTH, D, NF_IN * 128]); wt_d = din("wt", [DEPTH, D, 1024])
    wo_d = din("w_out", [DEPTH, D, D]); wu_d = din("w_up", [DEPTH, D, 2 * DFF]); wd_d = din("w_down", [DEPTH, DFF, D])
    rdl_d = din("rdl", [DEPTH, 16]); rb_d = din("rel_bias", [32, 8])
    lam_d = [din(n, [DEPTH, 32]) for n in ("lq1", "lk1", "lq2", "lk2")]
    dng_d = din("dng", [DEPTH, 64]); lng_d = din("ln_g", [DEPTH, 2, D]); lnb_d = din("ln_b", [DEPTH, 2, D])
    cw_d = din("conv_w", [DEPTH, 3, DFF]); cb_d = din("conv_b", [DEPTH, DFF])
    rope_d = din("rope", [4, 128, T]); augkp_d = din("augk_p", [32, 16384], BF16); augks_d = din("augk_s", [32, T], BF16)
    augc_d = din("augc", [32, T], BF16); pq_d = din("pq", [128, T]); small_d = din("small", [128, 8, 128])
    misc_d = din("misc", [128, 64]); misc2_d = din("misc2", [128, 16]); idn_d = din("idn", [128, 3, 128], BF16)
    sel_d = din("sel", [16, 2], BF16)
    y_out = {"p": nc.dram_tensor("yp", [T, D], F32, kind="ExternalOutput"),
             "s": nc.dram_tensor("ys", [T, D], F32, kind="ExternalOutput")}
    X1 = {g: dscr("x1_" + g, [T, D]) for g in "ps"}
    X2 = {g: dscr("x2_" + g, [T, D]) for g in "ps"}
    gate_d = dscr("gate", [4, 128, T])
    KTd = {g: [dscr(f"ktd_{g}{pc}", [128, T], BF16) for pc in range(4)] for g in "ps"}
    Vd = {g: [dscr(f"vd_{g}{h}", [128, 16 * VW], BF16) for h in range(8)] for g in "ps"}
    KTg4 = [dscr(f"ktg4_{pc}", [4 * 128, T], BF16) for pc in range(4)]
    KTg8 = [dscr(f"ktg8_{pc}", [8 * 128, T], BF16) for pc in range(4)]
    Vg4 = [dscr(f"vg4_{h}", [4 * 128, 16 * VW], BF16) for h in range(8)]
    Vg8 = [dscr(f"vg8_{h}", [8 * 128, 16 * VW], BF16) for h in range(8)]
    RSd = dscr("rsd", [1024, 128]); RSg4 = dscr("rsg4", [4096, 128]); RSg8 = dscr("rsg8", [8192, 128])
    HLd = dscr("hld", [2, D]); HLg4 = dscr("hlg4", [8, D]); HLg8 = dscr("hlg8", [16, D])
    RVd = dscr("rvd", [8, RVLEN])
    RKT = dscr("r_kt", [4, 128, T], BF16); RVR = dscr("r_vr", [4, 128, T], BF16); RKV = dscr("r_kv", [4, 128, 2 * 16 * 128])
    D_ktg, D_vg, D_rsg, D_hlg = Bf("d_ktg"), Bf("d_vg"), Bf("d_rsg"), Bf("d_hlg")
    D_rsg4 = Bf("d_rsg4")
    D_ktd = {g: Bf("d_ktd" + g) for g in "ps"}; D_vd = {g: Bf("d_vd" + g) for g in "ps"}
    D_x1 = {g: Bf("d_x1" + g) for g in "ps"}; D_x2 = {g: Bf("d_x2" + g) for g in "ps"}
    D_gate, D_rsd, D_hld, D_rvd = Bf("d_gate"), Bf("d_rsd"), Bf("d_hld"), Bf("d_rvd")
    D_ret = [Bf(f"d_ret{p}") for p in range(4)]
    G4 = [[0, 1, 2, 3], [4, 5, 6, 7]]; G2 = [[0, 4], [1, 5], [2, 6], [3, 7]]

    A0 = nc.alloc_sbuf_tensor("sb_a0", [128, 49152], BF16)
    QT = A0[:, 0:16384].rearrange("p (c t) -> p c t", c=8)
    MIXT = A0[:, 16384:32768].rearrange("p (c t) -> p c t", c=8)
    KTG = A0[:, 32768:49152]
    HT = A0[:, 0:22528].rearrange("p (c t) -> p c t", c=NFC)
    WD = A0[:, 22528:45056].rearrange("p (c f) -> p c f", c=NFC)
    B_qt = [Bf(f"qt{h}") for h in range(8)]; B_mix = [Bf(f"mix{c}") for c in range(8)]; B_ktg = Bf("ktg")
    B_ht = Bf("ht"); B_wd = Bf("wd")
    SCR = nc.alloc_sbuf_tensor("sb_scr", [128, 11264], F32)
    SCRB = SCR[:, :].bitcast(BF16)
    XT = nc.alloc_sbuf_tensor("sb_xt", [128, 8, T], BF16); B_xt = Bf("xt")
    XTH = nc.alloc_sbuf_tensor("sb_xth", [128, 8, 2], BF16); B_xth = Bf("xth")
    VGt = nc.alloc_sbuf_tensor("sb_vg", [128, 128 * VW], BF16); B_vg = Bf("vg")
    NWS = 6
    WS = nc.alloc_sbuf_tensor("sb_ws", [128, NWS, 1024], BF16); B_wsl = [Bf(f"ws{i}") for i in range(NWS)]
    ws_rr = [0]
    IDN = nc.alloc_sbuf_tensor("sb_idn", [128, 3, 128], BF16); B_const = Bf("const")
    MISC = nc.alloc_sbuf_tensor("sb_misc", [128, 64], F32); MISC2 = nc.alloc_sbuf_tensor("sb_misc2", [128, 16], F32)
    PRM = nc.alloc_sbuf_tensor("sb_prm", [128, 256], F32); B_prm = Bf("prm")
    CONV = nc.alloc_sbuf_tensor("sb_conv", [128, NFC, 4], F32); B_conv = Bf("conv")
    SELt = nc.alloc_sbuf_tensor("sb_sel", [16, 2], BF16)
    LGREP = PRM[:, 0:16]; LGP = PRM[:, 16:24]; W16 = PRM[:, 24:40]; DEC = PRM[:, 40:48]; W8 = PRM[:, 48:64]
    LAM = PRM[:, 64:68]; NLAM = PRM[:, 68:72]; DNG = PRM[:, 72:76]; VALS = PRM[:, 76:84]; TMPP = PRM[:, 84:148]
    LAMT = PRM[:, 148:164]
    PSW = [nc.psum_tensor(f"psw{i}", [128, 1024], F32).__enter__() for i in range(4)]
    PS = []
    for i in range(4):
        PS += [PSW[i][:, 0:512], PSW[i][:, 512:1024]]
    B_ps = [Bf(f"ps{i}") for i in range(8)]
    ident = IDN[:, 0, :]; antiid = IDN[:, 1, :]; bdones = IDN[:, 2, :]

    sp, pool = "sp", "pool"

    def mm(out, lhsT, rhs, start, stop, reads, writes, **kw):
        return P.op("pe", lambda e: e.matmul(out, lhsT=lhsT, rhs=rhs, start=start, stop=stop, **kw), reads, writes)

    def tr(out, in_, reads, writes):
        return P.op("pe", lambda e: e.transpose(out, in_, ident), reads, writes)

    def act(out, in_, func, reads, writes, **kw):
        return P.op("act", lambda e: e.activation(out=out, in_=in_, func=func, **kw), reads, writes)

    def ts(eng, out, in0, s1, s2, op0, op1, reads, writes):
        if s2 is None:
            return P.op(eng, lambda e: e.tensor_scalar(out=out, in0=in0, scalar1=s1, scalar2=None, op0=op0), reads, writes)
        return P.op(eng, lambda e: e.tensor_scalar(out=out, in0=in0, scalar1=s1, scalar2=s2, op0=op0, op1=op1), reads, writes)

    def tt(eng, out, in0, in1, op, reads, writes):
        return P.op(eng, lambda e: e.tensor_tensor(out=out, in0=in0, in1=in1, op=op), reads, writes)

    def stt(eng, out, in0, scalar, in1, op0, op1, reads, writes):
        return P.op(eng, lambda e: e.scalar_tensor_tensor(out=out, in0=in0, scalar=scalar, in1=in1, op0=op0, op1=op1), reads, writes)

    def cp(eng, out, in_, reads, writes):
        return P.op(eng, lambda e: e.tensor_copy(out=out, in_=in_), reads, writes)

    def ms(eng, ap, val, writes):
        return P.op(eng, lambda e: e.memset(ap, val), (), writes)

    def bcast_rows(dt_, off, n, cols):
        return bass.AP(dt_, off, [[0, n], [1, cols]])

    def sb3(ap2, mid, inner):
        return bass.AP(ap2.tensor, ap2.offset, [list(ap2.ap[0]), [0, mid], list(ap2.ap[-1])])

    B_scr = Bf("scr_init")
    P.dma(sp, IDN[:, :, :], idn_d[:, :, :], (), (B_const,))
    P.dma(sp, MISC[:, :], misc_d[:, :], (), (B_const,))
    P.dma(sp, MISC2[:, :], misc2_d[:, :], (), (B_const,))
    P.dma(sp, SELt[:, :], sel_d[:, :], (), (B_const,))
    RB = SCR[0:8, 0:32]; RVS = SCR[0:8, 64:64 + RVLEN]; ZR = SCR[0:8, 1408:1408 + RVLEN]
    P.dma(sp, RB, bass.AP(rb_d, 0, [[1, 8], [8, 32]]), (), (B_scr,), allow_slow_non_contiguous=True)
    ms("dve", ZR, 0.0, (B_scr,))
    for (b, lo, hi) in _rv_ranges():
        ts("dve", RVS[:, lo:hi], ZR[:, lo:hi], RB[:, b:b + 1], None, ALU.add, None, (B_scr,), (B_scr,))
    P.dma(sp, RVd[:, :], RVS, (B_scr,), (D_rvd,))
    LQ = [SCR[:, 3072 + i * 128: 3072 + (i + 1) * 128] for i in range(4)]
    for i in range(4):
        P.dma(sp, LQ[i], bcast_rows(lam_d[i], 0, 128, 128), (), (B_scr,))
    tt("dve", LQ[0], LQ[0], LQ[1], ALU.mult, (B_scr,), (B_scr,))
    tt("dve", LQ[2], LQ[2], LQ[3], ALU.mult, (B_scr,), (B_scr,))
    P.op("dve", lambda e: e.reduce_sum(out=LAMT[:, 0:4], in_=LQ[0].rearrange("p (l k) -> p l k", l=4), axis=AX.X), (B_scr,), (B_prm,))
    P.op("dve", lambda e: e.reduce_sum(out=LAMT[:, 4:8], in_=LQ[2].rearrange("p (l k) -> p l k", l=4), axis=AX.X), (B_scr,), (B_prm,))
    act(LAMT[:, 8:16], LAMT[:, 0:8], AF.Exp, (B_prm,), (B_prm,))
    tt("dve", LAM, LAMT[:, 8:12], LAMT[:, 12:16], ALU.subtract, (B_prm,), (B_prm,))
    lam_init = [0.8 - 0.6 * math.exp(-0.3 * l) for l in range(DEPTH)]
    P.dma(sp, DNG[0:64, :], bass.AP(dng_d, 0, [[1, 64], [64, 4]]), (), (B_prm,), allow_slow_non_contiguous=True)
    for l in range(DEPTH):
        ts("dve", LAM[:, l:l + 1], LAM[:, l:l + 1], float(lam_init[l]), None, ALU.add, None, (B_prm,), (B_prm,))
        ts("dve", DNG[0:64, l:l + 1], DNG[0:64, l:l + 1], float(1.0 - lam_init[l]), None, ALU.mult, None, (B_prm,), (B_prm,))
    ts("dve", NLAM, LAM, -1.0, None, ALU.mult, None, (B_prm,), (B_prm,))
    V32 = TMPP[:, 0:8]; VH = SCRB[:, 8192:8200]; VHF = TMPP[:, 8:16]; LO = TMPP[:, 16:24]; T1 = TMPP[:, 24:32]
    ms("dve", V32, 0.0, (B_prm,))
    for base in (32, 96):
        P.dma(sp, V32[base:base + 8, :], bcast_rows(rb_d, 15 * 8, 8, 8), (B_prm,), (B_prm,))
        P.dma(sp, V32[base + 8:base + 16, :], bcast_rows(rb_d, 31 * 8, 8, 8), (B_prm,), (B_prm,))
    cp("dve", VH, V32, (B_prm,), (B_scr,))
    cp("dve", VHF, VH, (B_scr,), (B_prm,))
    tt("dve", LO, V32, VHF, ALU.subtract, (B_prm,), (B_prm,))
    ts("dve", T1, VHF, MISC2[:, 8:9], None, ALU.mult, None, (B_prm, B_const), (B_prm,))
    stt("dve", VALS, LO, MISC2[:, 9:10], T1, ALU.mult, ALU.add, (B_prm, B_const), (B_prm,))
    ts("dve", VALS, VALS, MISC2[:, 10:11], None, ALU.add, None, (B_prm, B_const), (B_prm,))
    PQ = SCR[:, 4096:4096 + T]
    B_pq = Bf("pq")
    P.dma(sp, PQ, pq_d[:, :], (B_scr,), (B_pq,))
    for h in range(8):
        ts("dve", QT[:, h, :], PQ, VALS[:, h:h + 1], None, ALU.mult, None, (B_pq, B_prm), (B_qt[h],))

    dbg_list = []

    def dbg(name, ap, shape, reads, dt=F32):
        if dbg_out is None or name not in dbg_out:
            return
        o = nc.dram_tensor("dbg_" + name, list(shape), dt, kind="ExternalOutput")
        P.dma(sp, o.ap() if len(shape) != 2 else o[:, :], ap, reads, ())
        dbg_list.append("dbg_" + name)

    dbg("vals", VALS, [128, 8], (B_prm,))
    dbg("lam", LAM, [128, 4], (B_prm,))
    dbg("rvd", RVd[:, :], [8, RVLEN], (D_rvd,))
    dbg("qt0", QT[:, 0, :], [128, T], (B_qt[0],), BF16)

    S_Q = 32 ** -0.5

    class WT:
        def __init__(self, ap, buf):
            self.ap = ap; self.buf = buf
        def __getitem__(self, k):
            return self.ap[k]

    def load_w_chunk(slot, sub, src_ap):
        i = ws_rr[0]
        ws_rr[0] = (i + 1) % NWS
        dst = WS[:, i, :].rearrange("p (k f) -> p k f", k=8)
        P.dma(pool, dst, src_ap, (), (B_wsl[i],))
        return WT(dst, B_wsl[i])

    def wsrc(dt_, l, c0, n):
        return dt_[l, :, c0:c0 + n].rearrange("(k p) f -> p k f", p=128)

    def finish_token_tile(g, tile, XF, reads_xf):
        XB = SCRB[:, 20480:21504]
        B_xb = Bs["xb"]
        P.op("act", lambda e: e.activation(out=XB, in_=XF, func=AF.Copy), reads_xf, (B_xb,))
        pst = PS[7].bitcast(BF16)
        for kc in range(8):
            tr(pst[:, kc * 128:(kc + 1) * 128], XB[:, kc * 128:(kc + 1) * 128], (B_xb, B_const), (B_ps[7],))
        cp("dve", XT[:, :, tile * 128:(tile + 1) * 128], pst.rearrange("p (k t) -> p k t", k=8), (B_ps[7],), (B_xt,))

    Bs = {n: Bf(n) for n in ("xb", "xf0", "xf1", "kst", "sg", "rope0", "rope1", "rt", "qrt", "krt", "kr", "vr", "vrf",
                             "kvs", "rstate", "rp", "qft", "sm", "yret", "gt", "smallc", "rsg", "ktl", "vl", "ktc", "vc",
                             "bt", "p0", "p1", "p2", "ep", "stage", "lnx", "lny", "lnj", "lng", "lnb", "asb0", "asb1", "cc0", "cc1",
                             "gg0", "gg1", "hlsb")}

    ONESF = nc.alloc_sbuf_tensor("sb_onesf", [128, 64], F32)
    ms("dve", ONESF[:, :], 1.0, (B_const,))
    XTH2 = nc.alloc_sbuf_tensor("sb_xth2", [128, 8, 2], BF16)
    ST = PRM[:, 164:180]
    KTGR = A0[:, 32768:49152]
    W_OUT = KTGR.rearrange("p (k f) -> p k f", k=8)
    B_wout = Bf("wout")
    ret_bufs = [Bs[n] for n in ("qrt", "krt", "kr", "vr", "vrf", "rp", "qft", "sm")]
    scr_cur = [B_scr, B_pq]

    def scr_phase(names):
        new = [Bs[n] for n in names]
        P.alias(new, scr_cur)
        scr_cur.clear()
        scr_cur.extend(new)

    def layer_norm_tile(Y, Gt, Bt, OUT, rd, wr):
        JUNK = SCR[:, 10240:11264]
        bj = Bs["lnj"]
        P.op("dve", lambda e: e.reduce_sum(out=ST[:, 0:1], in_=Y, axis=AX.X), rd, (B_prm,))
        tt("dve", JUNK, Y, Y, ALU.mult, rd, (bj,))
        P.op("dve", lambda e: e.reduce_sum(out=ST[:, 1:2], in_=JUNK, axis=AX.X), (bj,), (B_prm,))
        ts("dve", ST[:, 2:3], ST[:, 0:1], 1.0 / D, None, ALU.mult, None, (B_prm,), (B_prm,))
        tt("dve", ST[:, 3:4], ST[:, 2:3], ST[:, 2:3], ALU.mult, (B_prm,), (B_prm,))
        stt("dve", ST[:, 4:5], ST[:, 1:2], 1.0 / D, ST[:, 3:4], ALU.mult, ALU.subtract, (B_prm,), (B_prm,))
        act(ST[:, 5:6], ST[:, 4:5], AF.Ln, (B_prm,), (B_prm,), bias=LN_EPS, scale=1.0)
        act(ST[:, 5:6], ST[:, 5:6], AF.Exp, (B_prm,), (B_prm,), scale=-0.5)
        stt("dve", ST[:, 6:7], ST[:, 2:3], -1.0, ST[:, 5:6], ALU.mult, ALU.mult, (B_prm,), (B_prm,))
        ts("dve", OUT, Y, ST[:, 5:6], ST[:, 6:7], ALU.mult, ALU.add, tuple(rd) + (B_prm,), wr)
        tt("dve", OUT, OUT, Gt, ALU.mult, tuple(wr) + (Bs["lng"],), wr)
        tt("dve", OUT, OUT, Bt, ALU.add, tuple(wr) + (Bs["lnb"],), wr)

    def finish_tile(tile, XF, rd, tbank=5):
        XB = SCRB[:, 20480:21504]
        bxb = Bs["lnj"]
        cp("dve", XB, XF, tuple(rd), (bxb,))
        pst = PS[tbank].bitcast(BF16)
        for kc in range(8):
            tr(pst[:, kc * 128:(kc + 1) * 128], XB[:, kc * 128:(kc + 1) * 128], (bxb, B_const), (B_ps[tbank],))
        cp("dve", XT[:, :, tile * 128:(tile + 1) * 128], pst.rearrange("p (k t) -> p k t", k=8), (B_ps[tbank],), (B_xt,))

    def proj8(bank, wtile, t0, n):
        for kc in range(8):
            mm(PS[bank][:, 0:n], wtile[:, kc, :], XT[:, kc, t0:t0 + n], kc == 0, kc == 7,
               (wtile.buf, B_xt), (B_ps[bank],))

    def logsig_params(l):
        X = TMPP[:, 0:16]; NX = TMPP[:, 16:32]; E = TMPP[:, 32:48]; Z = TMPP[:, 48:64]
        P.dma(sp, X, bcast_rows(rdl_d, l * 16, 128, 16), (B_prm,), (B_prm,))
        ts("dve", NX, X, -1.0, None, ALU.mult, None, (B_prm,), (B_prm,))
        tt("dve", E, X, NX, ALU.max, (B_prm,), (B_prm,))
        act(E, E, AF.Exp, (B_prm,), (B_prm,), scale=-1.0)
        ts("dve", Z, E, 2.0, None, ALU.add, None, (B_prm,), (B_prm,))
        P.op("dve", lambda e: e.reciprocal(out=Z, in_=Z), (B_prm,), (B_prm,))
        tt("dve", Z, Z, E, ALU.mult, (B_prm,), (B_prm,))
        tt("dve", E, Z, Z, ALU.mult, (B_prm,), (B_prm,))
        ts("dve", NX, E, 1.0 / 9, 1.0 / 7, ALU.mult, ALU.add, (B_prm,), (B_prm,))
        for cst in (1.0 / 5, 1.0 / 3, 1.0):
            tt("dve", NX, NX, E, ALU.mult, (B_prm,), (B_prm,))
            ts("dve", NX, NX, cst, None, ALU.add, None, (B_prm,), (B_prm,))
        tt("dve", NX, NX, Z, ALU.mult, (B_prm,), (B_prm,))
        ts("dve", X, X, 0.0, None, ALU.min, None, (B_prm,), (B_prm,))
        stt("dve", LGREP, NX, -2.0, X, ALU.mult, ALU.add, (B_prm,), (B_prm,))
        LGR4 = LGREP.rearrange("p (d q h) -> p d q h", d=2, q=4)
        cp("dve", LGP[0:64, :].rearrange("p (d q) -> p d q", d=2), LGR4[0:64, :, :, 0], (B_prm,), (B_prm,))
        cp("dve", LGP[64:128, :].rearrange("p (d q) -> p d q", d=2), LGR4[64:128, :, :, 1], (B_prm,), (B_prm,))
        ts("dve", TMPP[:, 0:8], LGREP[:, 0:8], MISC[:, 0:1], None, ALU.mult, None, (B_prm, B_const), (B_prm,))
        ts("dve", TMPP[:, 8:16], LGREP[:, 8:16], MISC[:, 1:2], None, ALU.mult, None, (B_prm, B_const), (B_prm,))
        act(W16, TMPP[:, 0:16], AF.Exp, (B_prm,), (B_prm,))
        act(DEC, LGP, AF.Exp, (B_prm,), (B_prm,), scale=128.0)

    def rope_tile(g, tt_, rb):
        RO = SCR[:, rb * 1024:(rb + 1) * 1024].rearrange("p (c t) -> p c t", c=2)
        ri = 0 if g == "p" else 2
        P.dma(sp, RO, rope_d[ri:ri + 2, :, tt_ * 512:(tt_ + 1) * 512].rearrange("c p t -> p c t"), (), (Bs[f"rope{rb}"],))
        return RO

    def rotary(bank_a, bank_b, RO, rb, OUT, out_buf):
        RT = SCR[:, 2048:3072]
        tt("dve", RT[:, 0:512], PS[bank_a], RO[:, 0, :], ALU.mult, (B_ps[bank_a], Bs[f"rope{rb}"]), (Bs["rt"],))
        tt("dve", RT[:, 512:1024], PS[bank_b], RO[:, 1, :], ALU.mult, (B_ps[bank_b], Bs[f"rope{rb}"]), (Bs["rt"],))
        tt("dve", OUT, RT[:, 0:512], RT[:, 512:1024], ALU.add, (Bs["rt"],), (out_buf,))

    QrT = KTGR[:, 0:2048]; KrT = KTGR[:, 2048:4096]; QfT = KTGR[:, 4096:6144]; QbT = KTGR[:, 6144:8192]
    KR = KTGR[:, 8192:10240].rearrange("p (n d) -> p n d", n=16); VR = KTGR[:, 10240:12288].rearrange("p (n d) -> p n d", n=16)
    VRF = KTGR[:, 12288:16384].rearrange("p (r n d) -> p r n d", r=2, n=16)
    RPb = KTGR[:, 12288:16384].rearrange("p (r n d) -> p r n d", r=2, n=16)
    SMt = KTGR[:, 4096 + 0:4096 + 0]
    KVS = SCR[:, 3072:7168].rearrange("p (r n d) -> p r n d", r=2, n=16)
    SMALLC = SCR[:, 7168:8192].rearrange("p (c d) -> p c d", c=8)
    MASK = SCR[:, 8192:8448].rearrange("p (h d) -> p h d", h=2)
    WFT = SCR[:, 8448:8704].rearrange("p (h d) -> p h d", h=2)
    QFB = SCR[:, 8704:8960].rearrange("p (h d) -> p h d", h=2)
    MTMP = SCR[:, 8960:9216].rearrange("p (h d) -> p h d", h=2)
    RST = SCR[:, 9216:9472].rearrange("p (h d) -> p h d", h=2)
    RSTART = SCR[:, 9472:9728].rearrange("p (h d) -> p h d", h=2)
    RSGS = KTGR[:, 8192:10240].bitcast(F32).rearrange("p (r d) -> p r d", r=8)
    SMB = SCRB[:, 21504:22528].rearrange("p (j h d) -> p j h d", j=4, h=2)
    BDm = SMALLC[:, 4, :]

    def ret_pass1(l, g, p):
        slot = p % 2
        wk = load_w_chunk(slot, 0, wsrc(wf_d, l, (8 + p) * 128, 128))
        wks = load_w_chunk(slot, 1, wsrc(wf_d, l, (12 + p) * 128, 128))
        wv = load_w_chunk(slot, 2, wsrc(wt_d, l, p * 128, 128))
        for t4 in range(4):
            rb = t4 % 2
            RO = rope_tile(g, t4, rb)
            bk = 2 * (t4 % 2)
            proj8(bk, wk, t4 * 512, 512)
            proj8(bk + 1, wks, t4 * 512, 512)
            rotary(bk, bk + 1, RO, rb, KrT[:, t4 * 512:(t4 + 1) * 512], Bs["krt"])
            for j in range(4):
                tok = (t4 * 4 + j) * 128
                for kc in range(8):
                    mm(PS[4][:, j * 128:(j + 1) * 128], XT[:, kc, tok:tok + 128], wv[:, kc, :], kc == 0, kc == 7,
                       (B_xt, wv.buf), (B_ps[4],))
            P.op("act", lambda e, o=VR[:, t4 * 4:(t4 + 1) * 4, :], i=PS[4].rearrange("p (j d) -> p j d", j=4):
                 e.activation(out=o, in_=i, func=AF.Copy), (B_ps[4],), (Bs["vr"],))
            pst = PS[5].bitcast(BF16)
            for j in range(4):
                tok = (t4 * 4 + j) * 128
                tr(pst[:, j * 128:(j + 1) * 128], KrT[:, tok:tok + 128], (Bs["krt"], B_const), (B_ps[5],))
            cp("dve", KR[:, t4 * 4:(t4 + 1) * 4, :], pst[:, 0:512].rearrange("p (j d) -> p j d", j=4), (B_ps[5],), (Bs["kr"],))
        ms("dve", WFT, 0.125, (Bs["smallc"],))
        for d_ in range(2):
            for hh in range(2):
                ts("dve", WFT[:, d_, hh * 64:(hh + 1) * 64], WFT[:, d_, hh * 64:(hh + 1) * 64],
                   W16[:, d_ * 8 + 2 * p + hh: d_ * 8 + 2 * p + hh + 1], None, ALU.mult, None, (B_prm, Bs["smallc"]), (Bs["smallc"],))
        for d_ in range(2):
            tt("dve", VRF[:, d_, :, :], VR, sb3(WFT[:, d_, :], 16, 128), ALU.mult, (Bs["vr"], Bs["smallc"]), (Bs["vrf"],))
        for n0 in range(0, 16, 2):
            for i in range(2):
                for d_ in range(2):
                    c0 = (d_ * 2 + i) * 128
                    mm(PS[6][:, c0:c0 + 128], KR[:, n0 + i, :], VRF[:, d_, n0 + i, :], True, True, (Bs["kr"], Bs["vrf"]), (B_ps[6],))
            bd4 = bass.AP(BDm.tensor, BDm.offset, [list(BDm.ap[0]), [0, 2], [0, 2], list(BDm.ap[-1])])
            tt("dve", KVS[:, :, n0:n0 + 2, :], PS[6].rearrange("p (r i d) -> p r i d", r=2, i=2), bd4, ALU.mult,
               (B_ps[6], Bs["smallc"]), (Bs["kvs"],))
        if g == "p":
            for d_ in range(2):
                ms("dve", RST[:, d_, :], 0.0, (Bs["rstate"],))
                order = range(16) if d_ == 0 else range(15, -1, -1)
                for n in order:
                    stt("dve", RST[:, d_, :], RST[:, d_, :], DEC[:, d_ * 4 + p: d_ * 4 + p + 1], KVS[:, d_, n, :],
                        ALU.mult, ALU.add, (Bs["rstate"], Bs["kvs"], B_prm), (Bs["rstate"],))
                P.dma(sp, RSd[(d_ * 4 + p) * 128:(d_ * 4 + p + 1) * 128, :], RST[:, d_, :], (Bs["rstate"],), (D_rsd,))
        P.dma(sp, RKT[p, :, :], KrT, (Bs["krt"],), (D_ret[p],))
        P.dma(sp, RVR[p, :, :], KTGR[:, 10240:12288], (Bs["vr"],), (D_ret[p],))
        P.dma(sp, RKV[p, :, :], SCR[:, 3072:7168], (Bs["kvs"],), (D_ret[p],))

    def ret_pass2(l, g, p):
        slot = p % 2
        wq = load_w_chunk(slot, 0, wsrc(wf_d, l, p * 128, 128))
        wqs = load_w_chunk(slot, 1, wsrc(wf_d, l, (4 + p) * 128, 128))
        P.alias([Bs["kvs"]], [Bs["kst"], Bs["sg"]])
        P.alias([Bs["rsg"]], [Bs["kr"]])
        P.dma(sp, KrT, RKT[p, :, :], (D_ret[p],), (Bs["krt"],))
        P.dma(sp, KTGR[:, 10240:12288], RVR[p, :, :], (D_ret[p],), (Bs["vr"],))
        P.dma(sp, SCR[:, 3072:7168], RKV[p, :, :], (D_ret[p],), (Bs["kvs"],))
        for t4 in range(4):
            rb = t4 % 2
            RO = rope_tile(g, t4, rb)
            proj8(6, wq, t4 * 512, 512)
            proj8(7, wqs, t4 * 512, 512)
            rotary(6, 7, RO, rb, QrT[:, t4 * 512:(t4 + 1) * 512], Bs["qrt"])
        for hh in range(2):
            h = 2 * p + hh
            act(MTMP[:, 0, :], SMALLC[:, 0, :], AF.Exp, (Bs["smallc"], B_prm), (Bs["smallc"],), scale=LGREP[:, h:h + 1])
            act(MTMP[:, 1, :], SMALLC[:, 1, :], AF.Exp, (Bs["smallc"], B_prm), (Bs["smallc"],), scale=LGREP[:, 8 + h:9 + h])
            stt("dve", MTMP[:, 0, :], MTMP[:, 0, :], 0.125, SMALLC[:, 2, :], ALU.mult, ALU.mult, (Bs["smallc"],), (Bs["smallc"],))
            stt("dve", MTMP[:, 1, :], MTMP[:, 1, :], 0.125, SMALLC[:, 3, :], ALU.mult, ALU.mult, (Bs["smallc"],), (Bs["smallc"],))
            tt("dve", MASK[:, hh, :], MTMP[:, 0, :], MTMP[:, 1, :], ALU.add, (Bs["smallc"],), (Bs["smallc"],))
        act(QFB[:, 0, :], SMALLC[:, 5, :], AF.Exp, (Bs["smallc"], B_prm), (Bs["smallc"],), scale=LGP[:, p:p + 1])
        act(QFB[:, 1, :], SMALLC[:, 6, :], AF.Exp, (Bs["smallc"], B_prm), (Bs["smallc"],), scale=LGP[:, 4 + p:5 + p])
        tt("dve", QfT.rearrange("p (n d) -> p n d", n=16), QrT.rearrange("p (n d) -> p n d", n=16), sb3(QFB[:, 0, :], 16, 128),
           ALU.mult, (Bs["qrt"], Bs["smallc"]), (Bs["qft"],))
        tt("dve", QbT.rearrange("p (n d) -> p n d", n=16), QrT.rearrange("p (n d) -> p n d", n=16), sb3(QFB[:, 1, :], 16, 128),
           ALU.mult, (Bs["qrt"], Bs["smallc"]), (Bs["qft"],))
        for d_ in range(2):
            if g == "p":
                src = bass.AP(RSg8, (d_ * 4 + p) * 128 * 128, [[128, 128], [1024 * 128, 8], [1, 128]])
                P.dma(sp, RSGS, src, (D_rsg,), (Bs["rsg"],))
                ecol = 34 if d_ == 0 else 42
                act(W8[:, d_ * 8:(d_ + 1) * 8], MISC[:, ecol:ecol + 8], AF.Exp, (B_const, B_prm), (B_prm,), scale=LGP[:, d_ * 4 + p: d_ * 4 + p + 1])
                msk = MISC[:, 50:58] if d_ == 0 else MISC2[:, 0:8]
                tt("dve", W8[:, d_ * 8:(d_ + 1) * 8], W8[:, d_ * 8:(d_ + 1) * 8], msk, ALU.mult, (B_prm, B_const), (B_prm,))
                ts("dve", RSTART[:, d_, :], RSGS[:, 0, :], W8[:, d_ * 8:d_ * 8 + 1], None, ALU.mult, None, (Bs["rsg"], B_prm), (Bs["rstate"],))
                for r in range(1, 8):
                    stt("dve", RSTART[:, d_, :], RSGS[:, r, :], W8[:, d_ * 8 + r:d_ * 8 + r + 1], RSTART[:, d_, :], ALU.mult, ALU.add,
                        (Bs["rsg"], B_prm, Bs["rstate"]), (Bs["rstate"],))
                cp("dve", RST[:, d_, :], RSTART[:, d_, :], (Bs["rstate"],), (Bs["rstate"],))
            else:
                ms("dve", RST[:, d_, :], 0.0, (Bs["rstate"],))
            order = range(16) if d_ == 0 else range(15, -1, -1)
            for n in order:
                cp("dve", RPb[:, d_, n, :], RST[:, d_, :], (Bs["rstate"],), (Bs["rp"],))
                stt("dve", RST[:, d_, :], RST[:, d_, :], DEC[:, d_ * 4 + p: d_ * 4 + p + 1], KVS[:, d_, n, :],
                    ALU.mult, ALU.add, (Bs["rstate"], Bs["kvs"], B_prm), (Bs["rstate"],))
        YT = SCR[:, 9728:10240]; GTt = SCR[:, 10240:10752]; CR = SCR[:, 0:512]; SQ = SCRB[:, 1024:1536]
        for t4 in range(4):
            for j in range(4):
                n = t4 * 4 + j
                c = n * 128
                mm(PS[0][:, j * 128:(j + 1) * 128], KrT[0:64, c:c + 128], QrT[0:64, c:c + 128], True, True, (Bs["krt"], Bs["qrt"]), (B_ps[0],))
                mm(PS[1][:, j * 128:(j + 1) * 128], KrT[64:128, c:c + 128], QrT[64:128, c:c + 128], True, True, (Bs["krt"], Bs["qrt"]), (B_ps[1],))
            for hh in range(2):
                tt("dve", SMB[:, :, hh, :], PS[hh].rearrange("p (j d) -> p j d", j=4), sb3(MASK[:, hh, :], 4, 128), ALU.mult,
                   (B_ps[hh], Bs["smallc"]), (Bs["sm"],))
            for j in range(4):
                n = t4 * 4 + j
                bank = 2 + j // 2
                c0 = (j % 2) * 256
                mm(PS[bank][:, c0:c0 + 256], VR[:, n, :], SMB[:, j, :, :].rearrange("p h d -> p (h d)"), True, True, (Bs["vr"], Bs["sm"]), (B_ps[bank],))
                c = n * 128
                mm(PS[4][:, j * 128:(j + 1) * 128], RPb[:, 0, n, :], QfT[:, c:c + 128], True, False, (Bs["rp"], Bs["qft"]), (B_ps[4],))
                mm(PS[4][:, j * 128:(j + 1) * 128], RPb[:, 1, n, :], QbT[:, c:c + 128], False, True, (Bs["rp"], Bs["qft"]), (B_ps[4],))
            act(CR, PS[4], AF.Copy, (B_ps[4],), (Bs["rope0"],))
            for j in range(4):
                bank = 2 + j // 2
                c0 = (j % 2) * 256
                tt("dve", YT[0:64, j * 128:(j + 1) * 128], PS[bank][0:64, c0:c0 + 128], CR[0:64, j * 128:(j + 1) * 128], ALU.add,
                   (B_ps[bank], Bs["rope0"]), (Bs["yret"],))
                tt("dve", YT[64:128, j * 128:(j + 1) * 128], PS[bank][64:128, c0 + 128:c0 + 256], CR[64:128, j * 128:(j + 1) * 128], ALU.add,
                   (B_ps[bank], Bs["rope0"]), (Bs["yret"],))
            P.dma(sp, GTt, gate_d[p, :, t4 * 512:(t4 + 1) * 512], (D_gate,), (Bs["gt"],))
            tt("dve", SQ, YT, YT, ALU.mult, (Bs["yret"],), (Bs["rope0"],))
            mm(PS[5], bdones, SQ, True, True, (B_const, Bs["rope0"]), (B_ps[5],))
            act(CR, PS[5], AF.Ln, (B_ps[5],), (Bs["rope0"],), bias=HN_EPS, scale=1.0 / 64)
            act(CR, CR, AF.Exp, (Bs["rope0"],), (Bs["rope0"],), scale=-0.5)
            tt("dve", YT, YT, CR, ALU.mult, (Bs["yret"], Bs["rope0"]), (Bs["yret"],))
            tt("dve", MIXT[:, p, t4 * 512:(t4 + 1) * 512], YT, GTt, ALU.mult, (Bs["yret"], Bs["gt"]), (B_mix[p],))

    def proj_dq_gate(l, g):
        PQt = SCR[:, 5120:7168]
        P.alias([Bs["sg"]], [Bs["kvs"], Bs["kst"]])
        P.dma(sp, PQt, pq_d[:, :], (), (Bs["kvs"],))
        for h in range(8):
            ts("dve", QT[:, h, :], PQt, VALS[:, h:h + 1], None, ALU.mult, None, (Bs["kvs"], B_prm), (B_qt[h],))
        SG = SCR[:, 4096:4608]
        for h in range(8):
            wq = load_w_chunk(0, 0, wsrc(wf_d, l, (20 + h) * 128, 128))
            for t4 in range(4):
                bq = t4 % 4
                tq = slice(t4 * 512, (t4 + 1) * 512)
                proj8(bq, wq, t4 * 512, 512)
                for r0 in (0, 64):
                    act(QT[r0:r0 + 32, h, tq], PS[bq][r0:r0 + 32, :], AF.Identity, (B_ps[bq],), (B_qt[h],), scale=float(S_Q))
        for p in range(4):
            wg = load_w_chunk(0, 0, wsrc(wf_d, l, (16 + p) * 128, 128))
            for t4 in range(4):
                b = 4 + (t4 % 2)
                proj8(b, wg, t4 * 512, 512)
                act(SG, PS[b], AF.Silu, (B_ps[b],), (Bs["sg"],))
                P.dma(sp, gate_d[p, :, t4 * 512:(t4 + 1) * 512], SG, (Bs["sg"],), (D_gate,))

    def proj_dk_dv(l, g):
        KST = SCRB[:, 6144:8192]
        P.alias([Bs["kst"]], [Bs["kvs"], Bs["sg"]])
        for h in range(8):
            wk = load_w_chunk(0, 0, wsrc(wf_d, l, (28 + h) * 128, 128))
            for t4 in range(4):
                bk_ = t4 % 4
                tq = slice(t4 * 512, (t4 + 1) * 512)
                proj8(bk_, wk, t4 * 512, 512)
                for r0 in (0, 64):
                    cp("dve", KST[r0:r0 + 32, tq], PS[bk_][r0:r0 + 32, :], (B_ps[bk_],), (Bs["kst"],))
            kr0 = (h % 2) * 64
            P.dma(sp, KTd[g][h // 2][kr0:kr0 + 32, :], KST[0:32, :], (Bs["kst"],), (D_ktd[g],))
            P.dma(sp, KTd[g][h // 2][kr0 + 32:kr0 + 64, :], KST[64:96, :], (Bs["kst"],), (D_ktd[g],))
        VST = VGt[:, 0:8 * 16 * VW]
        ms("pool", VST, 0.0, (B_vg,))
        VST4 = VST.rearrange("p (h b e) -> p h b e", h=8, b=16)
        ms("pool", VST4[:, :, :, 64:65], 1.0, (B_vg,))
        for jv in range(4):
            slot = jv % 2
            wv = load_w_chunk(slot, 0, wsrc(wt_d, l, 512 + jv * 128, 128))
            for t0 in range(0, 16, 4):
                b = 6 + ((t0 // 4) % 2)
                for j in range(4):
                    tok = (t0 + j) * 128
                    for kc in range(8):
                        mm(PS[b][:, j * 128:(j + 1) * 128], XT[:, kc, tok:tok + 128], wv[:, kc, :], kc == 0, kc == 7,
                           (B_xt, wv.buf), (B_ps[b],))
                pe0 = VST.ap[0]
                o = bass.AP(VST.tensor, VST.offset + (2 * jv) * 16 * VW + t0 * VW, [list(pe0), [VW, 4], [16 * VW, 2], [1, 64]])
                cp("dve", o, PS[b].rearrange("p (j h e) -> p j h e", j=4, h=2), (B_ps[b],), (B_vg,))
        for h in range(8):
            P.dma(sp, Vd[g][h][:, :], VST[:, h * 16 * VW:(h + 1) * 16 * VW], (B_vg,), (D_vd[g],))

    D_kt4 = [Bf(f"d_kt4_{i}") for i in range(4)]; D_v4 = [Bf(f"d_v4_{i}") for i in range(8)]

    def gather_stage(stage):
        if stage == 0:
            for pc in range(4):
                P.collective("AllGather", G4, KTd["p"][pc].ap().opt(), KTg4[pc].ap().opt(), (D_ktd["p"],), (D_kt4[pc],))
            for h in range(8):
                P.collective("AllGather", G4, Vd["p"][h].ap().opt(), Vg4[h].ap().opt(), (D_vd["p"],), (D_v4[h],))
        else:
            for pc in range(4):
                P.collective("AllGather", G2, KTg4[pc].ap().opt(), KTg8[pc].ap().opt(), (D_kt4[pc],), (D_ktg,))
            for h in range(8):
                P.collective("AllGather", G2, Vg4[h].ap().opt(), Vg8[h].ap().opt(), (D_v4[h],), (D_vg,))

    KTL = SCRB[:, 0:2048]; KTC = SCRB[:, 2048:4096]
    VL = SCRB[:, 4096:4096 + 16 * VW]; VC = SCRB[:, 5376:5376 + 16 * VW]
    BT = SCRB[:, 6656:6656 + 3072].rearrange("p (d q) -> p d q", d=6)
    PT = [SCRB[:, 9728:10752], SCRB[:, 10752:11776], SCRB[:, 18944:19968]]
    A12 = [SCR[:, 5888:6400], SCR[:, 6400:6912]]
    R12 = [SCR[:, 6912:7424], SCR[:, 7424:7936]]
    T1e = SCR[:, 7936:8448]; Oe = SCR[:, 8448:8960]
    SQe = SCRB[:, 17920:18432]; STG = SCRB[:, 18432:18944]

    def attention(l, g):
        prompt = g == "p"
        NB = 128 if prompt else 16
        NK = NB * 128
        scr_phase(["ktl", "vl", "ktc", "vc", "bt", "p0", "p1", "p2", "ep", "stage"])
        P.alias([B_ktg], ret_bufs + [Bs["rsg"], B_wout])
        aug = augkp_d if prompt else augks_d
        P.dma(sp, KTG[32:64, 0:NK], aug[:, :], (), (B_ktg,))
        P.dma(sp, KTG[96:128, 0:NK], aug[:, :], (), (B_ktg,))
        ms("pool", KTL, 0.0, (Bs["ktl"],))
        if prompt:
            ms("pool", KTC, 0.0, (Bs["ktc"],))
            P.dma(sp, KTC[32:64, :], augc_d[:, :], (Bs["ktc"],), (Bs["ktc"],))
            P.dma(sp, KTC[96:128, :], augc_d[:, :], (Bs["ktc"],), (Bs["ktc"],))
        B_kh = [Bf("ktgA"), Bf("ktgB")]; B_vh = [Bf("vgA"), Bf("vgB")]
        P.alias(B_kh, [B_ktg]); P.alias(B_vh, [B_vg])
        HK = NK // 2; HV = (NB // 2) * VW
        pend = [None, None]
        for h in range(8):
            for half, r0 in ((0, 0), (1, 64)):
                row = (h % 2) * 64 + half * 32
                pc = h // 2
                if prompt:
                    for hh_ in range(2):
                        src = bass.AP(KTg8[pc], row * T + hh_ * 4 * 128 * T, [[T, 32], [128 * T, 4], [1, T]])
                        P.dma(sp, KTG[r0:r0 + 32, hh_ * HK:(hh_ + 1) * HK].rearrange("p (r t) -> p r t", r=4), src, (D_ktg,), (B_kh[hh_],))
                    for ci, c0 in ((0, 15 * 128), (1, 0)):
                        srcc = bass.AP(KTg8[pc], row * T + c0, [[T, 32], [128 * T, 8], [1, 128]])
                        P.dma(sp, KTC[r0:r0 + 32, ci * 1024:(ci + 1) * 1024].rearrange("p (r t) -> p r t", r=8), srcc, (D_ktg,), (Bs["ktc"],))
                else:
                    for hh_ in range(2):
                        P.dma(sp, KTG[r0:r0 + 32, hh_ * HK:(hh_ + 1) * HK], KTd["s"][pc][row:row + 32, hh_ * HK:(hh_ + 1) * HK], (D_ktd["s"],), (B_kh[hh_],))
                P.dma(sp, KTL[r0:r0 + 32, :], KTd[g][pc][row:row + 32, :], (D_ktd[g],), (Bs["ktl"],))
            if prompt:
                for hh_ in range(2):
                    src = bass.AP(Vg8[h], hh_ * 4 * 128 * 16 * VW, [[16 * VW, 128], [128 * 16 * VW, 4], [1, 16 * VW]])
                    P.dma(sp, VGt[:, hh_ * HV:(hh_ + 1) * HV].rearrange("p (r f) -> p r f", r=4), src, (D_vg,), (B_vh[hh_],))
                for ci, c0 in ((0, 15 * VW), (1, 0)):
                    srcc = bass.AP(Vg8[h], c0, [[16 * VW, 128], [128 * 16 * VW, 8], [1, VW]])
                    P.dma(sp, VC[:, ci * 8 * VW:(ci + 1) * 8 * VW].rearrange("p (r f) -> p r f", r=8), srcc, (D_vg,), (Bs["vc"],))
            else:
                for hh_ in range(2):
                    P.dma(sp, VGt[:, hh_ * HV:(hh_ + 1) * HV], Vd["s"][h][:, hh_ * HV:(hh_ + 1) * HV], (D_vd["s"],), (B_vh[hh_],))
            P.dma(sp, VL, Vd[g][h][:, :], (D_vd[g],), (Bs["vl"],))
            for di in range(6):
                delta = di - 1
                src = bass.AP(RVd, h * RVLEN + (U0 - 128 * delta - 127), [[1, 128], [1, 512]])
                P.dma(pool, BT[:, di, :], src, (D_rvd,), (Bs["bt"],))
            for t in range(4):
                tq = slice(t * 512, (t + 1) * 512)
                blocks = [(KTG, j * 128, VGt, j * VW, None, B_kh[j // (NB // 2)], B_vh[j // (NB // 2)]) for j in range(NB)
                          if prompt or not (4 * t - 1 <= j <= 4 * t + 4)]
                for di in range(6):
                    jl = 4 * t + di - 1
                    if 0 <= jl < 16:
                        blocks.append((KTL, jl * 128, VL, jl * VW, BT[:, di, :], Bs["ktl"], Bs["vl"]))
                if prompt and t == 0:
                    blocks += [(KTC, i * 128, VC, i * VW, BT[:, 0, :], Bs["ktc"], Bs["vc"]) for i in range(8)]
                if prompt and t == 3:
                    blocks += [(KTC, i * 128, VC, i * VW, BT[:, 5, :], Bs["ktc"], Bs["vc"]) for i in range(8, 16)]
                nb = len(blocks)

                def qk(idx):
                    Ks, kc0, Vs, vc0, bias, bk, bv = blocks[idx]
                    sb_ = idx % 2
                    S1, S2 = PS[2 * sb_], PS[2 * sb_ + 1]
                    bS = (B_ps[2 * sb_], B_ps[2 * sb_ + 1])
                    mm(S1, Ks[0:64, kc0:kc0 + 128], QT[0:64, h, tq], True, bias is None, (bk, B_qt[h]), bS)
                    mm(S2, Ks[64:128, kc0:kc0 + 128], QT[64:128, h, tq], True, bias is None, (bk, B_qt[h]), bS)
                    if bias is not None:
                        mm(S1, antiid, bias, False, True, (B_const, Bs["bt"]), bS)
                        mm(S2, antiid, bias, False, True, (B_const, Bs["bt"]), bS)

                qk(0)
                qk(1)
                for idx in range(nb):
                    Ks, kc0, Vs, vc0, bias, bk, bv = blocks[idx]
                    sb_ = idx % 2
                    bS = (B_ps[2 * sb_], B_ps[2 * sb_ + 1])
                    pb_ = idx % 3
                    bp = Bs[f"p{pb_}"]
                    act(PT[pb_], PSW[sb_][:, :], AF.Exp, bS, (bp,))
                    if idx + 2 < nb:
                        qk(idx + 2)
                    mm(PS[4][0:VW, :], Vs[:, vc0:vc0 + VW], PT[pb_][:, 0:512], idx == 0, idx == nb - 1, (bv, bp), (B_ps[4],))
                    mm(PS[5][0:VW, :], Vs[:, vc0:vc0 + VW], PT[pb_][:, 512:1024], idx == 0, idx == nb - 1, (bv, bp), (B_ps[5],))
                    if idx == 2 and pend[0] is not None:
                        pend[0]()
                    if idx == 7 and pend[1] is not None:
                        pend[1]()
                bep = Bs["ep"]
                for k2 in range(2):
                    cp("dve", A12[k2][0:VW, :], PS[4 + k2][0:VW, :], (B_ps[4 + k2],), (bep,))

                def part_b1(bep=bep):
                    for k2 in range(2):
                        mm(PS[6 + k2][0:64, :], ONESF[64:65, 0:64], A12[k2][64:65, :], True, True, (B_const, bep), (B_ps[6 + k2],))
                    for k2 in range(2):
                        P.op("dve", lambda e, o=R12[k2][0:64, :], i=PS[6 + k2][0:64, :]: e.reciprocal(out=o, in_=i), (B_ps[6 + k2],), (bep,))
                    tt("dve", T1e[0:64, :], A12[0][0:64, :], R12[0][0:64, :], ALU.mult, (bep,), (bep,))
                    tt("dve", R12[1][0:64, :], A12[1][0:64, :], R12[1][0:64, :], ALU.mult, (bep,), (bep,))
                    stt("dve", Oe[0:64, :], R12[1][0:64, :], NLAM[0:64, l:l + 1], T1e[0:64, :], ALU.mult, ALU.add, (bep, B_prm), (bep,))
                    tt("dve", SQe[0:64, :], Oe[0:64, :], Oe[0:64, :], ALU.mult, (bep,), (bep,))
                    pend[0] = None

                def part_b2(bep=bep, h=h, tq=tq):
                    mm(PS[6][0:64, :], bdones[0:64, 0:64], SQe[0:64, :], True, True, (B_const, bep), (B_ps[6],))
                    act(R12[0][0:64, :], PS[6][0:64, :], AF.Ln, (B_ps[6],), (bep,), bias=HN_EPS, scale=1.0 / 64)
                    act(R12[0][0:64, :], R12[0][0:64, :], AF.Exp, (bep,), (bep,), scale=-0.5)
                    tt("dve", Oe[0:64, :], Oe[0:64, :], R12[0][0:64, :], ALU.mult, (bep,), (bep,))
                    c = 4 + h // 2
                    if h % 2 == 0:
                        ts("dve", MIXT[0:64, c, tq], Oe[0:64, :], DNG[0:64, l:l + 1], None, ALU.mult, None, (bep, B_prm), (B_mix[c],))
                    else:
                        ts("dve", STG[0:64, :], Oe[0:64, :], DNG[0:64, l:l + 1], None, ALU.mult, None, (bep, B_prm), (Bs["stage"],))
                        P.dma(sp, MIXT[64:128, c, tq], STG[0:64, :], (Bs["stage"],), (B_mix[c],))
                    pend[1] = None
                pend[0] = part_b1
                pend[1] = part_b2
        if pend[0] is not None:
            pend[0]()
        if pend[1] is not None:
            pend[1]()
        P.alias([B_ktg], B_kh); P.alias([B_vg], B_vh)

    XF2 = [SCR[:, 8192:9216], SCR[:, 9216:10240]]
    Gt = SCR[:, 6144:7168]; Bt_ = SCR[:, 7168:8192]

    def wout_ln1(l, g):
        prompt = g == "p"
        scr_phase(["xf0", "xf1", "lnj", "lng", "lnb", "hlsb"])
        P.alias([B_wout], [B_ktg])
        for q4 in range(4):
            P.dma(pool, W_OUT[:, :, q4 * 256:(q4 + 1) * 256], wo_d[l, :, q4 * 256:(q4 + 1) * 256].rearrange("(k p) f -> p k f", p=128), (), (B_wout,))
        P.dma(sp, Gt, bcast_rows(lng_d, (l * 2 + 0) * D, 128, D), (), (Bs["lng"],))
        P.dma(sp, Bt_, bcast_rows(lnb_d, (l * 2 + 0) * D, 128, D), (), (Bs["lnb"],))
        xsrc = x_in[g] if l == 0 else X2[g]
        xrd = () if l == 0 else (D_x2[g],)
        for tile in range(16):
            k = tile % 2
            rows = slice(tile * 128, (tile + 1) * 128)
            for hf in range(2):
                for c in range(8):
                    mm(PSW[k][:, hf * 512:(hf + 1) * 512], MIXT[:, c, rows], W_OUT[:, c, hf * 512:(hf + 1) * 512], c == 0, c == 7,
                       (B_mix[c], B_wout), (B_ps[2 * k], B_ps[2 * k + 1]))
            XF = XF2[k]; bxf = Bs[f"xf{k}"]
            P.dma(sp, XF, xsrc[rows, :], xrd, (bxf,))
            stt("dve", XF, XF, float(ALPHA), PSW[k][:, :], ALU.mult, ALU.add, (bxf, B_ps[2 * k], B_ps[2 * k + 1]), (bxf,))
            layer_norm_tile(XF, Gt, Bt_, XF, (bxf,), (bxf,))
            P.dma(sp, X1[g][rows, :], XF, (bxf,), (D_x1[g],))
            if prompt and tile == 0:
                P.dma(sp, HLd[0:1, :], XF[0:1, :], (bxf,), (D_hld,))
            if prompt and tile == 15:
                P.dma(sp, HLd[1:2, :], XF[127:128, :], (bxf,), (D_hld,))
            finish_tile(tile, XF, (bxf,))
        if prompt:
            P.collective("AllGather", G4, HLd.ap().opt(), HLg4.ap().opt(), (D_hld,), (D_hlg,))
            P.collective("AllGather", G2, HLg4.ap().opt(), HLg8.ap().opt(), (D_hlg,), (D_hlg,))
            HLSB = SCRB[0:16, 0:1024]
            P.dma(pool, HLSB, HLg8[:, :], (D_hlg,), (Bs["hlsb"],))
            for kc in range(8):
                mm(PS[6][:, kc * 2:(kc + 1) * 2], HLSB[:, kc * 128:(kc + 1) * 128], SELt[:, :], True, True, (Bs["hlsb"], B_const), (B_ps[6],))
            cp("dve", XTH[:, :, :], PS[6][:, 0:16].rearrange("p (k j) -> p k j", k=8), (B_ps[6],), (B_xth,))
        else:
            ms("dve", XTH[:, :, :], 0.0, (B_xth,))

    ASB = [SCR[:, 0:1026], SCR[:, 1032:1032 + 1026]]
    CC = SCR[:, 2064:3088]
    GG = [SCR[:, 3088:4112], SCR[:, 4112:5136]]

    def ffn_ln2(l, g, last):
        scr_phase(["xf0", "xf1", "lnj", "lng", "lnb", "asb0", "asb1", "cc0", "gg0", "gg1"])
        P.alias([B_ht, B_wd], B_qt + B_mix + [B_ktg, B_wout] + ret_bufs)
        for t3 in range(3):
            P.dma(sp, CONV[:, :, t3:t3 + 1], bass.AP(cw_d, (l * 3 + t3) * DFF, [[1, 128], [128, NFC], [1, 1]]), (), (B_conv,), allow_slow_non_contiguous=True)
        P.dma(sp, CONV[:, :, 3:4], bass.AP(cb_d, l * DFF, [[1, 128], [128, NFC], [1, 1]]), (), (B_conv,), allow_slow_non_contiguous=True)
        P.dma(sp, Gt, bcast_rows(lng_d, (l * 2 + 1) * D, 128, D), (), (Bs["lng"],))
        P.dma(sp, Bt_, bcast_rows(lnb_d, (l * 2 + 1) * D, 128, D), (), (Bs["lnb"],))
        cp("dve", XTH2[:, :, :], XT[:, :, 1023:1025], (B_xt,), (B_xth,))
        for half in range(2):
            tok0 = half * 1024
            for fc in range(NFC):
                slot = fc % 2
                wa = load_w_chunk(slot, 0, wsrc(wu_d, l, fc * 128, 128))
                wv = load_w_chunk(slot, 1, wsrc(wu_d, l, DFF + fc * 128, 128))
                if half == 0 and fc in (2, 5, 8, 11):
                    q4 = (fc - 2) // 3
                    P.dma(pool, WD[:, :, q4 * 256:(q4 + 1) * 256], wd_d[l, :, q4 * 256:(q4 + 1) * 256].rearrange("(c p) f -> p c f", p=128), (), (B_wd,))
                proj8(0, wa, tok0, 512)
                proj8(1, wa, tok0 + 512, 512)
                for kc in range(8):
                    lo = XTH[:, kc, 0:1] if half == 0 else XTH2[:, kc, 0:1]
                    hi = XTH2[:, kc, 1:2] if half == 0 else XTH[:, kc, 1:2]
                    mm(PS[2][:, 0:1], wa[:, kc, :], lo, kc == 0, kc == 7, (wa.buf, B_xth), (B_ps[2],))
                for kc in range(8):
                    hi = XTH2[:, kc, 1:2] if half == 0 else XTH[:, kc, 1:2]
                    mm(PS[2][:, 1:2], wa[:, kc, :], hi, kc == 0, kc == 7, (wa.buf, B_xth), (B_ps[2],))
                A = ASB[fc % 2]; ba = Bs[f"asb{fc % 2}"]
                act(A[:, 1:513], PS[0], AF.Copy, (B_ps[0],), (ba,))
                act(A[:, 513:1025], PS[1], AF.Copy, (B_ps[1],), (ba,))
                cp("dve", A[:, 0:1], PS[2][:, 0:1], (B_ps[2],), (ba,))
                cp("dve", A[:, 1025:1026], PS[2][:, 1:2], (B_ps[2],), (ba,))
                ts("dve", CC, A[:, 0:1024], CONV[:, fc, 0:1], CONV[:, fc, 3:4], ALU.mult, ALU.add, (ba, B_conv), (Bs["cc0"],))
                stt("dve", CC, A[:, 1:1025], CONV[:, fc, 1:2], CC, ALU.mult, ALU.add, (ba, B_conv, Bs["cc0"]), (Bs["cc0"],))
                stt("dve", CC, A[:, 2:1026], CONV[:, fc, 2:3], CC, ALU.mult, ALU.add, (ba, B_conv, Bs["cc0"]), (Bs["cc0"],))
                G_ = GG[fc % 2]; bg = Bs[f"gg{fc % 2}"]
                act(G_, CC, AF.Gelu, (Bs["cc0"],), (bg,))
                proj8(3, wv, tok0, 512)
                proj8(4, wv, tok0 + 512, 512)
                tt("dve", HT[:, fc, 0:512], G_[:, 0:512], PS[3], ALU.mult, (bg, B_ps[3]), (B_ht,))
                tt("dve", HT[:, fc, 512:1024], G_[:, 512:1024], PS[4], ALU.mult, (bg, B_ps[4]), (B_ht,))
            for t8 in range(8):
                tile = half * 8 + t8
                rows = slice(tile * 128, (tile + 1) * 128)
                for hf in range(2):
                    for fc in range(NFC):
                        mm(PSW[3][:, hf * 512:(hf + 1) * 512], HT[:, fc, t8 * 128:(t8 + 1) * 128], WD[:, fc, hf * 512:(hf + 1) * 512],
                           fc == 0, fc == NFC - 1, (B_ht, B_wd), (B_ps[6], B_ps[7]))
                k = tile % 2
                XF = XF2[k]; bxf = Bs[f"xf{k}"]
                P.dma(sp, XF, X1[g][rows, :], (D_x1[g],), (bxf,))
                stt("dve", XF, XF, float(ALPHA), PSW[3][:, :], ALU.mult, ALU.add, (bxf, B_ps[6], B_ps[7]), (bxf,))
                layer_norm_tile(XF, Gt, Bt_, XF, (bxf,), (bxf,))
                if last:
                    P.dma(sp, y_out[g][rows, :], XF, (bxf,), ())
                else:
                    P.dma(sp, X2[g][rows, :], XF, (bxf,), (D_x2[g],))
                    finish_tile(tile, XF, (bxf,))

    def block(l, g, last):
        prompt = g == "p"
        if l == 0:
            scr_phase(["xf0", "xf1", "lnj"])
            for tile in range(16):
                k = tile % 2
                P.dma(sp, XF2[k], x_in[g][tile * 128:(tile + 1) * 128, :], (), (Bs[f"xf{k}"],))
                finish_tile(tile, XF2[k], (Bs[f"xf{k}"],))
        scr_phase(["rope0", "rope1", "rt", "kvs", "smallc", "rstate", "yret", "gt", "sm", "kst", "sg"])
        P.alias(ret_bufs, [B_ktg, B_ht, B_wd, B_wout])
        P.alias(B_qt + B_mix, [B_ht, B_wd])
        logsig_params(l)
        P.dma(sp, SCR[:, 7168:8192], small_d[:, :, :].rearrange("p c d -> p (c d)"), (), (Bs["smallc"],))
        for p in range(4):
            ret_pass1(l, g, p)
        if prompt:
            P.collective("AllGather", G4, RSd.ap().opt(), RSg4.ap().opt(), (D_rsd,), (D_rsg4,))
            P.collective("AllGather", G2, RSg4.ap().opt(), RSg8.ap().opt(), (D_rsg4,), (D_rsg,))
        proj_dk_dv(l, g)
        if prompt:
            gather_stage(0)
        proj_dq_gate(l, g)
        ret_pass2(l, g, 0)
        ret_pass2(l, g, 1)
        if prompt:
            gather_stage(1)
        ret_pass2(l, g, 2)
        ret_pass2(l, g, 3)
        attention(l, g)
        wout_ln1(l, g)
        ffn_ln2(l, g, last)

    for g in groups:
        for l in range(nlayers):
            block(l, g, l == nlayers - 1)
    if dbg_out is not None:
        for g in groups:
            dbg("x1_" + g, X1[g][:, :], [T, D], (D_x1[g],))
    P.emit()
    return nc, P, dbg_list


_CACHE = {}


def _in_maps(inp):
    wf, wt = _host_weights(inp)
    maps = []
    shared = {
        "wf": wf, "wt": wt, "w_out": np.asarray(inp["w_out"], np.float32), "w_up": np.asarray(inp["w_up"], np.float32),
        "w_down": np.asarray(inp["w_down"], np.float32),
        "rdl": np.asarray(inp["ret_decay_logit"], np.float32).reshape(DEPTH, 16),
        "rel_bias": np.asarray(inp["rel_bias"], np.float32),
        "lq1": np.asarray(inp["lambda_q1"], np.float32), "lk1": np.asarray(inp["lambda_k1"], np.float32),
        "lq2": np.asarray(inp["lambda_q2"], np.float32), "lk2": np.asarray(inp["lambda_k2"], np.float32),
        "dng": np.asarray(inp["diff_norm_g"], np.float32), "ln_g": np.asarray(inp["ln_g"], np.float32),
        "ln_b": np.asarray(inp["ln_b"], np.float32), "conv_w": np.asarray(inp["conv_w"], np.float32),
        "conv_b": np.asarray(inp["conv_b"], np.float32),
    }
    xp = np.asarray(inp["x_prompt"], np.float32)[0]
    xs = np.asarray(inp["x_sample"], np.float32)
    for c in range(8):
        m = dict(shared)
        m["xp"] = np.ascontiguousarray(xp[c * T:(c + 1) * T])
        m["xs"] = np.ascontiguousarray(xs[c])
        m.update(_host_consts(c))
        maps.append(m)
    return maps


def kernel(**inp):
    if "nc" not in _CACHE:
        _CACHE["nc"] = build()[0]
    nc = _CACHE["nc"]
    maps = _in_maps(inp)
    res = run_bass_kernel_spmd(nc, maps, core_ids=list(range(8)))
    yp = np.concatenate([res.results[c]["yp"] for c in range(8)], axis=0)[None].astype(np.float32)
    ys = np.stack([res.results[c]["ys"] for c in range(8)], axis=0).astype(np.float32)
    return (yp, ys)
```
